# Optimizing a Trainium2 kernel written in Bass

```python
import math
import jax, jax.numpy as jnp
from jax import lax
import numpy as np

D_MODEL = 1024
BATCH = 8
SEQ = 2048
DEPTH = 1

D_MIX = D_MODEL
HEAD_DIM = 64
D_A = D_MIX // 2
D_B = D_MIX - D_A
N_HEADS_A = D_A // HEAD_DIM
N_HEADS_B = D_B // HEAD_DIM
IDX_HEADS = 16
IDX_DIM = 64
TOPK_MAX = 256
NUM_BUCKETS = 32
MAX_DISTANCE = 128
Q_BLOCK = 128
SPARSE_Q_BLOCK = 64
RMS_EPS = 1e-6
IDX_SCALE = (IDX_HEADS * IDX_DIM) ** -0.5

SPLIT_SIZES = (
    D_A, D_A, D_A, D_A,
    IDX_HEADS * IDX_DIM, IDX_DIM, IDX_HEADS,
    D_B, D_B, D_B, D_B,
)
D_IN_PROJ = sum(SPLIT_SIZES)

kernel_name = "hymba_dsa_stickbreaking_hybrid"


def rms_norm(x, gain):
    xf = x.astype(jnp.float32)
    y = xf * lax.rsqrt(jnp.mean(xf * xf, axis=-1, keepdims=True) + RMS_EPS)
    return (y * gain.astype(jnp.float32)).astype(x.dtype)


def split_columns(a):
    offsets = np.cumsum(SPLIT_SIZES)[:-1].tolist()
    return jnp.split(a, offsets, axis=-1)


def t5_bucket(dist):
    max_exact = NUM_BUCKETS // 2
    d = jnp.maximum(dist, 0)
    d_f = jnp.maximum(d, 1).astype(jnp.float32)
    large = max_exact + (jnp.log(d_f / max_exact) / math.log(MAX_DISTANCE / max_exact)
                         * (NUM_BUCKETS - max_exact)).astype(jnp.int32)
    large = jnp.minimum(large, NUM_BUCKETS - 1)
    return jnp.where(d < max_exact, d, large)


def to_blocks(a, block):
    b, l = a.shape[0], a.shape[1]
    return jnp.moveaxis(a.reshape(b, l // block, block, *a.shape[2:]), 1, 0)


def from_blocks(a):
    a = jnp.moveaxis(a, 0, 1)
    return a.reshape(a.shape[0], a.shape[1] * a.shape[2], *a.shape[3:])


def dsa_sparse_attention(q, k, v, q_idx, k_idx, w_idx, rel_bias):
    b, l, h, dh = q.shape
    topk = min(TOPK_MAX, l // 4)
    nb = l // SPARSE_Q_BLOCK
    key_pos = jnp.arange(l, dtype=jnp.int32)
    scale = dh ** -0.5
    gather = jax.vmap(lambda src, idx: src[idx])

    def block(args):
        qb, qib, wb, t0 = args
        t = t0 + jnp.arange(SPARSE_Q_BLOCK, dtype=jnp.int32)
        causal = key_pos[None, :] <= t[:, None]
        dots = jnp.einsum('btid,bsd->btis', qib, k_idx).astype(jnp.float32)
        score = jnp.einsum('bti,btis->bts', wb.astype(jnp.float32) * IDX_SCALE,
                           jax.nn.relu(dots))
        score = jnp.where(causal[None], score, -jnp.inf)
        _, sel = lax.top_k(score, topk)
        valid = sel <= t[None, :, None]
        k_sel = gather(k, sel)
        v_sel = gather(v, sel)
        logits = jnp.einsum('bthd,btkhd->bthk', qb, k_sel).astype(jnp.float32) * scale
        bias = rel_bias[t5_bucket(t[None, :, None] - sel)]
        logits = logits + jnp.moveaxis(bias.astype(jnp.float32), -1, 2)
        logits = jnp.where(valid[:, :, None, :], logits, -jnp.inf)
        p = jax.nn.softmax(logits, axis=-1)
        return jnp.einsum('bthk,btkhd->bthd', p.astype(v.dtype), v_sel)

    starts = jnp.arange(nb, dtype=jnp.int32) * SPARSE_Q_BLOCK
    out = lax.map(block, (to_blocks(q, SPARSE_Q_BLOCK), to_blocks(q_idx, SPARSE_Q_BLOCK),
                          to_blocks(w_idx, SPARSE_Q_BLOCK), starts))
    return from_blocks(out)


def stick_breaking_attention(q, k, v):
    b, l, h, dh = q.shape
    nb = l // Q_BLOCK
    key_pos = jnp.arange(l, dtype=jnp.int32)
    scale = dh ** -0.5

    def block(args):
        qb, t0 = args
        t = t0 + jnp.arange(Q_BLOCK, dtype=jnp.int32)
        strict = key_pos[None, :] < t[:, None]
        z = jnp.einsum('bthd,bshd->bhts', qb, k).astype(jnp.float32) * scale
        log_beta = jax.nn.log_sigmoid(z)
        log_one_minus = jnp.where(strict, jax.nn.log_sigmoid(-z), 0.0)
        suffix = lax.cumsum(log_one_minus, axis=3, reverse=True) - log_one_minus
        a = jnp.where(strict, jnp.exp(log_beta + suffix), 0.0)
        return jnp.einsum('bhts,bshd->bthd', a.astype(v.dtype), v)

    starts = jnp.arange(nb, dtype=jnp.int32) * Q_BLOCK
    out = lax.map(block, (to_blocks(q, Q_BLOCK), starts))
    return from_blocks(out)


def setup_inputs(seed: int = 0) -> dict:
    key = jax.random.key(seed)
    ks = jax.random.split(key, 8)
    x = jax.random.normal(ks[0], (BATCH, SEQ, D_MODEL), jnp.float32)
    norm_gain = 1.0 + 0.05 * jax.random.normal(ks[1], (DEPTH, D_MODEL), jnp.float32)
    w_in = jax.random.normal(ks[2], (DEPTH, D_MODEL, D_IN_PROJ), jnp.float32) * D_MODEL ** -0.5
    q_norm_gain = 1.0 + 0.05 * jax.random.normal(ks[3], (DEPTH, HEAD_DIM), jnp.float32)
    k_norm_gain = 1.0 + 0.05 * jax.random.normal(ks[4], (DEPTH, HEAD_DIM), jnp.float32)
    rel_bias = 0.5 * jax.random.normal(ks[5], (NUM_BUCKETS, N_HEADS_A), jnp.float32)
    w_out = jax.random.normal(ks[6], (DEPTH, D_MIX, D_MODEL), jnp.float32) * D_MIX ** -0.5
    return {"x": x, "norm_gain": norm_gain, "w_in": w_in, "q_norm_gain": q_norm_gain,
            "k_norm_gain": k_norm_gain, "rel_bias": rel_bias, "w_out": w_out}


def reference(x, norm_gain, w_in, q_norm_gain, k_norm_gain, rel_bias, w_out):
    b, l, _ = x.shape
    for layer in range(DEPTH):
        h = rms_norm(x, norm_gain[layer])
        proj = jnp.einsum('bld,dp->blp', h, w_in[layer])
        q_a, k_a, v_a, g_a, q_i, k_i, w_i, q_b, k_b, v_b, g_b = split_columns(proj)
        q_a = rms_norm(q_a.reshape(b, l, N_HEADS_A, HEAD_DIM), q_norm_gain[layer])
        k_a = rms_norm(k_a.reshape(b, l, N_HEADS_A, HEAD_DIM), k_norm_gain[layer])
        v_a = v_a.reshape(b, l, N_HEADS_A, HEAD_DIM)
        q_i = q_i.reshape(b, l, IDX_HEADS, IDX_DIM)
        o_a = dsa_sparse_attention(q_a, k_a, v_a, q_i, k_i, w_i, rel_bias)
        o_a = o_a.reshape(b, l, D_A) * jax.nn.silu(g_a)
        o_b = stick_breaking_attention(q_b.reshape(b, l, N_HEADS_B, HEAD_DIM),
                                       k_b.reshape(b, l, N_HEADS_B, HEAD_DIM),
                                       v_b.reshape(b, l, N_HEADS_B, HEAD_DIM))
        o_b = o_b.reshape(b, l, D_B) * jax.nn.silu(g_b)
        mixed = jnp.concatenate([o_a, o_b], axis=-1)
        x = x + jnp.einsum('blm,md->bld', mixed, w_out[layer])
    return x
```

```python
import math
import numpy as np
import ml_dtypes
import concourse.bass as bass
import concourse.mybir as mybir
from concourse.bass_utils import run_bass_kernel_spmd

F32 = mybir.dt.float32
BF16 = mybir.dt.bfloat16
AF = mybir.ActivationFunctionType
ALU = mybir.AluOpType
AX = mybir.AxisListType

D_MODEL = 1024
D_IN = 5200
IDX_SCALE = (16 * 64) ** -0.5
RMS_EPS = 1e-6
NEG_BIG = -30000.0


class _Op:
    __slots__ = ("eng", "fn", "deps", "tok_sem", "tok_val", "needed", "is_dma", "idx")

    def __init__(self, eng, fn, is_dma, idx):
        self.eng = eng
        self.fn = fn
        self.deps = []
        self.tok_sem = None
        self.tok_val = None
        self.needed = False
        self.is_dma = is_dma
        self.idx = idx


class Sched:
    ENGS = ("pe", "act", "dve", "pool", "sp")

    def __init__(self, nc):
        self.nc = nc
        self.ops = {e: [] for e in self.ENGS}
        self.last_w = {}
        self.readers = {}
        self.final_wait_ops = []

    def alias(self, dst_keys, src_keys):
        acc = []
        for k in src_keys:
            lw = self.last_w.get(k)
            if lw is not None:
                acc.append(lw)
            acc.extend(self.readers.get(k, ()))
        for k in dst_keys:
            self.readers.setdefault(k, [])
            self.readers[k] = list(self.readers[k]) + acc

    def _add(self, eng, fn, reads, writes, is_dma=False, dma_sem=None):
        op = _Op(eng, fn, is_dma, len(self.ops[eng]))
        excl = [k for k in reads if isinstance(k, str) and len(k) == 2 and k[0] == "B" and k[1].isdigit()]
        if excl:
            reads = [k for k in reads if k not in excl]
            writes = list(writes) + excl
        cand = []
        for k in reads:
            lw = self.last_w.get(k)
            if lw is not None:
                cand.append((lw, True))
        for k in writes:
            lw = self.last_w.get(k)
            if lw is not None:
                cand.append((lw, False))
            for r in self.readers.get(k, ()):
                cand.append((r, False))
        best = {}
        for d, raw in cand:
            if d is op:
                continue
            if (not d.is_dma) and (not is_dma) and d.eng == eng:
                if (not raw) or eng == "pe":
                    continue
            if d.is_dma:
                key = ("dma", d.tok_sem)
            else:
                key = ("eng", d.eng)
            cur = best.get(key)
            if cur is None or d.idx > cur.idx:
                best[key] = d
        op.deps = list(best.values())
        for k in writes:
            self.last_w[k] = op
            self.readers[k] = []
        for k in reads:
            self.readers.setdefault(k, []).append(op)
        if is_dma:
            op.tok_sem = dma_sem
        self.ops[eng].append(op)
        return op

    def pe(self, fn, reads=(), writes=()):
        return self._add("pe", fn, reads, writes)

    def act(self, fn, reads=(), writes=()):
        return self._add("act", fn, reads, writes)

    def dve(self, fn, reads=(), writes=()):
        return self._add("dve", fn, reads, writes)

    def pool(self, fn, reads=(), writes=()):
        return self._add("pool", fn, reads, writes)

    def dma(self, fn, reads=(), writes=(), sem="d0", queue="sp"):
        return self._add(queue, fn, reads, writes, is_dma=True, dma_sem=sem)

    def finalize(self):
        for e in self.ENGS:
            for op in self.ops[e]:
                for d in op.deps:
                    d.needed = True
        for op in self.final_wait_ops:
            op.needed = True
        cnt = {e: 0 for e in self.ENGS}
        dcnt = {}
        for e in self.ENGS:
            for op in self.ops[e]:
                if op.is_dma:
                    dcnt[op.tok_sem] = dcnt.get(op.tok_sem, 0) + 16
                    op.tok_val = dcnt[op.tok_sem]
                elif op.needed:
                    cnt[e] += 1
                    op.tok_sem = "E_" + e
                    op.tok_val = cnt[e]
        names = set(dcnt.keys()) | {"E_" + e for e in self.ENGS if cnt[e] > 0}
        return sorted(names)

    def emit(self, engine_obj, eng, sems):
        waited = {}
        for op in self.ops[eng]:
            for d in op.deps:
                s, v = d.tok_sem, d.tok_val
                if waited.get(s, 0) >= v:
                    continue
                engine_obj.wait_ge(sems[s], v)
                waited[s] = v
            ins = op.fn(engine_obj)
            if op.is_dma:
                ins.then_inc(sems[op.tok_sem], 16)
            elif op.needed:
                ins.then_inc(sems[op.tok_sem], 1)
        if eng == "sp":
            for op in self.final_wait_ops:
                s, v = op.tok_sem, op.tok_val
                if waited.get(s, 0) >= v:
                    continue
                engine_obj.wait_ge(sems[s], v)
                waited[s] = v

    def run(self):
        nc = self.nc
        names = self.finalize()
        sems = {n: nc.alloc_semaphore("s_" + n) for n in names}
        sch = self
        with nc.Block() as block:
            @block.sync
            def _(e):
                sch.emit(e, "sp", sems)

            @block.scalar
            def _(e):
                sch.emit(e, "act", sems)

            @block.vector
            def _(e):
                sch.emit(e, "dve", sems)

            @block.tensor
            def _(e):
                sch.emit(e, "pe", sems)

            @block.gpsimd
            def _(e):
                sch.emit(e, "pool", sems)


def _t5_bucket_np(d):
    d = np.maximum(d, 0).astype(np.int64)
    d_f = np.maximum(d, 1).astype(np.float32)
    large = 16 + (np.log(d_f / np.float32(16)) / np.float32(math.log(128 / 16))
                  * np.float32(16)).astype(np.int32)
    large = np.minimum(large, 31)
    return np.where(d < 16, d, large)


def _consts():
    bf = ml_dtypes.bfloat16
    c = {}
    c["ident"] = np.eye(128, dtype=np.float32).astype(bf)
    j = np.arange(128)[:, None]
    s = np.arange(128)[None, :]
    c["negtri"] = np.where(j >= s, -1.0, 0.0).astype(np.float32).astype(bf)
    c["strict"] = np.where(j < s, 1.0, 0.0).astype(np.float32).astype(bf)
    c["blockones"] = np.where((j // 64) == (s // 64), 1.0 / 64, 0.0).astype(np.float32).astype(bf)
    c["onescol"] = np.ones((128, 1), np.float32).astype(bf)
    oh = np.zeros((33, 384), np.float32)
    for i in range(384):
        d = i - 128
        if d < 0:
            oh[32, i] = NEG_BIG
        else:
            b = int(_t5_bucket_np(np.array([d]))[0])
            oh[b, i] += 1.0
            oh[31, i] -= 1.0
    c["onehot"] = oh
    return c


def build(L=2048, niter=22, dbg=False):
    NT = L // 128
    NCH = L // 512
    TOPK = min(256, L // 4)
    nc = bass.Bass("TRN2", target_bir_lowering=False, dynamic_dma_scratch_size=8192)

    def din(name, shape, dt=F32):
        return nc.dram_tensor(name, list(shape), dt, kind="ExternalInput").ap()

    x = din("x", [L, D_MODEL])
    w_in = din("w_in", [D_MODEL, D_IN])
    w_out = din("w_out", [D_MODEL, D_MODEL])
    gain_d = din("gain", [128, 8])
    gq_d = din("gq", [128, 1])
    gk_d = din("gk", [128, 1])
    b31_d = din("b31", [128, 8])
    relb_d = din("relb", [33, 8])
    onehot_d = din("onehot", [33, 384])
    ident_d = din("ident", [128, 128], BF16)
    negtri_d = din("negtri", [128, 128], BF16)
    strict_d = din("strict", [128, 128], BF16)
    blockones_d = din("blockones", [128, 128], BF16)
    onescol_d = din("onescol", [128, 1], BF16)
    out = nc.dram_tensor("out", [L, D_MODEL], F32, kind="ExternalOutput").ap()
    scr1_t = nc.dram_tensor("scr1", [8, 384], F32, kind="Internal")
    scr2_t = nc.dram_tensor("scr2", [128, 8 * 384], F32, kind="Internal")

    def sb(name, shape, dt):
        return nc.alloc_sbuf_tensor(name, list(shape), dt)

    gain = sb("gain_s", [128, 8], F32)
    gq = sb("gq_s", [128, 1], F32)
    gk = sb("gk_s", [128, 1], F32)
    b31 = sb("b31_s", [128, 8], F32)
    relb = sb("relb_s", [33, 8], F32)
    onehot = sb("onehot_s", [33, 384], F32)
    ident = sb("ident_s", [128, 128], BF16)
    negtri = sb("negtri_s", [128, 128], BF16)
    strict = sb("strict_s", [128, 128], BF16)
    blockones = sb("blockones_s", [128, 128], BF16)
    onescol = sb("onescol_s", [128, 1], BF16)
    fb = sb("fb_s", [8, 384], F32)
    btab = sb("btab", [128, 8, 2, 128], F32)
    kTa = sb("kTa", [128, 4, L], BF16)
    kTb = sb("kTb", [128, 4, L], BF16)
    kTi = sb("kTi", [128, L], BF16)
    Va = sb("Va", [128, NT, 8, 65], BF16)
    Vb = sb("Vb", [128, NT, 8, 64], BF16)
    hTc = sb("hTc", [128, 8, 512], BF16)
    qTa = sb("qTa", [128, 8, 512], BF16)
    qTb = sb("qTb", [128, 4, 512], BF16)
    qTi_raw = sb("qTi", [128, 4096], BF16)
    qTi = qTi_raw[:].rearrange("p (c t) -> p c t", c=8)
    mixed = qTi_raw[:].rearrange("p (j f) -> p j f", j=4)
    sg = sb("sg", [128, 4, 1024], BF16)
    ws = sb("ws", [128, 4, 16], F32)
    maskT = sb("maskT", [128, NT, 512], BF16)
    slab = [sb("slab0", [128, 8, 512], BF16), sb("slab1", [128, 8, 512], BF16)]
    score = [sb("score0", [128, L], F32), sb("score1", [128, L], F32)]
    accB_t = sb("accB", [128, 2048], F32)
    accB = accB_t[:].rearrange("p (j h d) -> p j h d", j=4, h=8)
    maskrow = [sb("maskrow0", [128, L], BF16), sb("maskrow1", [128, L], BF16)]
    xt = [sb("xt0", [128, 1024], F32), sb("xt1", [128, 1024], F32)]
    xs = [sb("xs0", [128, 1024], BF16), sb("xs1", [128, 1024], BF16)]
    junk = sb("junk", [128, 1024], BF16)
    NP32 = 6
    NP16 = 8
    p32 = [sb("p32_%d" % i, [128, 512], F32) for i in range(NP32)]
    p16 = [sb("p16_%d" % i, [128, 512], BF16) for i in range(NP16)]
    small = sb("small", [128, 64], F32)
    bis = sb("bis", [128, 2, 8 + 32], F32)
    halves = sb("halves", [128, 32], F32)
    rden = sb("rden", [128, 2, 4], F32)
    gsb = sb("gsb", [128, 2, 4], F32)

    banks = [nc.alloc_psum_tensor("B%d" % i, [128, 512], F32) for i in range(8)]
    B0bf = banks[0][:].bitcast(BF16)

    S = Sched(nc)
    cnt32 = [0]
    cnt16 = [0]

    def get32():
        i = 3 + (cnt32[0] % 3)
        cnt32[0] += 1
        return p32[i], ("p32", i)

    def get16():
        i = cnt16[0] % 4
        cnt16[0] += 1
        return p16[i], ("p16", i)

    def cload(dst, src, key):
        S.dma(lambda e: e.dma_start(out=dst[:], in_=src), writes=[key], sem="c_" + key)

    cload(gain, gain_d, "gain")
    cload(gq, gq_d, "gq")
    cload(gk, gk_d, "gk")
    cload(b31, b31_d, "b31")
    cload(relb, relb_d, "relb")
    cload(onehot, onehot_d, "onehot")
    cload(ident, ident_d, "ident")
    cload(negtri, negtri_d, "negtri")
    cload(strict, strict_d, "strict")
    cload(blockones, blockones_d, "blockones")
    cload(onescol, onescol_d, "onescol")
    for k in range(niter + 1):
        S.dve(lambda e, k=k: e.memset(halves[:, k:k + 1], 2.0 ** -(k + 1)), writes=[("halves", k)])
    S.pool(lambda e: e.memset(Va[:, :, :, 64:65], 1.0), writes=["Va_ones"])
    S.pool(lambda e: e.memset(qTa[:], 0.0), writes=[("qTa", f) for f in range(4)])

    S.pe(lambda e: e.matmul(out=banks[3][0:8, 0:384], lhsT=relb[:, :], rhs=onehot[:, :], start=True, stop=True),
         reads=["relb", "onehot"], writes=["B3"])
    S.act(lambda e: e.activation(out=fb[:], in_=banks[3][0:8, 0:384], func=AF.Copy), reads=["B3"], writes=["fb"])
    S.dma(lambda e: e.dma_start(out=scr1_t.ap(), in_=fb[:]), reads=["fb"], writes=["scr1"], sem="c_scr1")
    scr2_v = scr2_t.ap().rearrange("p (h i) -> p h i", h=8)
    S.dma(lambda e: e.dma_start(out=scr2_v, in_=scr1_t.ap().partition_broadcast(128)),
          reads=["scr1"], writes=["scr2"], sem="c_scr2")
    for r in range(2):
        src = bass.AP(tensor=scr2_t, offset=128 * (r + 1), ap=[[8 * 384 - 1, 128], [384, 8], [1, 128]])
        S.dma(lambda e, r=r, src=src: e.dma_start(out=btab[:, :, r, :], in_=src),
              reads=["scr2"], writes=[("btab", r)], sem="c_btab%d" % r)

    slab_ctr = [0]

    def load_slab(pieces, src=None):
        if src is None:
            src = w_in
        i = slab_ctr[0] % 2
        slab_ctr[0] += 1
        key = ("slab", i)
        for (d0, s0, n) in pieces:
            S.dma(lambda e, i=i, d0=d0, s0=s0, n=n, src=src: e.dma_start(
                out=slab[i][:, :, d0:d0 + n],
                in_=src[:, s0:s0 + n].rearrange("(c p) n -> p c n", p=128)),
                writes=[key], sem="w%d" % i, queue="pool")
        return slab[i], key

    pbank_ctr = [0]

    def proj_bank():
        i = 1 + (pbank_ctr[0] % 2)
        pbank_ctr[0] += 1
        return banks[i], "B%d" % i

    def fm_matmuls(sl, skey, f, ncols_feat=128, col0=None, bank=None):
        bk, bkey = proj_bank() if bank is None else bank
        c0 = f * 128 if col0 is None else col0
        for c8 in range(8):
            S.pe(lambda e, c8=c8, bk=bk, sl=sl, c0=c0: e.matmul(
                out=bk[0:ncols_feat, :], lhsT=sl[:, c8, c0:c0 + ncols_feat], rhs=hTc[:, c8, :],
                start=(c8 == 0), stop=(c8 == 7)),
                reads=[skey, ("hTc", c8)], writes=[bkey])
        return bk, bkey

    def tm_matmuls(sl, skey, j, ncols=512, col0=0, bank=None):
        bk, bkey = proj_bank() if bank is None else bank
        for c8 in range(8):
            S.pe(lambda e, c8=c8, bk=bk, sl=sl: e.matmul(
                out=bk[:, 0:ncols], lhsT=hTc[:, c8, j * 128:(j + 1) * 128], rhs=sl[:, c8, col0:col0 + ncols],
                start=(c8 == 0), stop=(c8 == 7)),
                reads=[skey, ("hTc", c8)], writes=[bkey])
        return bk, bkey

    def fm_g(sl, skey, f, bank):
        bk, bkey = bank
        c0 = f * 128
        for c8 in range(8):
            S.pe(lambda e, c8=c8, bk=bk, sl=sl, c0=c0: e.matmul(
                out=bk[:, :], lhsT=sl[:, c8, c0:c0 + 128], rhs=hTc[:, c8, :],
                start=(c8 == 0), stop=(c8 == 7)),
                reads=[skey, ("hTc", c8)], writes=[bkey])
            if c8 % 2 == 1 and c8 < 7:
                yield 0.25
        return bk, bkey

    def tm_g(sl, skey, j, bank):
        bk, bkey = bank
        for c8 in range(8):
            S.pe(lambda e, c8=c8, bk=bk, sl=sl: e.matmul(
                out=bk[:, 0:512], lhsT=hTc[:, c8, j * 128:(j + 1) * 128], rhs=sl[:, c8, 0:512],
                start=(c8 == 0), stop=(c8 == 7)),
                reads=[skey, ("hTc", c8)], writes=[bkey])
            if c8 % 2 == 1 and c8 < 7:
                yield 0.25
        return bk, bkey

    def qknorm(bk, bkey, gvec, gkey, dst_ap, dst_key):
        sq, sqk = get16()
        S.act(lambda e: e.activation(out=sq[:], in_=bk[:], func=AF.Square), reads=[bkey], writes=[sqk])
        S.pe(lambda e: e.matmul(out=banks[3][:], lhsT=blockones[:], rhs=sq[:], start=True, stop=True),
             reads=[sqk, "blockones"], writes=["B3"])
        lt, ltk = get32()
        S.act(lambda e: e.activation(out=lt[:], in_=banks[3][:], func=AF.Ln, bias=RMS_EPS, scale=1.0),
              reads=["B3"], writes=[ltk])
        S.act(lambda e: e.activation(out=lt[:], in_=lt[:], func=AF.Exp, scale=-0.5), reads=[ltk], writes=[ltk])
        if isinstance(dst_ap, tuple):
            for hf, d_ap in enumerate(dst_ap):
                lo_, hi_ = 64 * hf, 64 * hf + 64
                S.dve(lambda e, d_ap=d_ap, lo_=lo_, hi_=hi_: e.scalar_tensor_tensor(
                    out=d_ap, in0=bk[lo_:hi_, :], scalar=gvec[lo_:hi_, 0:1], in1=lt[lo_:hi_, :],
                    op0=ALU.mult, op1=ALU.mult),
                    reads=[bkey, ltk, gkey], writes=[dst_key])
        else:
            S.dve(lambda e: e.scalar_tensor_tensor(out=dst_ap, in0=bk[:], scalar=gvec[:, 0:1], in1=lt[:],
                                                   op0=ALU.mult, op1=ALU.mult),
                  reads=[bkey, ltk, gkey], writes=[dst_key])

    evac_ctr = [0]

    def evac_copy(dst_ap, src_ap, reads, writes, scale=None):
        i = evac_ctr[0]
        evac_ctr[0] += 1
        if i % 2 == 0:
            if scale is None:
                S.act(lambda e: e.activation(out=dst_ap, in_=src_ap, func=AF.Copy), reads=reads, writes=writes)
            else:
                S.act(lambda e: e.activation(out=dst_ap, in_=src_ap, func=AF.Copy, scale=scale),
                      reads=reads, writes=writes)
        else:
            if scale is None:
                S.dve(lambda e: e.tensor_copy(out=dst_ap, in_=src_ap), reads=reads, writes=writes)
            else:
                S.dve(lambda e: e.tensor_scalar(out=dst_ap, in0=src_ap, scalar1=scale, scalar2=None, op0=ALU.mult),
                      reads=reads, writes=writes)

    out_ops = []
    wo_pref = {}

    R32 = p32[0:3]
    E32 = p32[3:5]
    SPt = p16[0:2]
    APt = p16[2:4]
    Et = p16[4:6]
    EMt = p16[6:8]

    def emit_norm(c):
        for j in range(4):
            i = 4 * c + j
            sl_ = i % 2
            S.dma(lambda e, i=i, sl_=sl_: e.dma_start(out=xt[sl_][:], in_=x[i * 128:(i + 1) * 128, :]),
                  writes=[("xt", sl_)], sem="x%d" % sl_)
            S.act(lambda e, sl_=sl_: e.activation(out=junk[:], in_=xt[sl_][:], func=AF.Square,
                                                  accum_out=small[:, sl_:sl_ + 1]),
                  reads=[("xt", sl_)], writes=[("ss", sl_), "junk"])
            S.act(lambda e, sl_=sl_: e.activation(out=small[:, 2 + sl_:3 + sl_], in_=small[:, sl_:sl_ + 1],
                                                  func=AF.Ln, scale=1.0 / D_MODEL, bias=RMS_EPS),
                  reads=[("ss", sl_)], writes=[("lnv", sl_)])
            S.act(lambda e, sl_=sl_: e.activation(out=small[:, 4 + sl_:5 + sl_], in_=small[:, 2 + sl_:3 + sl_],
                                                  func=AF.Exp, scale=-0.5),
                  reads=[("lnv", sl_)], writes=[("rstd", sl_)])
            S.dve(lambda e, sl_=sl_: e.tensor_scalar(out=xs[sl_][:], in0=xt[sl_][:], scalar1=small[:, 4 + sl_:5 + sl_],
                                                     scalar2=None, op0=ALU.mult),
                  reads=[("xt", sl_), ("rstd", sl_)], writes=[("xs", sl_)])
            for c8 in range(8):
                S.pe(lambda e, c8=c8, sl_=sl_: e.transpose(out=B0bf[:, c8 * 128:(c8 + 1) * 128],
                                                           in_=xs[sl_][:, c8 * 128:(c8 + 1) * 128], identity=ident[:]),
                     reads=[("xs", sl_), "ident"], writes=["B0"])
            for c8 in range(8):
                dst = hTc[:, c8, j * 128:(j + 1) * 128]
                srcp = B0bf[:, c8 * 128:(c8 + 1) * 128]
                if j % 2 == 0:
                    S.dve(lambda e, dst=dst, srcp=srcp, c8=c8: e.tensor_scalar(
                        out=dst, in0=srcp, scalar1=gain[:, c8:c8 + 1], scalar2=None, op0=ALU.mult),
                        reads=["B0", "gain"], writes=[("hTc", c8)])
                else:
                    S.act(lambda e, dst=dst, srcp=srcp, c8=c8: e.activation(
                        out=dst, in_=srcp, func=AF.Copy, scale=gain[:, c8:c8 + 1]),
                        reads=["B0", "gain"], writes=[("hTc", c8)])

    def emit_proj1(c):
        tok0 = c * 512
        sl, sk = load_slab([(0, 3072, 64), (64, 3072, 64), (128, 3136, 16)])
        bk, bkey = fm_matmuls(sl, sk, 0)
        evac_copy(kTi[:, tok0:tok0 + 512], bk[:], [bkey], [("kTi", c)])
        for j in range(4):
            bk, bkey = tm_matmuls(sl, sk, j, ncols=16, col0=128)
            S.dve(lambda e, bk=bk, j=j: e.tensor_scalar(out=ws[:, j, :], in0=bk[:, 0:16], scalar1=IDX_SCALE,
                                                        scalar2=None, op0=ALU.mult),
                  reads=[bkey], writes=[("ws", j)])
        S.alias([("qTi", f8) for f8 in range(8)], ["mixed"])
        for half in range(2):
            sl, sk = load_slab([(0, 2048 + 512 * half, 512)])
            for f in range(4):
                bk, bkey = fm_matmuls(sl, sk, f)
                evac_copy(qTi[:, 4 * half + f, :], bk[:], [bkey], [("qTi", 4 * half + f)])

    def gen_proj2(c):
        tok0 = c * 512
        pb2 = [0]

        def bank2():
            i = 6 + (pb2[0] % 2)
            pb2[0] += 1
            return banks[i], "B%d" % i

        sl, sk = load_slab([(0, 512, 512)])
        for f in range(4):
            bk, bkey = yield from fm_g(sl, sk, f, bank2())
            qknorm(bk, bkey, gk, "gk", kTa[:, f, tok0:tok0 + 512], ("kTa", f, c))
            yield 0.25
        sl, sk = load_slab([(0, 1024, 512)])
        for j in range(4):
            i = 4 * c + j
            bk, bkey = yield from tm_g(sl, sk, j, bank2())
            evac_copy(Va[:, i, :, 0:64], bk[:].rearrange("p (h d) -> p h d", h=8), [bkey, "Va_ones"], [("Va", i)])
            yield 0.25
        sl, sk = load_slab([(0, 3664, 512)])
        for f in range(4):
            bk, bkey = yield from fm_g(sl, sk, f, bank2())
            evac_copy(kTb[:, f, tok0:tok0 + 512], bk[:], [bkey], [("kTb", f, c)], scale=0.125)
            yield 0.25
        sl, sk = load_slab([(0, 4176, 512)])
        for j in range(4):
            i = 4 * c + j
            bk, bkey = yield from tm_g(sl, sk, j, bank2())
            evac_copy(Vb[:, i, :, :], bk[:].rearrange("p (h d) -> p h d", h=8), [bkey], [("Vb", i)])
            yield 0.25
        sl, sk = load_slab([(0, 0, 512)])
        for f in range(4):
            bk, bkey = yield from fm_g(sl, sk, f, bank2())
            qknorm(bk, bkey, gq, "gq", (qTa[0:64, 2 * f, :], qTa[64:128, 2 * f + 1, :]), ("qTa", f))
            yield 0.25
        sl, sk = load_slab([(0, 3152, 512)])
        for f in range(4):
            bk, bkey = yield from fm_g(sl, sk, f, bank2())
            evac_copy(qTb[:, f, :], bk[:], [bkey], [("qTb", f)])
            yield 0.25
        for half, col in ((0, 1536), (1, 4688)):
            sl, sk = load_slab([(0, col, 512)])
            for j in range(4):
                bk, bkey = yield from tm_g(sl, sk, j, bank2())
                S.act(lambda e, bk=bk, j=j, half=half: e.activation(
                    out=sg[:, j, half * 512:(half + 1) * 512], in_=bk[:], func=AF.Silu),
                    reads=[bkey], writes=[("sg", j, half)])
                yield 0.25

    W_IDX = 1.0
    W_BIS = 2.2

    def gen_index_bisect(c):
        dctr = [0]
        rctr = [0]

        def indexer_tiles(js):
          for hh in range(16):
            f8 = hh // 2
            base = 64 * (hh % 2)
            for scn in range(c + 1):
              for j in js:
                sc_t = score[j % 2]
                sckey = ("score", j % 2)
                if True:
                    w = 512 if scn < c else (j + 1) * 128
                    bi = 1 + (dctr[0] % 2)
                    dctr[0] += 1
                    bk = banks[bi]
                    bkey = "B%d" % bi
                    S.pe(lambda e, bk=bk, f8=f8, base=base, scn=scn, w=w, j=j: e.matmul(
                        out=bk[:, 0:w], lhsT=qTi[base:base + 64, f8, j * 128:(j + 1) * 128],
                        rhs=kTi[base:base + 64, scn * 512:scn * 512 + w], start=True, stop=True),
                        reads=[("qTi", f8), ("kTi", scn)], writes=[bkey])
                    ri = rctr[0] % 3
                    rctr[0] += 1
                    r, rk = R32[ri], ("p32", ri)
                    S.act(lambda e, bk=bk, r=r, w=w: e.activation(out=r[:, 0:w], in_=bk[:, 0:w], func=AF.Relu),
                          reads=[bkey], writes=[rk])
                    dst = sc_t[:, scn * 512:scn * 512 + w]
                    if hh == 0:
                        S.dve(lambda e, dst=dst, r=r, w=w, j=j, hh=hh: e.tensor_scalar(
                            out=dst, in0=r[:, 0:w], scalar1=ws[:, j, hh:hh + 1], scalar2=None, op0=ALU.mult),
                            reads=[rk, ("ws", j)], writes=[sckey])
                    else:
                        S.dve(lambda e, dst=dst, r=r, w=w, j=j, hh=hh: e.scalar_tensor_tensor(
                            out=dst, in0=r[:, 0:w], scalar=ws[:, j, hh:hh + 1], in1=dst,
                            op0=ALU.mult, op1=ALU.add),
                            reads=[rk, ("ws", j), sckey], writes=[sckey])
            yield W_IDX

        def bisect_tiles(js):
            bs = [j % 2 for j in js]
            b0, b1 = min(bs), max(bs) + 1
            bkeys = [("bis", b) for b in bs]
            for j in js:
                i = 4 * c + j
                Si = 128 * (i + 1)
                b = j % 2
                sc_t = score[b]
                sckey = ("score", b)
                bk_ = ("bis", b)
                S.dve(lambda e, sc_t=sc_t, Si=Si, b=b: e.tensor_reduce(out=bis[:, b, 0:1], in_=sc_t[:, 0:Si],
                                                                       axis=AX.X, op=ALU.min),
                      reads=[sckey], writes=[bk_])
                S.pool(lambda e, sc_t=sc_t, Si=Si: e.affine_select(
                    out=sc_t[:, Si - 128:Si], in_=sc_t[:, Si - 128:Si], pattern=[[-1, 128]],
                    compare_op=ALU.is_ge, fill=-3.0e38, base=0, channel_multiplier=1),
                    reads=[sckey, bk_], writes=[sckey])
                S.dve(lambda e, sc_t=sc_t, Si=Si, b=b: e.tensor_reduce(out=bis[:, b, 1:2], in_=sc_t[:, 0:Si],
                                                                       axis=AX.X, op=ALU.max),
                      reads=[sckey, bk_], writes=[bk_])
                yield
            if len(bs) == 2:
                S.dve(lambda e: e.tensor_tensor(out=bis[:, 0, 0:1], in0=bis[:, 0, 0:1], in1=bis[:, 1, 0:1], op=ALU.min),
                      reads=bkeys, writes=[("bis", 0)])
                S.dve(lambda e: e.tensor_tensor(out=bis[:, 0, 1:2], in0=bis[:, 0, 1:2], in1=bis[:, 1, 1:2], op=ALU.max),
                      reads=bkeys, writes=[("bis", 0)])
            S.dve(lambda e: e.tensor_tensor(out=bis[:, b0, 2:3], in0=bis[:, b0, 1:2], in1=bis[:, b0, 0:1],
                                            op=ALU.subtract), reads=bkeys, writes=[("bis", b0)])
            S.dve(lambda e: e.tensor_scalar(out=bis[:, b0, 8:8 + niter + 1], in0=halves[:, 0:niter + 1],
                                            scalar1=bis[:, b0, 2:3], scalar2=None, op0=ALU.mult),
                  reads=bkeys + [("halves", k) for k in range(niter + 1)], writes=[("bis", b0)])
            for b in bs:
                S.dve(lambda e, b=b: e.tensor_tensor(out=bis[:, b, 3:4], in0=bis[:, b0, 0:1], in1=bis[:, b0, 8:9],
                                                     op=ALU.add), reads=bkeys, writes=[("bis", b)])
            yield
            for k in range(niter):
                for j in js:
                    i = 4 * c + j
                    Si = 128 * (i + 1)
                    b = j % 2
                    S.dve(lambda e, sc_t=score[b], Si=Si, b=b, jk=R32[b][:].bitcast(mybir.dt.uint8): e.tensor_scalar(
                        out=jk[:, 0:Si], in0=sc_t[:, 0:Si], scalar1=bis[:, b, 3:4], scalar2=None,
                        op0=ALU.is_ge, op1=ALU.add, accum_out=bis[:, b, 4:5]),
                        reads=[("score", b), ("bis", b)], writes=[("p32", b), ("bis", b)])
                S.dve(lambda e, k=k: e.tensor_scalar(
                    out=bis[:, b0:b1, 5], in0=bis[:, b0:b1, 4], scalar1=float(TOPK) - 0.5,
                    scalar2=bis[:, b0, 8 + k:9 + k], op0=(ALU.is_ge if k < niter - 1 else ALU.is_lt),
                    op1=ALU.mult),
                    reads=bkeys, writes=bkeys)
                if k < niter - 1:
                    S.dve(lambda e, k=k: e.scalar_tensor_tensor(
                        out=bis[:, b0:b1, 3], in0=bis[:, b0:b1, 3], scalar=bis[:, b0, 9 + k:10 + k],
                        in1=bis[:, b0:b1, 5], op0=ALU.subtract, op1=ALU.add),
                        reads=bkeys, writes=bkeys)
                else:
                    S.dve(lambda e: e.tensor_tensor(out=bis[:, b0:b1, 6], in0=bis[:, b0:b1, 3],
                                                    in1=bis[:, b0:b1, 5], op=ALU.subtract),
                          reads=bkeys, writes=bkeys)
                yield W_BIS
            for j in js:
                i = 4 * c + j
                Si = 128 * (i + 1)
                b = j % 2
                sc_t = score[b]
                sckey = ("score", b)
                bk_ = ("bis", b)
                mr = maskrow[b]
                mrk = ("maskrow", b)
                S.dve(lambda e, sc_t=sc_t, Si=Si, b=b, mr=mr: e.tensor_scalar(
                    out=mr[:, 0:Si], in0=sc_t[:, 0:Si], scalar1=bis[:, b, 6:7], scalar2=None, op0=ALU.is_ge),
                    reads=[sckey, bk_], writes=[mrk])
                a0 = 0
                while a0 <= i:
                    n = min(8, i + 1 - a0)
                    for q in range(n):
                        a = a0 + q
                        S.pe(lambda e, mr=mr, a=a, q=q: e.transpose(out=B0bf[:, q * 128:(q + 1) * 128],
                                                                    in_=mr[:, a * 128:(a + 1) * 128], identity=ident[:]),
                             reads=[mrk, "ident"], writes=["B0"])
                    dst = maskT[:, a0:a0 + n, j * 128:(j + 1) * 128]
                    srcp = B0bf[:, 0:n * 128].rearrange("p (n t) -> p n t", n=n)
                    S.act(lambda e, dst=dst, srcp=srcp: e.activation(out=dst, in_=srcp, func=AF.Copy),
                          reads=["B0"], writes=[("maskT", j)])
                    a0 += n
                    yield

        for jp in range(2):
            js = []
            for j in (2 * jp, 2 * jp + 1):
                i = 4 * c + j
                if 128 * (i + 1) <= TOPK:
                    S.pool(lambda e, i=i, j=j: e.memset(maskT[:, 0:i + 1, j * 128:(j + 1) * 128], 1.0),
                           writes=[("maskT", j)])
                    yield
                else:
                    js.append(j)
            if js:
                yield from indexer_tiles(js)
                yield from bisect_tiles(js)

    def gen_attnB(c):
        na = 4 * c + 4
        tiles = [(h, a) for h in range(8) for a in range(na)]
        NTL = len(tiles)
        zb, zkey = banks[4], "B4"
        lb, lkey = banks[5], "B5"
        st = {}

        def info(n):
            h, a = tiles[n]
            f = h // 2
            base = 64 * (h % 2)
            i0 = max(0, a - 4 * c)
            w = 512 - i0 * 128
            diag = a >= 4 * c
            kl = kTb[base:base + 64, f, a * 128:(a + 1) * 128]
            qr = qTb[base:base + 64, f, i0 * 128:512]
            spi = (0, 1, 4)[n % 3]
            SP, SPk = p16[spi], ("p16", spi)
            Ap, Apk = APt[n % 2], ("p16", 2 + n % 2)
            pbi = 6 + (n % 2)
            return h, a, f, i0, w, diag, kl, qr, SP, SPk, Ap, Apk, banks[pbi], "B%d" % pbi, n % 2

        def stage_a(n):
            h, a, f, i0, w, diag, kl, qr, SP, SPk, Ap, Apk, pbk, pkey, gb = info(n)
            S.pe(lambda e: e.matmul(out=zb[:, 0:w], lhsT=kl, rhs=qr, start=True, stop=True),
                 reads=[("kTb", f, a // 4), ("qTb", f)], writes=[zkey])
            e1, e1k = E32[n % 2], ("p32", 3 + n % 2)
            S.act(lambda e: e.activation(out=e1[:, 0:w], in_=zb[:, 0:w], func=AF.Exp), reads=[zkey], writes=[e1k])
            S.act(lambda e: e.activation(out=SP[:, 0:w], in_=e1[:, 0:w], func=AF.Ln, bias=1.0, scale=1.0),
                  reads=[e1k], writes=[SPk])
            if diag:
                S.pool(lambda e: e.tensor_tensor(out=SP[:, 0:128], in0=SP[:, 0:128], in1=strict[:], op=ALU.mult),
                       reads=[SPk, "strict"], writes=[SPk])

        def stage_b(n):
            h, a, f, i0, w, diag, kl, qr, SP, SPk, Ap, Apk, pbk, pkey, gb = info(n)
            S.pe(lambda e: e.matmul(out=lb[:, 0:w], lhsT=kl, rhs=qr, start=True, stop=False),
                 reads=[("kTb", f, a // 4), ("qTb", f)], writes=[lkey])
            S.pe(lambda e: e.matmul(out=lb[:, 0:w], lhsT=negtri[:], rhs=SP[:, 0:w], start=False, stop=True),
                 reads=[SPk, "negtri"], writes=[lkey])
            if a > 0:
                fcs = True
                for jj in range(i0, 4):
                    blk = (jj - i0) * 128
                    S.pe(lambda e, blk=blk, jj=jj, fcs=fcs: e.matmul(
                        out=banks[3][:, jj:jj + 1], lhsT=SP[:, blk:blk + 128], rhs=onescol[:, 0:1],
                        start=fcs, stop=(jj == 3), skip_group_check=True),
                        reads=[SPk, "onescol"], writes=["B3"])
                    fcs = False
                S.act(lambda e: e.activation(out=gsb[:, gb, i0:4], in_=banks[3][:, i0:4], func=AF.Exp, scale=-1.0),
                      reads=["B3"], writes=[("gsb", gb)])
            S.act(lambda e: e.activation(out=Ap[:, 0:w], in_=lb[:, 0:w], func=AF.Exp), reads=[lkey], writes=[Apk])
            if diag:
                S.pool(lambda e: e.tensor_tensor(out=Ap[:, 0:128], in0=Ap[:, 0:128], in1=strict[:], op=ALU.mult),
                       reads=[Apk, "strict"], writes=[Apk])

        def stage_c(n):
            h, a, f, i0, w, diag, kl, qr, SP, SPk, Ap, Apk, pbk, pkey, gb = info(n)
            fp = True
            for jj in range(i0, 4):
                blk = (jj - i0) * 128
                S.pe(lambda e, jj=jj, blk=blk, fp=fp: e.matmul(
                    out=pbk[:, jj * 64:(jj + 1) * 64], lhsT=Ap[:, blk:blk + 128], rhs=Vb[:, a, h, :],
                    start=fp, stop=(jj == 3), skip_group_check=True),
                    reads=[Apk, ("Vb", a)], writes=[pkey])
                fp = False

        def stage_d1(n):
            h, a, f, i0, w, diag, kl, qr, SP, SPk, Ap, Apk, pbk, pkey, gb = info(n)
            if a == 0:
                return
            nb = 4 - i0
            accv = accB[:, i0:4, h, :]
            gv = gsb[:, gb, i0:4].unsqueeze(2).broadcast_to([128, nb, 64])
            S.dve(lambda e: e.tensor_tensor(out=accv, in0=accv, in1=gv, op=ALU.mult),
                  reads=[("gsb", gb), ("accB", h)], writes=[("accB", h)])

        def stage_d2(n):
            h, a, f, i0, w, diag, kl, qr, SP, SPk, Ap, Apk, pbk, pkey, gb = info(n)
            nb = 4 - i0
            accv = accB[:, i0:4, h, :]
            pv = pbk[:, i0 * 64:256].rearrange("p (j d) -> p j d", j=nb)
            if a == 0:
                S.dve(lambda e: e.tensor_copy(out=accv, in_=pv), reads=[pkey], writes=[("accB", h)])
            else:
                S.dve(lambda e: e.tensor_tensor(out=accv, in0=accv, in1=pv, op=ALU.add),
                      reads=[pkey, ("accB", h)], writes=[("accB", h)])

        for n in range(-3, NTL + 1):
            if 0 <= n - 1 < NTL:
                stage_d2(n - 1)
            if 0 <= n + 3 < NTL:
                stage_a(n + 3)
            if 0 <= n + 1 < NTL:
                stage_b(n + 1)
            if 0 <= n < NTL:
                stage_d1(n)
                stage_c(n)
            yield

    def gen_attnA(c):
        na = 4 * c + 4
        tiles = [(h, a) for h in range(8) for a in range(na)]
        NTL = len(tiles)

        def info(n):
            h, a = tiles[n]
            f = h // 2
            base = 64 * (h % 2)
            i0 = max(0, a - 4 * c)
            w = 512 - i0 * 128
            sbi = 4 + (n % 2)
            ei = (4, 5, 0)[n % 3]
            E, Ek = p16[ei], ("p16", ei)
            EM, EMk = EMt[n % 2], ("p16", 6 + n % 2)
            return h, a, f, base, i0, w, banks[sbi], "B%d" % sbi, E, Ek, EM, EMk

        def stage_a(n):
            h, a, f, base, i0, w, sbk, sbkey, E, Ek, EM, EMk = info(n)
            S.pe(lambda e: e.matmul(out=sbk[:, 0:w], lhsT=kTa[:, f, a * 128:(a + 1) * 128],
                                    rhs=qTa[:, h, i0 * 128:512], start=True, stop=True),
                 reads=[("kTa", f, a // 4), ("qTa", f)], writes=[sbkey])
            far0 = None
            nn = 0
            for jj in range(i0, 4):
                d = 4 * c + jj - a
                blk = (jj - i0) * 128
                if d <= 1:
                    tmp, tk = E32[nn % 2], ("p32", 3 + nn % 2)
                    nn += 1
                    S.dve(lambda e, blk=blk, tmp=tmp, d=d: e.scalar_tensor_tensor(
                        out=tmp[:, 0:128], in0=sbk[:, blk:blk + 128], scalar=0.125, in1=btab[:, h, d, :],
                        op0=ALU.mult, op1=ALU.add),
                        reads=[sbkey, ("btab", d)], writes=[tk])
                    S.act(lambda e, blk=blk, tmp=tmp: e.activation(
                        out=E[:, blk:blk + 128], in_=tmp[:, 0:128], func=AF.Exp, bias=b31[:, h:h + 1], scale=1.0),
                        reads=[tk, "b31"], writes=[Ek])
                else:
                    far0 = blk
                    break
            if far0 is not None:
                S.act(lambda e, far0=far0: e.activation(
                    out=E[:, far0:w], in_=sbk[:, far0:w], func=AF.Exp, bias=b31[:, h:h + 1], scale=0.125),
                    reads=[sbkey, "b31"], writes=[Ek])

        def stage_b(n):
            h, a, f, base, i0, w, sbk, sbkey, E, Ek, EM, EMk = info(n)
            S.dve(lambda e: e.tensor_tensor(out=EM[:, 0:w], in0=E[:, 0:w], in1=maskT[:, a, i0 * 128:512], op=ALU.mult),
                  reads=[Ek] + [("maskT", jj) for jj in range(i0, 4)], writes=[EMk])

        def stage_c(n):
            h, a, f, base, i0, w, sbk, sbkey, E, Ek, EM, EMk = info(n)
            obi = 6 + (h % 2)
            ob, obkey = banks[obi], "B%d" % obi
            for jj in range(i0, 4):
                blk = (jj - i0) * 128
                first = (a == 0 and jj == i0)
                last = (a == 4 * c + jj)
                S.pe(lambda e, jj=jj, blk=blk, first=first, last=last: e.matmul(
                    out=ob[:, jj * 65:(jj + 1) * 65], lhsT=EM[:, blk:blk + 128], rhs=Va[:, a, h, :],
                    start=first, stop=last, skip_group_check=True),
                    reads=[EMk, ("Va", a), "Va_ones"], writes=[obkey])
            if a == na - 1:
                rb = h % 2
                S.dve(lambda e: e.reciprocal(
                    out=rden[:, rb, :], in_=ob[:, 0:260].rearrange("p (j d) -> p j d", j=4)[:, :, 64]),
                    reads=[obkey], writes=[("rden", rb)])
                for jj in range(4):
                    S.dve(lambda e, jj=jj: e.scalar_tensor_tensor(
                        out=mixed[:, jj, h * 64:(h + 1) * 64], in0=ob[:, jj * 65:jj * 65 + 64],
                        scalar=rden[:, rb, jj:jj + 1], in1=sg[:, jj, h * 64:(h + 1) * 64],
                        op0=ALU.mult, op1=ALU.mult),
                        reads=[obkey, ("rden", rb), ("sg", jj, 0)], writes=["mixed"])

        for n in range(-3, NTL):
            if 0 <= n + 1 < NTL:
                stage_b(n + 1)
            if 0 <= n + 3 < NTL:
                stage_a(n + 3)
            if 0 <= n < NTL:
                stage_c(n)
            yield

    def emit_finish(c):
        for jj in range(4):
            S.dve(lambda e, jj=jj: e.tensor_tensor(
                out=mixed[:, jj, 512:1024], in0=accB[:, jj, :, :].rearrange("p h d -> p (h d)"),
                in1=sg[:, jj, 512:1024], op=ALU.mult),
                reads=[("accB", h) for h in range(8)] + [("sg", jj, 1)], writes=["mixed"])
        for jj in range(4):
            for c8 in range(8):
                S.pe(lambda e, jj=jj, c8=c8: e.transpose(out=B0bf[:, c8 * 128:(c8 + 1) * 128],
                                                         in_=mixed[:, jj, c8 * 128:(c8 + 1) * 128], identity=ident[:]),
                     reads=["mixed", "ident"], writes=["B0"])
            evac_copy(hTc[:, :, jj * 128:(jj + 1) * 128], B0bf[:, :].rearrange("p (c t) -> p c t", c=8),
                      ["B0"], [("hTc", c8) for c8 in range(8)])
        wo = wo_pref.pop(c)
        for jj in range(4):
            i = 4 * c + jj
            sl_ = i % 2
            S.dma(lambda e, i=i, sl_=sl_: e.dma_start(out=xt[sl_][:], in_=x[i * 128:(i + 1) * 128, :]),
                  writes=[("xt", sl_)], sem="x%d" % sl_)
            for half in range(2):
                sl, sk = wo[half]
                bk, bkey = proj_bank()
                for c8 in range(8):
                    S.pe(lambda e, c8=c8, bk=bk, sl=sl, jj=jj: e.matmul(
                        out=bk[:], lhsT=hTc[:, c8, jj * 128:(jj + 1) * 128], rhs=sl[:, c8, :],
                        start=(c8 == 0), stop=(c8 == 7)),
                        reads=[sk, ("hTc", c8)], writes=[bkey])
                S.dve(lambda e, bk=bk, sl_=sl_, half=half: e.tensor_tensor(
                    out=xt[sl_][:, half * 512:(half + 1) * 512], in0=bk[:], in1=xt[sl_][:, half * 512:(half + 1) * 512],
                    op=ALU.add),
                    reads=[bkey, ("xt", sl_)], writes=[("xt", sl_)])
            o = S.dma(lambda e, i=i, sl_=sl_: e.dma_start(out=out[i * 128:(i + 1) * 128, :], in_=xt[sl_][:]),
                      reads=[("xt", sl_)], writes=[("out", i)], sem="o%d" % sl_)
            out_ops.append(o)

    def run_interleaved(gens, totals):
        done = [0] * len(gens)
        alive = [True] * len(gens)
        while any(alive):
            best = None
            for gi in range(len(gens)):
                if not alive[gi]:
                    continue
                frac = done[gi] / float(totals[gi])
                if best is None or frac < best[0]:
                    best = (frac, gi)
            gi = best[1]
            try:
                wv = next(gens[gi])
                done[gi] += 1.0 if wv is None else wv
            except StopIteration:
                alive[gi] = False

    def count_steps(genf, c):
        return None

    marks = []

    def mark(name):
        marks.append((name, {e: len(S.ops[e]) for e in S.ENGS}))

    for c in range(NCH):
        mark("norm%d" % c)
        emit_norm(c)
        mark("proj1_%d" % c)
        emit_proj1(c)
        mark("X%d" % c)
        na = 4 * c + 4
        n_b = 8 * na + 4
        npair = 1 if c == 0 else 2
        n_ib = npair * (16 * W_IDX + 2 + niter * W_BIS) + sum(((4 * c + j) // 8 + 1) for j in range(4)) + (2 if c == 0 else 0)

        def chain(c=c):
            yield from gen_proj2(c)
            yield from gen_attnB(c)

        run_interleaved([gen_index_bisect(c), chain()], [n_ib, n_b + 32])
        S.alias(["mixed"], [("qTi", f8) for f8 in range(8)])
        mark("A%d" % c)
        wo_pref[c] = [load_slab([(0, 0, 512)], src=w_out), load_slab([(0, 512, 512)], src=w_out)]
        for _ in gen_attnA(c):
            pass
        mark("fin%d" % c)
        emit_finish(c)
    mark("end")
    nc._marks = marks

    S.final_wait_ops = out_ops
    S.run()
    return nc


_CACHE = {}


def _host_inputs(x_b, norm_gain, w_in, q_norm_gain, k_norm_gain, rel_bias, w_out, consts):
    m = dict(consts)
    m["x"] = np.ascontiguousarray(x_b, dtype=np.float32)
    m["w_in"] = np.ascontiguousarray(w_in[0], dtype=np.float32)
    m["w_out"] = np.ascontiguousarray(w_out[0], dtype=np.float32)
    m["gain"] = np.ascontiguousarray(norm_gain[0].reshape(8, 128).T, dtype=np.float32)
    m["gq"] = np.ascontiguousarray(np.tile(q_norm_gain[0], 2).reshape(128, 1), dtype=np.float32)
    m["gk"] = np.ascontiguousarray(np.tile(k_norm_gain[0], 2).reshape(128, 1), dtype=np.float32)
    m["b31"] = np.ascontiguousarray(np.broadcast_to(rel_bias[31][None, :], (128, 8)), dtype=np.float32)
    m["relb"] = np.ascontiguousarray(np.concatenate([rel_bias, np.ones((1, 8), np.float32)], axis=0),
                                     dtype=np.float32)
    return m


def kernel(x, norm_gain, w_in, q_norm_gain, k_norm_gain, rel_bias, w_out):
    x = np.asarray(x)
    B, L, _ = x.shape
    key = ("nc", L)
    consts = _consts()
    nc = build(L=L)
    in_maps = [_host_inputs(x[b], np.asarray(norm_gain), np.asarray(w_in), np.asarray(q_norm_gain),
                            np.asarray(k_norm_gain), np.asarray(rel_bias), np.asarray(w_out), consts)
               for b in range(B)]
    res = run_bass_kernel_spmd(nc, in_maps, core_ids=list(range(B)))
    return np.stack([np.asarray(r["out"]) for r in res.results], axis=0).astype(np.float32)
```

```python
import math
import numpy as np
import ml_dtypes
import concourse.bass as bass
import concourse.mybir as mybir
from concourse.bass_utils import run_bass_kernel_spmd

F32 = mybir.dt.float32
BF16 = mybir.dt.bfloat16
AF = mybir.ActivationFunctionType
ALU = mybir.AluOpType
AX = mybir.AxisListType

D_MODEL = 1024
D_IN = 5200
IDX_SCALE = (16 * 64) ** -0.5
RMS_EPS = 1e-6
NEG_BIG = -30000.0


class _Op:
    __slots__ = ("eng", "fn", "deps", "tok_sem", "tok_val", "needed", "is_dma", "idx")

    def __init__(self, eng, fn, is_dma, idx):
        self.eng = eng
        self.fn = fn
        self.deps = []
        self.tok_sem = None
        self.tok_val = None
        self.needed = False
        self.is_dma = is_dma
        self.idx = idx


class Sched:
    ENGS = ("pe", "act", "dve", "pool", "sp")

    def __init__(self, nc):
        self.nc = nc
        self.ops = {e: [] for e in self.ENGS}
        self.last_w = {}
        self.readers = {}
        self.final_wait_ops = []

    def alias(self, dst_keys, src_keys):
        acc = []
        for k in src_keys:
            lw = self.last_w.get(k)
            if lw is not None:
                acc.append(lw)
            acc.extend(self.readers.get(k, ()))
        for k in dst_keys:
            self.readers.setdefault(k, [])
            self.readers[k] = list(self.readers[k]) + acc

    def _add(self, eng, fn, reads, writes, is_dma=False, dma_sem=None):
        op = _Op(eng, fn, is_dma, len(self.ops[eng]))
        excl = [k for k in reads if isinstance(k, str) and len(k) == 2 and k[0] == "B" and k[1].isdigit()]
        if excl:
            reads = [k for k in reads if k not in excl]
            writes = list(writes) + excl
        cand = []
        for k in reads:
            lw = self.last_w.get(k)
            if lw is not None:
                cand.append((lw, True))
        for k in writes:
            lw = self.last_w.get(k)
            if lw is not None:
                cand.append((lw, False))
            for r in self.readers.get(k, ()):
                cand.append((r, False))
        best = {}
        for d, raw in cand:
            if d is op:
                continue
            if (not d.is_dma) and (not is_dma) and d.eng == eng:
                if (not raw) or eng == "pe":
                    continue
            if d.is_dma:
                key = ("dma", d.tok_sem)
            else:
                key = ("eng", d.eng)
            cur = best.get(key)
            if cur is None or d.idx > cur.idx:
                best[key] = d
        op.deps = list(best.values())
        for k in writes:
            self.last_w[k] = op
            self.readers[k] = []
        for k in reads:
            self.readers.setdefault(k, []).append(op)
        if is_dma:
            op.tok_sem = dma_sem
        self.ops[eng].append(op)
        return op

    def pe(self, fn, reads=(), writes=()):
        return self._add("pe", fn, reads, writes)

    def act(self, fn, reads=(), writes=()):
        return self._add("act", fn, reads, writes)

    def dve(self, fn, reads=(), writes=()):
        return self._add("dve", fn, reads, writes)

    def pool(self, fn, reads=(), writes=()):
        return self._add("pool", fn, reads, writes)

    def dma(self, fn, reads=(), writes=(), sem="d0", queue="sp"):
        return self._add(queue, fn, reads, writes, is_dma=True, dma_sem=sem)

    def finalize(self):
        for e in self.ENGS:
            for op in self.ops[e]:
                for d in op.deps:
                    d.needed = True
        for op in self.final_wait_ops:
            op.needed = True
        cnt = {e: 0 for e in self.ENGS}
        dcnt = {}
        for e in self.ENGS:
            for op in self.ops[e]:
                if op.is_dma:
                    dcnt[op.tok_sem] = dcnt.get(op.tok_sem, 0) + 16
                    op.tok_val = dcnt[op.tok_sem]
                elif op.needed:
                    cnt[e] += 1
                    op.tok_sem = "E_" + e
                    op.tok_val = cnt[e]
        names = set(dcnt.keys()) | {"E_" + e for e in self.ENGS if cnt[e] > 0}
        return sorted(names)

    def emit(self, engine_obj, eng, sems):
        waited = {}
        for op in self.ops[eng]:
            for d in op.deps:
                s, v = d.tok_sem, d.tok_val
                if waited.get(s, 0) >= v:
                    continue
                engine_obj.wait_ge(sems[s], v)
                waited[s] = v
            ins = op.fn(engine_obj)
            if op.is_dma:
                ins.then_inc(sems[op.tok_sem], 16)
            elif op.needed:
                ins.then_inc(sems[op.tok_sem], 1)
        if eng == "sp":
            for op in self.final_wait_ops:
                s, v = op.tok_sem, op.tok_val
                if waited.get(s, 0) >= v:
                    continue
                engine_obj.wait_ge(sems[s], v)
                waited[s] = v

    def run(self):
        nc = self.nc
        names = self.finalize()
        sems = {n: nc.alloc_semaphore("s_" + n) for n in names}
        sch = self
        with nc.Block() as block:
            @block.sync
            def _(e):
                sch.emit(e, "sp", sems)

            @block.scalar
            def _(e):
                sch.emit(e, "act", sems)

            @block.vector
            def _(e):
                sch.emit(e, "dve", sems)

            @block.tensor
            def _(e):
                sch.emit(e, "pe", sems)

            @block.gpsimd
            def _(e):
                sch.emit(e, "pool", sems)


def _t5_bucket_np(d):
    d = np.maximum(d, 0).astype(np.int64)
    d_f = np.maximum(d, 1).astype(np.float32)
    large = 16 + (np.log(d_f / np.float32(16)) / np.float32(math.log(128 / 16))
                  * np.float32(16)).astype(np.int32)
    large = np.minimum(large, 31)
    return np.where(d < 16, d, large)


def _consts():
    bf = ml_dtypes.bfloat16
    c = {}
    c["ident"] = np.eye(128, dtype=np.float32).astype(bf)
    j = np.arange(128)[:, None]
    s = np.arange(128)[None, :]
    c["negtri"] = np.where(j >= s, -1.0, 0.0).astype(np.float32).astype(bf)
    c["strict"] = np.where(j < s, 1.0, 0.0).astype(np.float32).astype(bf)
    c["blockones"] = np.where((j // 64) == (s // 64), 1.0 / 64, 0.0).astype(np.float32).astype(bf)
    c["onescol"] = np.ones((128, 1), np.float32).astype(bf)
    oh = np.zeros((33, 384), np.float32)
    for i in range(384):
        d = i - 128
        if d < 0:
            oh[32, i] = NEG_BIG
        else:
            b = int(_t5_bucket_np(np.array([d]))[0])
            oh[b, i] += 1.0
            oh[31, i] -= 1.0
    c["onehot"] = oh
    return c


def build(L=2048, niter=22, dbg=False):
    NT = L // 128
    NCH = L // 512
    TOPK = min(256, L // 4)
    nc = bass.Bass("TRN2", target_bir_lowering=False, dynamic_dma_scratch_size=8192)

    def din(name, shape, dt=F32):
        return nc.dram_tensor(name, list(shape), dt, kind="ExternalInput").ap()

    x = din("x", [L, D_MODEL])
    w_in = din("w_in", [D_MODEL, D_IN])
    w_out = din("w_out", [D_MODEL, D_MODEL])
    gain_d = din("gain", [128, 8])
    gq_d = din("gq", [128, 1])
    gk_d = din("gk", [128, 1])
    b31_d = din("b31", [128, 8])
    relb_d = din("relb", [33, 8])
    onehot_d = din("onehot", [33, 384])
    ident_d = din("ident", [128, 128], BF16)
    negtri_d = din("negtri", [128, 128], BF16)
    strict_d = din("strict", [128, 128], BF16)
    blockones_d = din("blockones", [128, 128], BF16)
    onescol_d = din("onescol", [128, 1], BF16)
    out = nc.dram_tensor("out", [L, D_MODEL], F32, kind="ExternalOutput").ap()
    scr1_t = nc.dram_tensor("scr1", [8, 384], F32, kind="Internal")
    scr2_t = nc.dram_tensor("scr2", [128, 8 * 384], F32, kind="Internal")

    def sb(name, shape, dt):
        return nc.alloc_sbuf_tensor(name, list(shape), dt)

    gain = sb("gain_s", [128, 8], F32)
    gq = sb("gq_s", [128, 1], F32)
    gk = sb("gk_s", [128, 1], F32)
    b31 = sb("b31_s", [128, 8], F32)
    relb = sb("relb_s", [33, 8], F32)
    onehot = sb("onehot_s", [33, 384], F32)
    ident = sb("ident_s", [128, 128], BF16)
    negtri = sb("negtri_s", [128, 128], BF16)
    strict = sb("strict_s", [128, 128], BF16)
    blockones = sb("blockones_s", [128, 128], BF16)
    onescol = sb("onescol_s", [128, 1], BF16)
    fb = sb("fb_s", [8, 384], F32)
    btab = sb("btab", [128, 8, 2, 128], F32)
    kTa = sb("kTa", [128, 4, L], BF16)
    kTb = sb("kTb", [128, 4, L], BF16)
    kTi = sb("kTi", [128, L], BF16)
    Va = sb("Va", [128, NT, 8, 65], BF16)
    Vb = sb("Vb", [128, NT, 8, 64], BF16)
    hTc = sb("hTc", [128, 8, 512], BF16)
    qTa = sb("qTa", [128, 4, 512], BF16)
    qTb = sb("qTb", [128, 4, 512], BF16)
    qTi_raw = sb("qTi", [128, 4096], BF16)
    qTi = qTi_raw[:].rearrange("p (c t) -> p c t", c=8)
    mixed = qTi_raw[:].rearrange("p (j f) -> p j f", j=4)
    sg = sb("sg", [128, 4, 1024], BF16)
    ws = sb("ws", [128, 4, 16], F32)
    maskT = sb("maskT", [128, NT, 512], BF16)
    slab = [sb("slab0", [128, 8, 512], BF16), sb("slab1", [128, 8, 512], BF16)]
    score = [sb("score0", [128, L], F32), sb("score1", [128, L], F32)]
    accB_t = sb("accB", [128, 2048], F32)
    accB = accB_t[:].rearrange("p (j h d) -> p j h d", j=4, h=8)
    maskrow = [sb("maskrow0", [128, L], BF16), sb("maskrow1", [128, L], BF16)]
    xt = [sb("xt0", [128, 1024], F32), sb("xt1", [128, 1024], F32)]
    xs = [sb("xs0", [128, 1024], BF16), sb("xs1", [128, 1024], BF16)]
    junk = sb("junk", [128, 1024], BF16)
    NP32 = 6
    NP16 = 8
    p32 = [sb("p32_%d" % i, [128, 512], F32) for i in range(NP32)]
    p16 = [sb("p16_%d" % i, [128, 512], BF16) for i in range(NP16)]
    small = sb("small", [128, 64], F32)
    bis = sb("bis", [128, 2, 8 + 32], F32)
    halves = sb("halves", [128, 32], F32)
    rden = sb("rden", [128, 2, 4], F32)
    gsb = sb("gsb", [128, 2, 4], F32)

    banks = [nc.alloc_psum_tensor("B%d" % i, [128, 512], F32) for i in range(8)]
    B0bf = banks[0][:].bitcast(BF16)

    S = Sched(nc)
    cnt32 = [0]
    cnt16 = [0]

    def get32():
        i = 3 + (cnt32[0] % 3)
        cnt32[0] += 1
        return p32[i], ("p32", i)

    def get16():
        i = cnt16[0] % 4
        cnt16[0] += 1
        return p16[i], ("p16", i)

    def cload(dst, src, key):
        S.dma(lambda e: e.dma_start(out=dst[:], in_=src), writes=[key], sem="c_" + key)

    cload(gain, gain_d, "gain")
    cload(gq, gq_d, "gq")
    cload(gk, gk_d, "gk")
    cload(b31, b31_d, "b31")
    cload(relb, relb_d, "relb")
    cload(onehot, onehot_d, "onehot")
    cload(ident, ident_d, "ident")
    cload(negtri, negtri_d, "negtri")
    cload(strict, strict_d, "strict")
    cload(blockones, blockones_d, "blockones")
    cload(onescol, onescol_d, "onescol")
    for k in range(niter + 1):
        S.dve(lambda e, k=k: e.memset(halves[:, k:k + 1], 2.0 ** -(k + 1)), writes=[("halves", k)])
    S.pool(lambda e: e.memset(Va[:, :, :, 64:65], 1.0), writes=["Va_ones"])

    S.pe(lambda e: e.matmul(out=banks[3][0:8, 0:384], lhsT=relb[:, :], rhs=onehot[:, :], start=True, stop=True),
         reads=["relb", "onehot"], writes=["B3"])
    S.act(lambda e: e.activation(out=fb[:], in_=banks[3][0:8, 0:384], func=AF.Copy), reads=["B3"], writes=["fb"])
    S.dma(lambda e: e.dma_start(out=scr1_t.ap(), in_=fb[:]), reads=["fb"], writes=["scr1"], sem="c_scr1")
    scr2_v = scr2_t.ap().rearrange("p (h i) -> p h i", h=8)
    S.dma(lambda e: e.dma_start(out=scr2_v, in_=scr1_t.ap().partition_broadcast(128)),
          reads=["scr1"], writes=["scr2"], sem="c_scr2")
    for r in range(2):
        src = bass.AP(tensor=scr2_t, offset=128 * (r + 1), ap=[[8 * 384 - 1, 128], [384, 8], [1, 128]])
        S.dma(lambda e, r=r, src=src: e.dma_start(out=btab[:, :, r, :], in_=src),
              reads=["scr2"], writes=[("btab", r)], sem="c_btab%d" % r)

    slab_ctr = [0]

    def load_slab(pieces, src=None):
        if src is None:
            src = w_in
        i = slab_ctr[0] % 2
        slab_ctr[0] += 1
        key = ("slab", i)
        for (d0, s0, n) in pieces:
            S.dma(lambda e, i=i, d0=d0, s0=s0, n=n, src=src: e.dma_start(
                out=slab[i][:, :, d0:d0 + n],
                in_=src[:, s0:s0 + n].rearrange("(c p) n -> p c n", p=128)),
                writes=[key], sem="w%d" % i, queue="pool")
        return slab[i], key

    pbank_ctr = [0]

    def proj_bank():
        i = 1 + (pbank_ctr[0] % 2)
        pbank_ctr[0] += 1
        return banks[i], "B%d" % i

    def fm_matmuls(sl, skey, f, ncols_feat=128, col0=None, bank=None):
        bk, bkey = proj_bank() if bank is None else bank
        c0 = f * 128 if col0 is None else col0
        for c8 in range(8):
            S.pe(lambda e, c8=c8, bk=bk, sl=sl, c0=c0: e.matmul(
                out=bk[0:ncols_feat, :], lhsT=sl[:, c8, c0:c0 + ncols_feat], rhs=hTc[:, c8, :],
                start=(c8 == 0), stop=(c8 == 7)),
                reads=[skey, ("hTc", c8)], writes=[bkey])
        return bk, bkey

    def tm_matmuls(sl, skey, j, ncols=512, col0=0, bank=None):
        bk, bkey = proj_bank() if bank is None else bank
        for c8 in range(8):
            S.pe(lambda e, c8=c8, bk=bk, sl=sl: e.matmul(
                out=bk[:, 0:ncols], lhsT=hTc[:, c8, j * 128:(j + 1) * 128], rhs=sl[:, c8, col0:col0 + ncols],
                start=(c8 == 0), stop=(c8 == 7)),
                reads=[skey, ("hTc", c8)], writes=[bkey])
        return bk, bkey

    def fm_g(sl, skey, f, bank):
        bk, bkey = bank
        c0 = f * 128
        for c8 in range(8):
            S.pe(lambda e, c8=c8, bk=bk, sl=sl, c0=c0: e.matmul(
                out=bk[:, :], lhsT=sl[:, c8, c0:c0 + 128], rhs=hTc[:, c8, :],
                start=(c8 == 0), stop=(c8 == 7)),
                reads=[skey, ("hTc", c8)], writes=[bkey])
            if c8 % 2 == 1 and c8 < 7:
                yield 0.25
        return bk, bkey

    def tm_g(sl, skey, j, bank):
        bk, bkey = bank
        for c8 in range(8):
            S.pe(lambda e, c8=c8, bk=bk, sl=sl: e.matmul(
                out=bk[:, 0:512], lhsT=hTc[:, c8, j * 128:(j + 1) * 128], rhs=sl[:, c8, 0:512],
                start=(c8 == 0), stop=(c8 == 7)),
                reads=[skey, ("hTc", c8)], writes=[bkey])
            if c8 % 2 == 1 and c8 < 7:
                yield 0.25
        return bk, bkey

    def qknorm(bk, bkey, gvec, gkey, dst_ap, dst_key):
        sq, sqk = get16()
        S.act(lambda e: e.activation(out=sq[:], in_=bk[:], func=AF.Square), reads=[bkey], writes=[sqk])
        S.pe(lambda e: e.matmul(out=banks[3][:], lhsT=blockones[:], rhs=sq[:], start=True, stop=True),
             reads=[sqk, "blockones"], writes=["B3"])
        lt, ltk = get32()
        S.act(lambda e: e.activation(out=lt[:], in_=banks[3][:], func=AF.Ln, bias=RMS_EPS, scale=1.0),
              reads=["B3"], writes=[ltk])
        S.act(lambda e: e.activation(out=lt[:], in_=lt[:], func=AF.Exp, scale=-0.5), reads=[ltk], writes=[ltk])
        S.dve(lambda e: e.scalar_tensor_tensor(out=dst_ap, in0=bk[:], scalar=gvec[:, 0:1], in1=lt[:],
                                               op0=ALU.mult, op1=ALU.mult),
              reads=[bkey, ltk, gkey], writes=[dst_key])

    evac_ctr = [0]

    def evac_copy(dst_ap, src_ap, reads, writes, scale=None):
        i = evac_ctr[0]
        evac_ctr[0] += 1
        if i % 2 == 0:
            if scale is None:
                S.act(lambda e: e.activation(out=dst_ap, in_=src_ap, func=AF.Copy), reads=reads, writes=writes)
            else:
                S.act(lambda e: e.activation(out=dst_ap, in_=src_ap, func=AF.Copy, scale=scale),
                      reads=reads, writes=writes)
        else:
            if scale is None:
                S.dve(lambda e: e.tensor_copy(out=dst_ap, in_=src_ap), reads=reads, writes=writes)
            else:
                S.dve(lambda e: e.tensor_scalar(out=dst_ap, in0=src_ap, scalar1=scale, scalar2=None, op0=ALU.mult),
                      reads=reads, writes=writes)

    out_ops = []
    wo_pref = {}

    R32 = p32[0:3]
    E32 = p32[3:5]
    SPt = p16[0:2]
    APt = p16[2:4]
    Et = p16[4:6]
    EMt = p16[6:8]

    def emit_norm(c):
        for j in range(4):
            i = 4 * c + j
            sl_ = i % 2
            S.dma(lambda e, i=i, sl_=sl_: e.dma_start(out=xt[sl_][:], in_=x[i * 128:(i + 1) * 128, :]),
                  writes=[("xt", sl_)], sem="x%d" % sl_)
            S.act(lambda e, sl_=sl_: e.activation(out=junk[:], in_=xt[sl_][:], func=AF.Square,
                                                  accum_out=small[:, sl_:sl_ + 1]),
                  reads=[("xt", sl_)], writes=[("ss", sl_), "junk"])
            S.act(lambda e, sl_=sl_: e.activation(out=small[:, 2 + sl_:3 + sl_], in_=small[:, sl_:sl_ + 1],
                                                  func=AF.Ln, scale=1.0 / D_MODEL, bias=RMS_EPS),
                  reads=[("ss", sl_)], writes=[("lnv", sl_)])
            S.act(lambda e, sl_=sl_: e.activation(out=small[:, 4 + sl_:5 + sl_], in_=small[:, 2 + sl_:3 + sl_],
                                                  func=AF.Exp, scale=-0.5),
                  reads=[("lnv", sl_)], writes=[("rstd", sl_)])
            S.dve(lambda e, sl_=sl_: e.tensor_scalar(out=xs[sl_][:], in0=xt[sl_][:], scalar1=small[:, 4 + sl_:5 + sl_],
                                                     scalar2=None, op0=ALU.mult),
                  reads=[("xt", sl_), ("rstd", sl_)], writes=[("xs", sl_)])
            for c8 in range(8):
                S.pe(lambda e, c8=c8, sl_=sl_: e.transpose(out=B0bf[:, c8 * 128:(c8 + 1) * 128],
                                                           in_=xs[sl_][:, c8 * 128:(c8 + 1) * 128], identity=ident[:]),
                     reads=[("xs", sl_), "ident"], writes=["B0"])
            for c8 in range(8):
                dst = hTc[:, c8, j * 128:(j + 1) * 128]
                srcp = B0bf[:, c8 * 128:(c8 + 1) * 128]
                if j % 2 == 0:
                    S.dve(lambda e, dst=dst, srcp=srcp, c8=c8: e.tensor_scalar(
                        out=dst, in0=srcp, scalar1=gain[:, c8:c8 + 1], scalar2=None, op0=ALU.mult),
                        reads=["B0", "gain"], writes=[("hTc", c8)])
                else:
                    S.act(lambda e, dst=dst, srcp=srcp, c8=c8: e.activation(
                        out=dst, in_=srcp, func=AF.Copy, scale=gain[:, c8:c8 + 1]),
                        reads=["B0", "gain"], writes=[("hTc", c8)])

    def emit_proj1(c):
        tok0 = c * 512
        sl, sk = load_slab([(0, 3072, 64), (64, 3072, 64), (128, 3136, 16)])
        bk, bkey = fm_matmuls(sl, sk, 0)
        evac_copy(kTi[:, tok0:tok0 + 512], bk[:], [bkey], [("kTi", c)])
        for j in range(4):
            bk, bkey = tm_matmuls(sl, sk, j, ncols=16, col0=128)
            S.dve(lambda e, bk=bk, j=j: e.tensor_scalar(out=ws[:, j, :], in0=bk[:, 0:16], scalar1=IDX_SCALE,
                                                        scalar2=None, op0=ALU.mult),
                  reads=[bkey], writes=[("ws", j)])
        S.alias([("qTi", f8) for f8 in range(8)], ["mixed"])
        for half in range(2):
            sl, sk = load_slab([(0, 2048 + 512 * half, 512)])
            for f in range(4):
                bk, bkey = fm_matmuls(sl, sk, f)
                evac_copy(qTi[:, 4 * half + f, :], bk[:], [bkey], [("qTi", 4 * half + f)])

    def gen_proj2(c):
        tok0 = c * 512
        pb2 = [0]

        def bank2():
            i = 6 + (pb2[0] % 2)
            pb2[0] += 1
            return banks[i], "B%d" % i

        sl, sk = load_slab([(0, 512, 512)])
        for f in range(4):
            bk, bkey = yield from fm_g(sl, sk, f, bank2())
            qknorm(bk, bkey, gk, "gk", kTa[:, f, tok0:tok0 + 512], ("kTa", f, c))
            yield 0.25
        sl, sk = load_slab([(0, 1024, 512)])
        for j in range(4):
            i = 4 * c + j
            bk, bkey = yield from tm_g(sl, sk, j, bank2())
            evac_copy(Va[:, i, :, 0:64], bk[:].rearrange("p (h d) -> p h d", h=8), [bkey, "Va_ones"], [("Va", i)])
            yield 0.25
        sl, sk = load_slab([(0, 3664, 512)])
        for f in range(4):
            bk, bkey = yield from fm_g(sl, sk, f, bank2())
            evac_copy(kTb[:, f, tok0:tok0 + 512], bk[:], [bkey], [("kTb", f, c)], scale=0.125)
            yield 0.25
        sl, sk = load_slab([(0, 4176, 512)])
        for j in range(4):
            i = 4 * c + j
            bk, bkey = yield from tm_g(sl, sk, j, bank2())
            evac_copy(Vb[:, i, :, :], bk[:].rearrange("p (h d) -> p h d", h=8), [bkey], [("Vb", i)])
            yield 0.25
        sl, sk = load_slab([(0, 0, 512)])
        for f in range(4):
            bk, bkey = yield from fm_g(sl, sk, f, bank2())
            qknorm(bk, bkey, gq, "gq", qTa[:, f, :], ("qTa", f))
            yield 0.25
        sl, sk = load_slab([(0, 3152, 512)])
        for f in range(4):
            bk, bkey = yield from fm_g(sl, sk, f, bank2())
            evac_copy(qTb[:, f, :], bk[:], [bkey], [("qTb", f)])
            yield 0.25
        for half, col in ((0, 1536), (1, 4688)):
            sl, sk = load_slab([(0, col, 512)])
            for j in range(4):
                bk, bkey = yield from tm_g(sl, sk, j, bank2())
                S.act(lambda e, bk=bk, j=j, half=half: e.activation(
                    out=sg[:, j, half * 512:(half + 1) * 512], in_=bk[:], func=AF.Silu),
                    reads=[bkey], writes=[("sg", j, half)])
                yield 0.25

    W_IDX = 1.0
    W_BIS = 2.2

    def gen_index_bisect(c):
        dctr = [0]
        rctr = [0]

        def indexer_tiles(js):
          for hh in range(16):
            f8 = hh // 2
            base = 64 * (hh % 2)
            for scn in range(c + 1):
              for j in js:
                sc_t = score[j % 2]
                sckey = ("score", j % 2)
                if True:
                    w = 512 if scn < c else (j + 1) * 128
                    bi = (1, 2, 0)[dctr[0] % 3]
                    dctr[0] += 1
                    bk = banks[bi]
                    bkey = "B%d" % bi
                    S.pe(lambda e, bk=bk, f8=f8, base=base, scn=scn, w=w, j=j: e.matmul(
                        out=bk[:, 0:w], lhsT=qTi[base:base + 64, f8, j * 128:(j + 1) * 128],
                        rhs=kTi[base:base + 64, scn * 512:scn * 512 + w], start=True, stop=True),
                        reads=[("qTi", f8), ("kTi", scn)], writes=[bkey])
                    ri = rctr[0] % 3
                    rctr[0] += 1
                    r, rk = R32[ri], ("p32", ri)
                    S.act(lambda e, bk=bk, r=r, w=w: e.activation(out=r[:, 0:w], in_=bk[:, 0:w], func=AF.Relu),
                          reads=[bkey], writes=[rk])
                    dst = sc_t[:, scn * 512:scn * 512 + w]
                    if hh == 0:
                        S.dve(lambda e, dst=dst, r=r, w=w, j=j, hh=hh: e.tensor_scalar(
                            out=dst, in0=r[:, 0:w], scalar1=ws[:, j, hh:hh + 1], scalar2=None, op0=ALU.mult),
                            reads=[rk, ("ws", j)], writes=[sckey])
                    else:
                        S.dve(lambda e, dst=dst, r=r, w=w, j=j, hh=hh: e.scalar_tensor_tensor(
                            out=dst, in0=r[:, 0:w], scalar=ws[:, j, hh:hh + 1], in1=dst,
                            op0=ALU.mult, op1=ALU.add),
                            reads=[rk, ("ws", j), sckey], writes=[sckey])
            yield W_IDX

        def bisect_tiles(js):
            bs = [j % 2 for j in js]
            b0, b1 = min(bs), max(bs) + 1
            bkeys = [("bis", b) for b in bs]
            for j in js:
                i = 4 * c + j
                Si = 128 * (i + 1)
                b = j % 2
                sc_t = score[b]
                sckey = ("score", b)
                bk_ = ("bis", b)
                S.dve(lambda e, sc_t=sc_t, Si=Si, b=b: e.tensor_reduce(out=bis[:, b, 0:1], in_=sc_t[:, 0:Si],
                                                                       axis=AX.X, op=ALU.min),
                      reads=[sckey], writes=[bk_])
                S.pool(lambda e, sc_t=sc_t, Si=Si: e.affine_select(
                    out=sc_t[:, Si - 128:Si], in_=sc_t[:, Si - 128:Si], pattern=[[-1, 128]],
                    compare_op=ALU.is_ge, fill=-3.0e38, base=0, channel_multiplier=1),
                    reads=[sckey, bk_], writes=[sckey])
                S.dve(lambda e, sc_t=sc_t, Si=Si, b=b: e.tensor_reduce(out=bis[:, b, 1:2], in_=sc_t[:, 0:Si],
                                                                       axis=AX.X, op=ALU.max),
                      reads=[sckey, bk_], writes=[bk_])
                yield
            if len(bs) == 2:
                S.dve(lambda e: e.tensor_tensor(out=bis[:, 0, 0:1], in0=bis[:, 0, 0:1], in1=bis[:, 1, 0:1], op=ALU.min),
                      reads=bkeys, writes=[("bis", 0)])
                S.dve(lambda e: e.tensor_tensor(out=bis[:, 0, 1:2], in0=bis[:, 0, 1:2], in1=bis[:, 1, 1:2], op=ALU.max),
                      reads=bkeys, writes=[("bis", 0)])
            S.dve(lambda e: e.tensor_tensor(out=bis[:, b0, 2:3], in0=bis[:, b0, 1:2], in1=bis[:, b0, 0:1],
                                            op=ALU.subtract), reads=bkeys, writes=[("bis", b0)])
            S.dve(lambda e: e.tensor_scalar(out=bis[:, b0, 8:8 + niter + 1], in0=halves[:, 0:niter + 1],
                                            scalar1=bis[:, b0, 2:3], scalar2=None, op0=ALU.mult),
                  reads=bkeys + [("halves", k) for k in range(niter + 1)], writes=[("bis", b0)])
            for b in bs:
                S.dve(lambda e, b=b: e.tensor_tensor(out=bis[:, b, 3:4], in0=bis[:, b0, 0:1], in1=bis[:, b0, 8:9],
                                                     op=ALU.add), reads=bkeys, writes=[("bis", b)])
            yield
            for k in range(niter):
                for j in js:
                    i = 4 * c + j
                    Si = 128 * (i + 1)
                    b = j % 2
                    S.dve(lambda e, sc_t=score[b], Si=Si, b=b, jk=R32[b][:].bitcast(mybir.dt.uint8): e.tensor_scalar(
                        out=jk[:, 0:Si], in0=sc_t[:, 0:Si], scalar1=bis[:, b, 3:4], scalar2=None,
                        op0=ALU.is_ge, op1=ALU.add, accum_out=bis[:, b, 4:5]),
                        reads=[("score", b), ("bis", b)], writes=[("p32", b), ("bis", b)])
                S.dve(lambda e, k=k: e.tensor_scalar(
                    out=bis[:, b0:b1, 5], in0=bis[:, b0:b1, 4], scalar1=float(TOPK) - 0.5,
                    scalar2=bis[:, b0, 8 + k:9 + k], op0=(ALU.is_ge if k < niter - 1 else ALU.is_lt),
                    op1=ALU.mult),
                    reads=bkeys, writes=bkeys)
                if k < niter - 1:
                    S.dve(lambda e, k=k: e.scalar_tensor_tensor(
                        out=bis[:, b0:b1, 3], in0=bis[:, b0:b1, 3], scalar=bis[:, b0, 9 + k:10 + k],
                        in1=bis[:, b0:b1, 5], op0=ALU.subtract, op1=ALU.add),
                        reads=bkeys, writes=bkeys)
                else:
                    S.dve(lambda e: e.tensor_tensor(out=bis[:, b0:b1, 6], in0=bis[:, b0:b1, 3],
                                                    in1=bis[:, b0:b1, 5], op=ALU.subtract),
                          reads=bkeys, writes=bkeys)
                yield W_BIS
            for j in js:
                i = 4 * c + j
                Si = 128 * (i + 1)
                b = j % 2
                sc_t = score[b]
                sckey = ("score", b)
                bk_ = ("bis", b)
                mr = maskrow[b]
                mrk = ("maskrow", b)
                S.dve(lambda e, sc_t=sc_t, Si=Si, b=b, mr=mr: e.tensor_scalar(
                    out=mr[:, 0:Si], in0=sc_t[:, 0:Si], scalar1=bis[:, b, 6:7], scalar2=None, op0=ALU.is_ge),
                    reads=[sckey, bk_], writes=[mrk])
                a0 = 0
                while a0 <= i:
                    n = min(8, i + 1 - a0)
                    for q in range(n):
                        a = a0 + q
                        S.pe(lambda e, mr=mr, a=a, q=q: e.transpose(out=B0bf[:, q * 128:(q + 1) * 128],
                                                                    in_=mr[:, a * 128:(a + 1) * 128], identity=ident[:]),
                             reads=[mrk, "ident"], writes=["B0"])
                    dst = maskT[:, a0:a0 + n, j * 128:(j + 1) * 128]
                    srcp = B0bf[:, 0:n * 128].rearrange("p (n t) -> p n t", n=n)
                    S.act(lambda e, dst=dst, srcp=srcp: e.activation(out=dst, in_=srcp, func=AF.Copy),
                          reads=["B0"], writes=[("maskT", j)])
                    a0 += n
                    yield

        for jp in range(2):
            js = []
            for j in (2 * jp, 2 * jp + 1):
                i = 4 * c + j
                if 128 * (i + 1) <= TOPK:
                    S.pool(lambda e, i=i, j=j: e.memset(maskT[:, 0:i + 1, j * 128:(j + 1) * 128], 1.0),
                           writes=[("maskT", j)])
                    yield
                else:
                    js.append(j)
            if js:
                yield from indexer_tiles(js)
                yield from bisect_tiles(js)

    def gen_attnB(c):
        na = 4 * c + 4
        tiles = [(h, a) for h in range(8) for a in range(na)]
        NTL = len(tiles)
        zb, zkey = banks[4], "B4"
        lb, lkey = banks[5], "B5"
        st = {}

        def info(n):
            h, a = tiles[n]
            f = h // 2
            base = 64 * (h % 2)
            i0 = max(0, a - 4 * c)
            w = 512 - i0 * 128
            diag = a >= 4 * c
            kl = kTb[base:base + 64, f, a * 128:(a + 1) * 128]
            qr = qTb[base:base + 64, f, i0 * 128:512]
            spi = (0, 1, 4)[n % 3]
            SP, SPk = p16[spi], ("p16", spi)
            Ap, Apk = APt[n % 2], ("p16", 2 + n % 2)
            pbi = 6 + (n % 2)
            return h, a, f, i0, w, diag, kl, qr, SP, SPk, Ap, Apk, banks[pbi], "B%d" % pbi, n % 2

        def stage_a(n):
            h, a, f, i0, w, diag, kl, qr, SP, SPk, Ap, Apk, pbk, pkey, gb = info(n)
            S.pe(lambda e: e.matmul(out=zb[:, 0:w], lhsT=kl, rhs=qr, start=True, stop=True),
                 reads=[("kTb", f, a // 4), ("qTb", f)], writes=[zkey])
            e1, e1k = E32[n % 2], ("p32", 3 + n % 2)
            S.act(lambda e: e.activation(out=e1[:, 0:w], in_=zb[:, 0:w], func=AF.Exp), reads=[zkey], writes=[e1k])
            S.act(lambda e: e.activation(out=SP[:, 0:w], in_=e1[:, 0:w], func=AF.Ln, bias=1.0, scale=1.0),
                  reads=[e1k], writes=[SPk])
            if diag:
                S.pool(lambda e: e.tensor_tensor(out=SP[:, 0:128], in0=SP[:, 0:128], in1=strict[:], op=ALU.mult),
                       reads=[SPk, "strict"], writes=[SPk])

        def stage_b(n):
            h, a, f, i0, w, diag, kl, qr, SP, SPk, Ap, Apk, pbk, pkey, gb = info(n)
            S.pe(lambda e: e.matmul(out=lb[:, 0:w], lhsT=kl, rhs=qr, start=True, stop=False),
                 reads=[("kTb", f, a // 4), ("qTb", f)], writes=[lkey])
            S.pe(lambda e: e.matmul(out=lb[:, 0:w], lhsT=negtri[:], rhs=SP[:, 0:w], start=False, stop=True),
                 reads=[SPk, "negtri"], writes=[lkey])
            if a > 0:
                fcs = True
                for jj in range(i0, 4):
                    blk = (jj - i0) * 128
                    S.pe(lambda e, blk=blk, jj=jj, fcs=fcs: e.matmul(
                        out=banks[3][:, jj:jj + 1], lhsT=SP[:, blk:blk + 128], rhs=onescol[:, 0:1],
                        start=fcs, stop=(jj == 3), skip_group_check=True),
                        reads=[SPk, "onescol"], writes=["B3"])
                    fcs = False
                S.act(lambda e: e.activation(out=gsb[:, gb, i0:4], in_=banks[3][:, i0:4], func=AF.Exp, scale=-1.0),
                      reads=["B3"], writes=[("gsb", gb)])
            S.act(lambda e: e.activation(out=Ap[:, 0:w], in_=lb[:, 0:w], func=AF.Exp), reads=[lkey], writes=[Apk])
            if diag:
                S.pool(lambda e: e.tensor_tensor(out=Ap[:, 0:128], in0=Ap[:, 0:128], in1=strict[:], op=ALU.mult),
                       reads=[Apk, "strict"], writes=[Apk])

        def stage_c(n):
            h, a, f, i0, w, diag, kl, qr, SP, SPk, Ap, Apk, pbk, pkey, gb = info(n)
            fp = True
            for jj in range(i0, 4):
                blk = (jj - i0) * 128
                S.pe(lambda e, jj=jj, blk=blk, fp=fp: e.matmul(
                    out=pbk[:, jj * 64:(jj + 1) * 64], lhsT=Ap[:, blk:blk + 128], rhs=Vb[:, a, h, :],
                    start=fp, stop=(jj == 3), skip_group_check=True),
                    reads=[Apk, ("Vb", a)], writes=[pkey])
                fp = False

        def stage_d1(n):
            h, a, f, i0, w, diag, kl, qr, SP, SPk, Ap, Apk, pbk, pkey, gb = info(n)
            if a == 0:
                return
            nb = 4 - i0
            accv = accB[:, i0:4, h, :]
            gv = gsb[:, gb, i0:4].unsqueeze(2).broadcast_to([128, nb, 64])
            S.dve(lambda e: e.tensor_tensor(out=accv, in0=accv, in1=gv, op=ALU.mult),
                  reads=[("gsb", gb), ("accB", h)], writes=[("accB", h)])

        def stage_d2(n):
            h, a, f, i0, w, diag, kl, qr, SP, SPk, Ap, Apk, pbk, pkey, gb = info(n)
            nb = 4 - i0
            accv = accB[:, i0:4, h, :]
            pv = pbk[:, i0 * 64:256].rearrange("p (j d) -> p j d", j=nb)
            if a == 0:
                S.dve(lambda e: e.tensor_copy(out=accv, in_=pv), reads=[pkey], writes=[("accB", h)])
            else:
                S.dve(lambda e: e.tensor_tensor(out=accv, in0=accv, in1=pv, op=ALU.add),
                      reads=[pkey, ("accB", h)], writes=[("accB", h)])

        for n in range(-3, NTL + 1):
            if 0 <= n - 1 < NTL:
                stage_d2(n - 1)
            if 0 <= n + 3 < NTL:
                stage_a(n + 3)
            if 0 <= n + 1 < NTL:
                stage_b(n + 1)
            if 0 <= n < NTL:
                stage_d1(n)
                stage_c(n)
            yield

    def gen_attnA(c):
        na = 4 * c + 4
        tiles = [(h, a) for h in range(8) for a in range(na)]
        NTL = len(tiles)

        def info(n):
            h, a = tiles[n]
            f = h // 2
            base = 64 * (h % 2)
            i0 = max(0, a - 4 * c)
            w = 512 - i0 * 128
            sbi = 4 + (n % 2)
            ei = (4, 5, 0)[n % 3]
            E, Ek = p16[ei], ("p16", ei)
            EM, EMk = EMt[n % 2], ("p16", 6 + n % 2)
            return h, a, f, base, i0, w, banks[sbi], "B%d" % sbi, E, Ek, EM, EMk

        def stage_a(n):
            h, a, f, base, i0, w, sbk, sbkey, E, Ek, EM, EMk = info(n)
            S.pe(lambda e: e.matmul(out=sbk[:, 0:w], lhsT=kTa[base:base + 64, f, a * 128:(a + 1) * 128],
                                    rhs=qTa[base:base + 64, f, i0 * 128:512], start=True, stop=True),
                 reads=[("kTa", f, a // 4), ("qTa", f)], writes=[sbkey])
            far0 = None
            nn = 0
            for jj in range(i0, 4):
                d = 4 * c + jj - a
                blk = (jj - i0) * 128
                if d <= 1:
                    tmp, tk = E32[nn % 2], ("p32", 3 + nn % 2)
                    nn += 1
                    S.dve(lambda e, blk=blk, tmp=tmp, d=d: e.scalar_tensor_tensor(
                        out=tmp[:, 0:128], in0=sbk[:, blk:blk + 128], scalar=0.125, in1=btab[:, h, d, :],
                        op0=ALU.mult, op1=ALU.add),
                        reads=[sbkey, ("btab", d)], writes=[tk])
                    S.act(lambda e, blk=blk, tmp=tmp: e.activation(
                        out=E[:, blk:blk + 128], in_=tmp[:, 0:128], func=AF.Exp, bias=b31[:, h:h + 1], scale=1.0),
                        reads=[tk, "b31"], writes=[Ek])
                else:
                    far0 = blk
                    break
            if far0 is not None:
                S.act(lambda e, far0=far0: e.activation(
                    out=E[:, far0:w], in_=sbk[:, far0:w], func=AF.Exp, bias=b31[:, h:h + 1], scale=0.125),
                    reads=[sbkey, "b31"], writes=[Ek])

        def stage_b(n):
            h, a, f, base, i0, w, sbk, sbkey, E, Ek, EM, EMk = info(n)
            S.dve(lambda e: e.tensor_tensor(out=EM[:, 0:w], in0=E[:, 0:w], in1=maskT[:, a, i0 * 128:512], op=ALU.mult),
                  reads=[Ek] + [("maskT", jj) for jj in range(i0, 4)], writes=[EMk])

        def stage_c(n):
            h, a, f, base, i0, w, sbk, sbkey, E, Ek, EM, EMk = info(n)
            obi = 6 + (h % 2)
            ob, obkey = banks[obi], "B%d" % obi
            for jj in range(i0, 4):
                blk = (jj - i0) * 128
                first = (a == 0 and jj == i0)
                last = (a == 4 * c + jj)
                S.pe(lambda e, jj=jj, blk=blk, first=first, last=last: e.matmul(
                    out=ob[:, jj * 65:(jj + 1) * 65], lhsT=EM[:, blk:blk + 128], rhs=Va[:, a, h, :],
                    start=first, stop=last, skip_group_check=True),
                    reads=[EMk, ("Va", a), "Va_ones"], writes=[obkey])
            if a == na - 1:
                rb = h % 2
                S.dve(lambda e: e.reciprocal(
                    out=rden[:, rb, :], in_=ob[:, 0:260].rearrange("p (j d) -> p j d", j=4)[:, :, 64]),
                    reads=[obkey], writes=[("rden", rb)])
                for jj in range(4):
                    S.dve(lambda e, jj=jj: e.scalar_tensor_tensor(
                        out=mixed[:, jj, h * 64:(h + 1) * 64], in0=ob[:, jj * 65:jj * 65 + 64],
                        scalar=rden[:, rb, jj:jj + 1], in1=sg[:, jj, h * 64:(h + 1) * 64],
                        op0=ALU.mult, op1=ALU.mult),
                        reads=[obkey, ("rden", rb), ("sg", jj, 0)], writes=["mixed"])

        for n in range(-3, NTL):
            if 0 <= n + 1 < NTL:
                stage_b(n + 1)
            if 0 <= n + 3 < NTL:
                stage_a(n + 3)
            if 0 <= n < NTL:
                stage_c(n)
            yield

    def emit_finish(c):
        for jj in range(4):
            S.dve(lambda e, jj=jj: e.tensor_tensor(
                out=mixed[:, jj, 512:1024], in0=accB[:, jj, :, :].rearrange("p h d -> p (h d)"),
                in1=sg[:, jj, 512:1024], op=ALU.mult),
                reads=[("accB", h) for h in range(8)] + [("sg", jj, 1)], writes=["mixed"])
        for jj in range(4):
            for c8 in range(8):
                S.pe(lambda e, jj=jj, c8=c8: e.transpose(out=B0bf[:, c8 * 128:(c8 + 1) * 128],
                                                         in_=mixed[:, jj, c8 * 128:(c8 + 1) * 128], identity=ident[:]),
                     reads=["mixed", "ident"], writes=["B0"])
            evac_copy(hTc[:, :, jj * 128:(jj + 1) * 128], B0bf[:, :].rearrange("p (c t) -> p c t", c=8),
                      ["B0"], [("hTc", c8) for c8 in range(8)])
        wo = wo_pref.pop(c)
        for jj in range(4):
            i = 4 * c + jj
            sl_ = i % 2
            S.dma(lambda e, i=i, sl_=sl_: e.dma_start(out=xt[sl_][:], in_=x[i * 128:(i + 1) * 128, :]),
                  writes=[("xt", sl_)], sem="x%d" % sl_)
            for half in range(2):
                sl, sk = wo[half]
                bk, bkey = proj_bank()
                for c8 in range(8):
                    S.pe(lambda e, c8=c8, bk=bk, sl=sl, jj=jj: e.matmul(
                        out=bk[:], lhsT=hTc[:, c8, jj * 128:(jj + 1) * 128], rhs=sl[:, c8, :],
                        start=(c8 == 0), stop=(c8 == 7)),
                        reads=[sk, ("hTc", c8)], writes=[bkey])
                S.dve(lambda e, bk=bk, sl_=sl_, half=half: e.tensor_tensor(
                    out=xt[sl_][:, half * 512:(half + 1) * 512], in0=bk[:], in1=xt[sl_][:, half * 512:(half + 1) * 512],
                    op=ALU.add),
                    reads=[bkey, ("xt", sl_)], writes=[("xt", sl_)])
            o = S.dma(lambda e, i=i, sl_=sl_: e.dma_start(out=out[i * 128:(i + 1) * 128, :], in_=xt[sl_][:]),
                      reads=[("xt", sl_)], writes=[("out", i)], sem="o%d" % sl_)
            out_ops.append(o)

    def run_interleaved(gens, totals):
        done = [0] * len(gens)
        alive = [True] * len(gens)
        while any(alive):
            best = None
            for gi in range(len(gens)):
                if not alive[gi]:
                    continue
                frac = done[gi] / float(totals[gi])
                if best is None or frac < best[0]:
                    best = (frac, gi)
            gi = best[1]
            try:
                wv = next(gens[gi])
                done[gi] += 1.0 if wv is None else wv
            except StopIteration:
                alive[gi] = False

    def count_steps(genf, c):
        return None

    marks = []

    def mark(name):
        marks.append((name, {e: len(S.ops[e]) for e in S.ENGS}))

    for c in range(NCH):
        mark("norm%d" % c)
        emit_norm(c)
        mark("proj1_%d" % c)
        emit_proj1(c)
        mark("X%d" % c)
        na = 4 * c + 4
        n_b = 8 * na + 4
        npair = 1 if c == 0 else 2
        n_ib = npair * (16 * W_IDX + 2 + niter * W_BIS) + sum(((4 * c + j) // 8 + 1) for j in range(4)) + (2 if c == 0 else 0)

        def chain(c=c):
            yield from gen_proj2(c)
            yield from gen_attnB(c)

        run_interleaved([gen_index_bisect(c), chain()], [n_ib, n_b + 32])
        S.alias(["mixed"], [("qTi", f8) for f8 in range(8)])
        mark("A%d" % c)
        wo_pref[c] = [load_slab([(0, 0, 512)], src=w_out), load_slab([(0, 512, 512)], src=w_out)]
        for _ in gen_attnA(c):
            pass
        mark("fin%d" % c)
        emit_finish(c)
    mark("end")
    nc._marks = marks

    S.final_wait_ops = out_ops
    S.run()
    return nc


_CACHE = {}


def _host_inputs(x_b, norm_gain, w_in, q_norm_gain, k_norm_gain, rel_bias, w_out, consts):
    m = dict(consts)
    m["x"] = np.ascontiguousarray(x_b, dtype=np.float32)
    m["w_in"] = np.ascontiguousarray(w_in[0], dtype=np.float32)
    m["w_out"] = np.ascontiguousarray(w_out[0], dtype=np.float32)
    m["gain"] = np.ascontiguousarray(norm_gain[0].reshape(8, 128).T, dtype=np.float32)
    m["gq"] = np.ascontiguousarray(np.tile(q_norm_gain[0], 2).reshape(128, 1), dtype=np.float32)
    m["gk"] = np.ascontiguousarray(np.tile(k_norm_gain[0], 2).reshape(128, 1), dtype=np.float32)
    m["b31"] = np.ascontiguousarray(np.broadcast_to(rel_bias[31][None, :], (128, 8)), dtype=np.float32)
    m["relb"] = np.ascontiguousarray(np.concatenate([rel_bias, np.ones((1, 8), np.float32)], axis=0),
                                     dtype=np.float32)
    return m


def kernel(x, norm_gain, w_in, q_norm_gain, k_norm_gain, rel_bias, w_out):
    x = np.asarray(x)
    B, L, _ = x.shape
    key = ("nc", L)
    consts = _consts()
    nc = build(L=L)
    in_maps = [_host_inputs(x[b], np.asarray(norm_gain), np.asarray(w_in), np.asarray(q_norm_gain),
                            np.asarray(k_norm_gain), np.asarray(rel_bias), np.asarray(w_out), consts)
               for b in range(B)]
    res = run_bass_kernel_spmd(nc, in_maps, core_ids=list(range(B)))
    return np.stack([np.asarray(r["out"]) for r in res.results], axis=0).astype(np.float32)
```

```python
import math
import numpy as np
import ml_dtypes
import concourse.bass as bass
import concourse.mybir as mybir
from concourse.bass_utils import run_bass_kernel_spmd

F32 = mybir.dt.float32
BF16 = mybir.dt.bfloat16
AF = mybir.ActivationFunctionType
ALU = mybir.AluOpType
AX = mybir.AxisListType

D_MODEL = 1024
D_IN = 5200
IDX_SCALE = (16 * 64) ** -0.5
RMS_EPS = 1e-6
NEG_BIG = -30000.0


class _Op:
    __slots__ = ("eng", "fn", "deps", "tok_sem", "tok_val", "needed", "is_dma", "idx")

    def __init__(self, eng, fn, is_dma, idx):
        self.eng = eng
        self.fn = fn
        self.deps = []
        self.tok_sem = None
        self.tok_val = None
        self.needed = False
        self.is_dma = is_dma
        self.idx = idx


class Sched:
    ENGS = ("pe", "act", "dve", "pool", "sp")

    def __init__(self, nc):
        self.nc = nc
        self.ops = {e: [] for e in self.ENGS}
        self.last_w = {}
        self.readers = {}
        self.final_wait_ops = []

    def alias(self, dst_keys, src_keys):
        acc = []
        for k in src_keys:
            lw = self.last_w.get(k)
            if lw is not None:
                acc.append(lw)
            acc.extend(self.readers.get(k, ()))
        for k in dst_keys:
            self.readers.setdefault(k, [])
            self.readers[k] = list(self.readers[k]) + acc

    def _add(self, eng, fn, reads, writes, is_dma=False, dma_sem=None):
        op = _Op(eng, fn, is_dma, len(self.ops[eng]))
        excl = [k for k in reads if isinstance(k, str) and len(k) == 2 and k[0] == "B" and k[1].isdigit()]
        if excl:
            reads = [k for k in reads if k not in excl]
            writes = list(writes) + excl
        cand = []
        for k in reads:
            lw = self.last_w.get(k)
            if lw is not None:
                cand.append((lw, True))
        for k in writes:
            lw = self.last_w.get(k)
            if lw is not None:
                cand.append((lw, False))
            for r in self.readers.get(k, ()):
                cand.append((r, False))
        best = {}
        for d, raw in cand:
            if d is op:
                continue
            if (not d.is_dma) and (not is_dma) and d.eng == eng:
                if (not raw) or eng == "pe":
                    continue
            if d.is_dma:
                key = ("dma", d.tok_sem)
            else:
                key = ("eng", d.eng)
            cur = best.get(key)
            if cur is None or d.idx > cur.idx:
                best[key] = d
        op.deps = list(best.values())
        for k in writes:
            self.last_w[k] = op
            self.readers[k] = []
        for k in reads:
            self.readers.setdefault(k, []).append(op)
        if is_dma:
            op.tok_sem = dma_sem
        self.ops[eng].append(op)
        return op

    def pe(self, fn, reads=(), writes=()):
        return self._add("pe", fn, reads, writes)

    def act(self, fn, reads=(), writes=()):
        return self._add("act", fn, reads, writes)

    def dve(self, fn, reads=(), writes=()):
        return self._add("dve", fn, reads, writes)

    def pool(self, fn, reads=(), writes=()):
        return self._add("pool", fn, reads, writes)

    def dma(self, fn, reads=(), writes=(), sem="d0", queue="sp"):
        return self._add(queue, fn, reads, writes, is_dma=True, dma_sem=sem)

    def finalize(self):
        for e in self.ENGS:
            for op in self.ops[e]:
                for d in op.deps:
                    d.needed = True
        for op in self.final_wait_ops:
            op.needed = True
        cnt = {e: 0 for e in self.ENGS}
        dcnt = {}
        for e in self.ENGS:
            for op in self.ops[e]:
                if op.is_dma:
                    dcnt[op.tok_sem] = dcnt.get(op.tok_sem, 0) + 16
                    op.tok_val = dcnt[op.tok_sem]
                elif op.needed:
                    cnt[e] += 1
                    op.tok_sem = "E_" + e
                    op.tok_val = cnt[e]
        names = set(dcnt.keys()) | {"E_" + e for e in self.ENGS if cnt[e] > 0}
        return sorted(names)

    def emit(self, engine_obj, eng, sems):
        waited = {}
        for op in self.ops[eng]:
            for d in op.deps:
                s, v = d.tok_sem, d.tok_val
                if waited.get(s, 0) >= v:
                    continue
                engine_obj.wait_ge(sems[s], v)
                waited[s] = v
            ins = op.fn(engine_obj)
            if op.is_dma:
                ins.then_inc(sems[op.tok_sem], 16)
            elif op.needed:
                ins.then_inc(sems[op.tok_sem], 1)
        if eng == "sp":
            for op in self.final_wait_ops:
                s, v = op.tok_sem, op.tok_val
                if waited.get(s, 0) >= v:
                    continue
                engine_obj.wait_ge(sems[s], v)
                waited[s] = v

    def run(self):
        nc = self.nc
        names = self.finalize()
        sems = {n: nc.alloc_semaphore("s_" + n) for n in names}
        sch = self
        with nc.Block() as block:
            @block.sync
            def _(e):
                sch.emit(e, "sp", sems)

            @block.scalar
            def _(e):
                sch.emit(e, "act", sems)

            @block.vector
            def _(e):
                sch.emit(e, "dve", sems)

            @block.tensor
            def _(e):
                sch.emit(e, "pe", sems)

            @block.gpsimd
            def _(e):
                sch.emit(e, "pool", sems)


def _t5_bucket_np(d):
    d = np.maximum(d, 0).astype(np.int64)
    d_f = np.maximum(d, 1).astype(np.float32)
    large = 16 + (np.log(d_f / np.float32(16)) / np.float32(math.log(128 / 16))
                  * np.float32(16)).astype(np.int32)
    large = np.minimum(large, 31)
    return np.where(d < 16, d, large)


def _consts():
    bf = ml_dtypes.bfloat16
    c = {}
    c["ident"] = np.eye(128, dtype=np.float32).astype(bf)
    j = np.arange(128)[:, None]
    s = np.arange(128)[None, :]
    c["negtri"] = np.where(j >= s, -1.0, 0.0).astype(np.float32).astype(bf)
    c["strict"] = np.where(j < s, 1.0, 0.0).astype(np.float32).astype(bf)
    c["blockones"] = np.where((j // 64) == (s // 64), 1.0 / 64, 0.0).astype(np.float32).astype(bf)
    c["onescol"] = np.ones((128, 1), np.float32).astype(bf)
    oh = np.zeros((33, 384), np.float32)
    for i in range(384):
        d = i - 128
        if d < 0:
            oh[32, i] = NEG_BIG
        else:
            b = int(_t5_bucket_np(np.array([d]))[0])
            oh[b, i] += 1.0
            oh[31, i] -= 1.0
    c["onehot"] = oh
    return c


def build(L=2048, niter=22, dbg=False):
    NT = L // 128
    NCH = L // 512
    TOPK = min(256, L // 4)
    nc = bass.Bass("TRN2", target_bir_lowering=False, dynamic_dma_scratch_size=8192)

    def din(name, shape, dt=F32):
        return nc.dram_tensor(name, list(shape), dt, kind="ExternalInput").ap()

    x = din("x", [L, D_MODEL])
    w_in = din("w_in", [D_MODEL, D_IN])
    w_out = din("w_out", [D_MODEL, D_MODEL])
    gain_d = din("gain", [128, 8])
    gq_d = din("gq", [128, 1])
    gk_d = din("gk", [128, 1])
    b31_d = din("b31", [128, 8])
    relb_d = din("relb", [33, 8])
    onehot_d = din("onehot", [33, 384])
    ident_d = din("ident", [128, 128], BF16)
    negtri_d = din("negtri", [128, 128], BF16)
    strict_d = din("strict", [128, 128], BF16)
    blockones_d = din("blockones", [128, 128], BF16)
    onescol_d = din("onescol", [128, 1], BF16)
    out = nc.dram_tensor("out", [L, D_MODEL], F32, kind="ExternalOutput").ap()
    scr1_t = nc.dram_tensor("scr1", [8, 384], F32, kind="Internal")
    scr2_t = nc.dram_tensor("scr2", [128, 8 * 384], F32, kind="Internal")

    def sb(name, shape, dt):
        return nc.alloc_sbuf_tensor(name, list(shape), dt)

    gain = sb("gain_s", [128, 8], F32)
    gq = sb("gq_s", [128, 1], F32)
    gk = sb("gk_s", [128, 1], F32)
    b31 = sb("b31_s", [128, 8], F32)
    relb = sb("relb_s", [33, 8], F32)
    onehot = sb("onehot_s", [33, 384], F32)
    ident = sb("ident_s", [128, 128], BF16)
    negtri = sb("negtri_s", [128, 128], BF16)
    strict = sb("strict_s", [128, 128], BF16)
    blockones = sb("blockones_s", [128, 128], BF16)
    onescol = sb("onescol_s", [128, 1], BF16)
    fb = sb("fb_s", [8, 384], F32)
    btab = sb("btab", [128, 8, 2, 128], F32)
    kTa = sb("kTa", [128, 4, L], BF16)
    kTb = sb("kTb", [128, 4, L], BF16)
    kTi = sb("kTi", [128, L], BF16)
    Va = sb("Va", [128, NT, 8, 65], BF16)
    Vb = sb("Vb", [128, NT, 8, 64], BF16)
    hTc = sb("hTc", [128, 8, 512], BF16)
    qTa = sb("qTa", [128, 4, 512], BF16)
    qTb = sb("qTb", [128, 4, 512], BF16)
    qTi_raw = sb("qTi", [128, 4096], BF16)
    qTi = qTi_raw[:].rearrange("p (c t) -> p c t", c=8)
    mixed = qTi_raw[:].rearrange("p (j f) -> p j f", j=4)
    sg = sb("sg", [128, 4, 1024], BF16)
    ws = sb("ws", [128, 4, 16], F32)
    maskT = sb("maskT", [128, NT, 512], BF16)
    slab = [sb("slab0", [128, 8, 512], BF16), sb("slab1", [128, 8, 512], BF16)]
    score = [sb("score0", [128, L], F32), sb("score1", [128, L], F32)]
    accB_t = sb("accB", [128, 2048], F32)
    accB = accB_t[:].rearrange("p (j h d) -> p j h d", j=4, h=8)
    maskrow = [sb("maskrow0", [128, L], BF16), sb("maskrow1", [128, L], BF16)]
    xt = [sb("xt0", [128, 1024], F32), sb("xt1", [128, 1024], F32)]
    xs = [sb("xs0", [128, 1024], BF16), sb("xs1", [128, 1024], BF16)]
    junk = sb("junk", [128, 1024], BF16)
    NP32 = 8
    NP16 = 8
    p32 = [sb("p32_%d" % i, [128, 512], F32) for i in range(NP32)]
    p16 = [sb("p16_%d" % i, [128, 512], BF16) for i in range(NP16)]
    small = sb("small", [128, 64], F32)
    bis = sb("bis", [128, 2, 8 + 32], F32)
    halves = sb("halves", [128, 32], F32)
    rden = sb("rden", [128, 2, 4], F32)
    gsb = sb("gsb", [128, 2, 4], F32)

    banks = [nc.alloc_psum_tensor("B%d" % i, [128, 512], F32) for i in range(8)]
    B0bf = banks[0][:].bitcast(BF16)

    S = Sched(nc)
    cnt32 = [0]
    cnt16 = [0]

    def get32():
        i = 3 + (cnt32[0] % 3)
        cnt32[0] += 1
        return p32[i], ("p32", i)

    def get16():
        i = cnt16[0] % 4
        cnt16[0] += 1
        return p16[i], ("p16", i)

    def cload(dst, src, key):
        S.dma(lambda e: e.dma_start(out=dst[:], in_=src), writes=[key], sem="c_" + key)

    cload(gain, gain_d, "gain")
    cload(gq, gq_d, "gq")
    cload(gk, gk_d, "gk")
    cload(b31, b31_d, "b31")
    cload(relb, relb_d, "relb")
    cload(onehot, onehot_d, "onehot")
    cload(ident, ident_d, "ident")
    cload(negtri, negtri_d, "negtri")
    cload(strict, strict_d, "strict")
    cload(blockones, blockones_d, "blockones")
    cload(onescol, onescol_d, "onescol")
    for k in range(niter + 1):
        S.dve(lambda e, k=k: e.memset(halves[:, k:k + 1], 2.0 ** -(k + 1)), writes=[("halves", k)])
    S.pool(lambda e: e.memset(Va[:, :, :, 64:65], 1.0), writes=["Va_ones"])

    S.pe(lambda e: e.matmul(out=banks[3][0:8, 0:384], lhsT=relb[:, :], rhs=onehot[:, :], start=True, stop=True),
         reads=["relb", "onehot"], writes=["B3"])
    S.act(lambda e: e.activation(out=fb[:], in_=banks[3][0:8, 0:384], func=AF.Copy), reads=["B3"], writes=["fb"])
    S.dma(lambda e: e.dma_start(out=scr1_t.ap(), in_=fb[:]), reads=["fb"], writes=["scr1"], sem="c_scr1")
    scr2_v = scr2_t.ap().rearrange("p (h i) -> p h i", h=8)
    S.dma(lambda e: e.dma_start(out=scr2_v, in_=scr1_t.ap().partition_broadcast(128)),
          reads=["scr1"], writes=["scr2"], sem="c_scr2")
    for r in range(2):
        src = bass.AP(tensor=scr2_t, offset=128 * (r + 1), ap=[[8 * 384 - 1, 128], [384, 8], [1, 128]])
        S.dma(lambda e, r=r, src=src: e.dma_start(out=btab[:, :, r, :], in_=src),
              reads=["scr2"], writes=[("btab", r)], sem="c_btab%d" % r)

    slab_ctr = [0]

    def load_slab(pieces, src=None):
        if src is None:
            src = w_in
        i = slab_ctr[0] % 2
        slab_ctr[0] += 1
        key = ("slab", i)
        for (d0, s0, n) in pieces:
            S.dma(lambda e, i=i, d0=d0, s0=s0, n=n, src=src: e.dma_start(
                out=slab[i][:, :, d0:d0 + n],
                in_=src[:, s0:s0 + n].rearrange("(c p) n -> p c n", p=128)),
                writes=[key], sem="w%d" % i, queue="pool")
        return slab[i], key

    pbank_ctr = [0]

    def proj_bank():
        i = 1 + (pbank_ctr[0] % 2)
        pbank_ctr[0] += 1
        return banks[i], "B%d" % i

    def fm_matmuls(sl, skey, f, ncols_feat=128, col0=None, bank=None):
        bk, bkey = proj_bank() if bank is None else bank
        c0 = f * 128 if col0 is None else col0
        for c8 in range(8):
            S.pe(lambda e, c8=c8, bk=bk, sl=sl, c0=c0: e.matmul(
                out=bk[0:ncols_feat, :], lhsT=sl[:, c8, c0:c0 + ncols_feat], rhs=hTc[:, c8, :],
                start=(c8 == 0), stop=(c8 == 7)),
                reads=[skey, ("hTc", c8)], writes=[bkey])
        return bk, bkey

    def tm_matmuls(sl, skey, j, ncols=512, col0=0, bank=None):
        bk, bkey = proj_bank() if bank is None else bank
        for c8 in range(8):
            S.pe(lambda e, c8=c8, bk=bk, sl=sl: e.matmul(
                out=bk[:, 0:ncols], lhsT=hTc[:, c8, j * 128:(j + 1) * 128], rhs=sl[:, c8, col0:col0 + ncols],
                start=(c8 == 0), stop=(c8 == 7)),
                reads=[skey, ("hTc", c8)], writes=[bkey])
        return bk, bkey

    def fm_g(sl, skey, f, bank):
        bk, bkey = bank
        c0 = f * 128
        for c8 in range(8):
            S.pe(lambda e, c8=c8, bk=bk, sl=sl, c0=c0: e.matmul(
                out=bk[:, :], lhsT=sl[:, c8, c0:c0 + 128], rhs=hTc[:, c8, :],
                start=(c8 == 0), stop=(c8 == 7)),
                reads=[skey, ("hTc", c8)], writes=[bkey])
            if c8 % 2 == 1 and c8 < 7:
                yield 0.25
        return bk, bkey

    def tm_g(sl, skey, j, bank):
        bk, bkey = bank
        for c8 in range(8):
            S.pe(lambda e, c8=c8, bk=bk, sl=sl: e.matmul(
                out=bk[:, 0:512], lhsT=hTc[:, c8, j * 128:(j + 1) * 128], rhs=sl[:, c8, 0:512],
                start=(c8 == 0), stop=(c8 == 7)),
                reads=[skey, ("hTc", c8)], writes=[bkey])
            if c8 % 2 == 1 and c8 < 7:
                yield 0.25
        return bk, bkey

    def qknorm(bk, bkey, gvec, gkey, dst_ap, dst_key):
        sq, sqk = get16()
        S.act(lambda e: e.activation(out=sq[:], in_=bk[:], func=AF.Square), reads=[bkey], writes=[sqk])
        S.pe(lambda e: e.matmul(out=banks[3][:], lhsT=blockones[:], rhs=sq[:], start=True, stop=True),
             reads=[sqk, "blockones"], writes=["B3"])
        lt, ltk = get32()
        S.act(lambda e: e.activation(out=lt[:], in_=banks[3][:], func=AF.Ln, bias=RMS_EPS, scale=1.0),
              reads=["B3"], writes=[ltk])
        S.act(lambda e: e.activation(out=lt[:], in_=lt[:], func=AF.Exp, scale=-0.5), reads=[ltk], writes=[ltk])
        S.dve(lambda e: e.scalar_tensor_tensor(out=dst_ap, in0=bk[:], scalar=gvec[:, 0:1], in1=lt[:],
                                               op0=ALU.mult, op1=ALU.mult),
              reads=[bkey, ltk, gkey], writes=[dst_key])

    evac_ctr = [0]

    def evac_copy(dst_ap, src_ap, reads, writes, scale=None):
        i = evac_ctr[0]
        evac_ctr[0] += 1
        if i % 2 == 0:
            if scale is None:
                S.act(lambda e: e.activation(out=dst_ap, in_=src_ap, func=AF.Copy), reads=reads, writes=writes)
            else:
                S.act(lambda e: e.activation(out=dst_ap, in_=src_ap, func=AF.Copy, scale=scale),
                      reads=reads, writes=writes)
        else:
            if scale is None:
                S.dve(lambda e: e.tensor_copy(out=dst_ap, in_=src_ap), reads=reads, writes=writes)
            else:
                S.dve(lambda e: e.tensor_scalar(out=dst_ap, in0=src_ap, scalar1=scale, scalar2=None, op0=ALU.mult),
                      reads=reads, writes=writes)

    out_ops = []
    wo_pref = {}

    R32 = p32[0:3]
    E32 = p32[3:5]
    SPt = p16[0:2]
    APt = p16[2:4]
    Et = p16[4:6]
    EMt = p16[6:8]

    def emit_norm(c):
        for j in range(4):
            i = 4 * c + j
            sl_ = i % 2
            S.dma(lambda e, i=i, sl_=sl_: e.dma_start(out=xt[sl_][:], in_=x[i * 128:(i + 1) * 128, :]),
                  writes=[("xt", sl_)], sem="x%d" % sl_)
            S.act(lambda e, sl_=sl_: e.activation(out=junk[:], in_=xt[sl_][:], func=AF.Square,
                                                  accum_out=small[:, sl_:sl_ + 1]),
                  reads=[("xt", sl_)], writes=[("ss", sl_), "junk"])
            S.act(lambda e, sl_=sl_: e.activation(out=small[:, 2 + sl_:3 + sl_], in_=small[:, sl_:sl_ + 1],
                                                  func=AF.Ln, scale=1.0 / D_MODEL, bias=RMS_EPS),
                  reads=[("ss", sl_)], writes=[("lnv", sl_)])
            S.act(lambda e, sl_=sl_: e.activation(out=small[:, 4 + sl_:5 + sl_], in_=small[:, 2 + sl_:3 + sl_],
                                                  func=AF.Exp, scale=-0.5),
                  reads=[("lnv", sl_)], writes=[("rstd", sl_)])
            S.dve(lambda e, sl_=sl_: e.tensor_scalar(out=xs[sl_][:], in0=xt[sl_][:], scalar1=small[:, 4 + sl_:5 + sl_],
                                                     scalar2=None, op0=ALU.mult),
                  reads=[("xt", sl_), ("rstd", sl_)], writes=[("xs", sl_)])
            for c8 in range(8):
                S.pe(lambda e, c8=c8, sl_=sl_: e.transpose(out=B0bf[:, c8 * 128:(c8 + 1) * 128],
                                                           in_=xs[sl_][:, c8 * 128:(c8 + 1) * 128], identity=ident[:]),
                     reads=[("xs", sl_), "ident"], writes=["B0"])
            for c8 in range(8):
                dst = hTc[:, c8, j * 128:(j + 1) * 128]
                srcp = B0bf[:, c8 * 128:(c8 + 1) * 128]
                if j % 2 == 0:
                    S.dve(lambda e, dst=dst, srcp=srcp, c8=c8: e.tensor_scalar(
                        out=dst, in0=srcp, scalar1=gain[:, c8:c8 + 1], scalar2=None, op0=ALU.mult),
                        reads=["B0", "gain"], writes=[("hTc", c8)])
                else:
                    S.act(lambda e, dst=dst, srcp=srcp, c8=c8: e.activation(
                        out=dst, in_=srcp, func=AF.Copy, scale=gain[:, c8:c8 + 1]),
                        reads=["B0", "gain"], writes=[("hTc", c8)])

    def emit_proj1(c):
        tok0 = c * 512
        sl, sk = load_slab([(0, 3072, 64), (64, 3072, 64), (128, 3136, 16)])
        bk, bkey = fm_matmuls(sl, sk, 0)
        evac_copy(kTi[:, tok0:tok0 + 512], bk[:], [bkey], [("kTi", c)])
        for j in range(4):
            bk, bkey = tm_matmuls(sl, sk, j, ncols=16, col0=128)
            S.dve(lambda e, bk=bk, j=j: e.tensor_scalar(out=ws[:, j, :], in0=bk[:, 0:16], scalar1=IDX_SCALE,
                                                        scalar2=None, op0=ALU.mult),
                  reads=[bkey], writes=[("ws", j)])
        S.alias([("qTi", f8) for f8 in range(8)], ["mixed"])
        for half in range(2):
            sl, sk = load_slab([(0, 2048 + 512 * half, 512)])
            for f in range(4):
                bk, bkey = fm_matmuls(sl, sk, f)
                evac_copy(qTi[:, 4 * half + f, :], bk[:], [bkey], [("qTi", 4 * half + f)])

    def gen_proj2(c):
        tok0 = c * 512
        pb2 = [0]

        def bank2():
            i = 6 + (pb2[0] % 2)
            pb2[0] += 1
            return banks[i], "B%d" % i

        sl, sk = load_slab([(0, 512, 512)])
        for f in range(4):
            bk, bkey = yield from fm_g(sl, sk, f, bank2())
            qknorm(bk, bkey, gk, "gk", kTa[:, f, tok0:tok0 + 512], ("kTa", f, c))
            yield 0.25
        sl, sk = load_slab([(0, 1024, 512)])
        for j in range(4):
            i = 4 * c + j
            bk, bkey = yield from tm_g(sl, sk, j, bank2())
            evac_copy(Va[:, i, :, 0:64], bk[:].rearrange("p (h d) -> p h d", h=8), [bkey, "Va_ones"], [("Va", i)])
            yield 0.25
        sl, sk = load_slab([(0, 3664, 512)])
        for f in range(4):
            bk, bkey = yield from fm_g(sl, sk, f, bank2())
            evac_copy(kTb[:, f, tok0:tok0 + 512], bk[:], [bkey], [("kTb", f, c)], scale=0.125)
            yield 0.25
        sl, sk = load_slab([(0, 4176, 512)])
        for j in range(4):
            i = 4 * c + j
            bk, bkey = yield from tm_g(sl, sk, j, bank2())
            evac_copy(Vb[:, i, :, :], bk[:].rearrange("p (h d) -> p h d", h=8), [bkey], [("Vb", i)])
            yield 0.25
        sl, sk = load_slab([(0, 0, 512)])
        for f in range(4):
            bk, bkey = yield from fm_g(sl, sk, f, bank2())
            qknorm(bk, bkey, gq, "gq", qTa[:, f, :], ("qTa", f))
            yield 0.25
        sl, sk = load_slab([(0, 3152, 512)])
        for f in range(4):
            bk, bkey = yield from fm_g(sl, sk, f, bank2())
            evac_copy(qTb[:, f, :], bk[:], [bkey], [("qTb", f)])
            yield 0.25
        for half, col in ((0, 1536), (1, 4688)):
            sl, sk = load_slab([(0, col, 512)])
            for j in range(4):
                bk, bkey = yield from tm_g(sl, sk, j, bank2())
                S.act(lambda e, bk=bk, j=j, half=half: e.activation(
                    out=sg[:, j, half * 512:(half + 1) * 512], in_=bk[:], func=AF.Silu),
                    reads=[bkey], writes=[("sg", j, half)])
                yield 0.25

    W_IDX = 1.0
    W_BIS = 2.2

    def gen_index_bisect(c):
        dctr = [0]
        rctr = [0]

        def indexer_tiles(js):
          for hh in range(16):
            f8 = hh // 2
            base = 64 * (hh % 2)
            for scn in range(c + 1):
              for j in js:
                sc_t = score[j % 2]
                sckey = ("score", j % 2)
                if True:
                    w = 512 if scn < c else (j + 1) * 128
                    bi = (1, 2, 0)[dctr[0] % 3]
                    dctr[0] += 1
                    bk = banks[bi]
                    bkey = "B%d" % bi
                    S.pe(lambda e, bk=bk, f8=f8, base=base, scn=scn, w=w, j=j: e.matmul(
                        out=bk[:, 0:w], lhsT=qTi[base:base + 64, f8, j * 128:(j + 1) * 128],
                        rhs=kTi[base:base + 64, scn * 512:scn * 512 + w], start=True, stop=True),
                        reads=[("qTi", f8), ("kTi", scn)], writes=[bkey])
                    ri = (0, 1, 2, 6, 7)[rctr[0] % 5]
                    rctr[0] += 1
                    r, rk = p32[ri], ("p32", ri)
                    S.act(lambda e, bk=bk, r=r, w=w: e.activation(out=r[:, 0:w], in_=bk[:, 0:w], func=AF.Relu),
                          reads=[bkey], writes=[rk])
                    dst = sc_t[:, scn * 512:scn * 512 + w]
                    if hh == 0:
                        S.dve(lambda e, dst=dst, r=r, w=w, j=j, hh=hh: e.tensor_scalar(
                            out=dst, in0=r[:, 0:w], scalar1=ws[:, j, hh:hh + 1], scalar2=None, op0=ALU.mult),
                            reads=[rk, ("ws", j)], writes=[sckey])
                    else:
                        S.dve(lambda e, dst=dst, r=r, w=w, j=j, hh=hh: e.scalar_tensor_tensor(
                            out=dst, in0=r[:, 0:w], scalar=ws[:, j, hh:hh + 1], in1=dst,
                            op0=ALU.mult, op1=ALU.add),
                            reads=[rk, ("ws", j), sckey], writes=[sckey])
            yield W_IDX

        def bisect_tiles(js):
            bs = [j % 2 for j in js]
            b0, b1 = min(bs), max(bs) + 1
            bkeys = [("bis", b) for b in bs]
            for j in js:
                i = 4 * c + j
                Si = 128 * (i + 1)
                b = j % 2
                sc_t = score[b]
                sckey = ("score", b)
                bk_ = ("bis", b)
                S.dve(lambda e, sc_t=sc_t, Si=Si, b=b: e.tensor_reduce(out=bis[:, b, 0:1], in_=sc_t[:, 0:Si],
                                                                       axis=AX.X, op=ALU.min),
                      reads=[sckey], writes=[bk_])
                S.pool(lambda e, sc_t=sc_t, Si=Si: e.affine_select(
                    out=sc_t[:, Si - 128:Si], in_=sc_t[:, Si - 128:Si], pattern=[[-1, 128]],
                    compare_op=ALU.is_ge, fill=-3.0e38, base=0, channel_multiplier=1),
                    reads=[sckey, bk_], writes=[sckey])
                S.dve(lambda e, sc_t=sc_t, Si=Si, b=b: e.tensor_reduce(out=bis[:, b, 1:2], in_=sc_t[:, 0:Si],
                                                                       axis=AX.X, op=ALU.max),
                      reads=[sckey, bk_], writes=[bk_])
                yield
            if len(bs) == 2:
                S.dve(lambda e: e.tensor_tensor(out=bis[:, 0, 0:1], in0=bis[:, 0, 0:1], in1=bis[:, 1, 0:1], op=ALU.min),
                      reads=bkeys, writes=[("bis", 0)])
                S.dve(lambda e: e.tensor_tensor(out=bis[:, 0, 1:2], in0=bis[:, 0, 1:2], in1=bis[:, 1, 1:2], op=ALU.max),
                      reads=bkeys, writes=[("bis", 0)])
            S.dve(lambda e: e.tensor_tensor(out=bis[:, b0, 2:3], in0=bis[:, b0, 1:2], in1=bis[:, b0, 0:1],
                                            op=ALU.subtract), reads=bkeys, writes=[("bis", b0)])
            S.dve(lambda e: e.tensor_scalar(out=bis[:, b0, 8:8 + niter + 1], in0=halves[:, 0:niter + 1],
                                            scalar1=bis[:, b0, 2:3], scalar2=None, op0=ALU.mult),
                  reads=bkeys + [("halves", k) for k in range(niter + 1)], writes=[("bis", b0)])
            for b in bs:
                S.dve(lambda e, b=b: e.tensor_tensor(out=bis[:, b, 3:4], in0=bis[:, b0, 0:1], in1=bis[:, b0, 8:9],
                                                     op=ALU.add), reads=bkeys, writes=[("bis", b)])
            yield
            for k in range(niter):
                for j in js:
                    i = 4 * c + j
                    Si = 128 * (i + 1)
                    b = j % 2
                    S.dve(lambda e, sc_t=score[b], Si=Si, b=b, jk=R32[b][:].bitcast(mybir.dt.uint8): e.tensor_scalar(
                        out=jk[:, 0:Si], in0=sc_t[:, 0:Si], scalar1=bis[:, b, 3:4], scalar2=None,
                        op0=ALU.is_ge, op1=ALU.add, accum_out=bis[:, b, 4:5]),
                        reads=[("score", b), ("bis", b)], writes=[("p32", b), ("bis", b)])
                S.dve(lambda e, k=k: e.tensor_scalar(
                    out=bis[:, b0:b1, 5], in0=bis[:, b0:b1, 4], scalar1=float(TOPK) - 0.5,
                    scalar2=bis[:, b0, 8 + k:9 + k], op0=(ALU.is_ge if k < niter - 1 else ALU.is_lt),
                    op1=ALU.mult),
                    reads=bkeys, writes=bkeys)
                if k < niter - 1:
                    S.dve(lambda e, k=k: e.scalar_tensor_tensor(
                        out=bis[:, b0:b1, 3], in0=bis[:, b0:b1, 3], scalar=bis[:, b0, 9 + k:10 + k],
                        in1=bis[:, b0:b1, 5], op0=ALU.subtract, op1=ALU.add),
                        reads=bkeys, writes=bkeys)
                else:
                    S.dve(lambda e: e.tensor_tensor(out=bis[:, b0:b1, 6], in0=bis[:, b0:b1, 3],
                                                    in1=bis[:, b0:b1, 5], op=ALU.subtract),
                          reads=bkeys, writes=bkeys)
                yield W_BIS
            for j in js:
                i = 4 * c + j
                Si = 128 * (i + 1)
                b = j % 2
                sc_t = score[b]
                sckey = ("score", b)
                bk_ = ("bis", b)
                mr = maskrow[b]
                mrk = ("maskrow", b)
                S.dve(lambda e, sc_t=sc_t, Si=Si, b=b, mr=mr: e.tensor_scalar(
                    out=mr[:, 0:Si], in0=sc_t[:, 0:Si], scalar1=bis[:, b, 6:7], scalar2=None, op0=ALU.is_ge),
                    reads=[sckey, bk_], writes=[mrk])
                a0 = 0
                while a0 <= i:
                    n = min(8, i + 1 - a0)
                    for q in range(n):
                        a = a0 + q
                        S.pe(lambda e, mr=mr, a=a, q=q: e.transpose(out=B0bf[:, q * 128:(q + 1) * 128],
                                                                    in_=mr[:, a * 128:(a + 1) * 128], identity=ident[:]),
                             reads=[mrk, "ident"], writes=["B0"])
                    dst = maskT[:, a0:a0 + n, j * 128:(j + 1) * 128]
                    srcp = B0bf[:, 0:n * 128].rearrange("p (n t) -> p n t", n=n)
                    S.act(lambda e, dst=dst, srcp=srcp: e.activation(out=dst, in_=srcp, func=AF.Copy),
                          reads=["B0"], writes=[("maskT", j)])
                    a0 += n
                    yield

        for jp in range(2):
            js = []
            for j in (2 * jp, 2 * jp + 1):
                i = 4 * c + j
                if 128 * (i + 1) <= TOPK:
                    S.pool(lambda e, i=i, j=j: e.memset(maskT[:, 0:i + 1, j * 128:(j + 1) * 128], 1.0),
                           writes=[("maskT", j)])
                    yield
                else:
                    js.append(j)
            if js:
                yield from indexer_tiles(js)
                yield from bisect_tiles(js)

    def gen_attnB(c):
        na = 4 * c + 4
        tiles = [(h, a) for h in range(8) for a in range(na)]
        NTL = len(tiles)
        zb, zkey = banks[4], "B4"
        lb, lkey = banks[5], "B5"
        st = {}

        def info(n):
            h, a = tiles[n]
            f = h // 2
            base = 64 * (h % 2)
            i0 = max(0, a - 4 * c)
            w = 512 - i0 * 128
            diag = a >= 4 * c
            kl = kTb[base:base + 64, f, a * 128:(a + 1) * 128]
            qr = qTb[base:base + 64, f, i0 * 128:512]
            spi = (0, 1, 4)[n % 3]
            SP, SPk = p16[spi], ("p16", spi)
            Ap, Apk = APt[n % 2], ("p16", 2 + n % 2)
            pbi = 6 + (n % 2)
            return h, a, f, i0, w, diag, kl, qr, SP, SPk, Ap, Apk, banks[pbi], "B%d" % pbi, n % 2

        def stage_a(n):
            h, a, f, i0, w, diag, kl, qr, SP, SPk, Ap, Apk, pbk, pkey, gb = info(n)
            S.pe(lambda e: e.matmul(out=zb[:, 0:w], lhsT=kl, rhs=qr, start=True, stop=True),
                 reads=[("kTb", f, a // 4), ("qTb", f)], writes=[zkey])
            e1, e1k = E32[n % 2], ("p32", 3 + n % 2)
            S.act(lambda e: e.activation(out=e1[:, 0:w], in_=zb[:, 0:w], func=AF.Exp), reads=[zkey], writes=[e1k])
            S.act(lambda e: e.activation(out=SP[:, 0:w], in_=e1[:, 0:w], func=AF.Ln, bias=1.0, scale=1.0),
                  reads=[e1k], writes=[SPk])
            if diag:
                S.pool(lambda e: e.tensor_tensor(out=SP[:, 0:128], in0=SP[:, 0:128], in1=strict[:], op=ALU.mult),
                       reads=[SPk, "strict"], writes=[SPk])

        def stage_b(n):
            h, a, f, i0, w, diag, kl, qr, SP, SPk, Ap, Apk, pbk, pkey, gb = info(n)
            S.pe(lambda e: e.matmul(out=lb[:, 0:w], lhsT=kl, rhs=qr, start=True, stop=False),
                 reads=[("kTb", f, a // 4), ("qTb", f)], writes=[lkey])
            S.pe(lambda e: e.matmul(out=lb[:, 0:w], lhsT=negtri[:], rhs=SP[:, 0:w], start=False, stop=True),
                 reads=[SPk, "negtri"], writes=[lkey])
            if a > 0:
                fcs = True
                for jj in range(i0, 4):
                    blk = (jj - i0) * 128
                    S.pe(lambda e, blk=blk, jj=jj, fcs=fcs: e.matmul(
                        out=banks[3][:, jj:jj + 1], lhsT=SP[:, blk:blk + 128], rhs=onescol[:, 0:1],
                        start=fcs, stop=(jj == 3), skip_group_check=True),
                        reads=[SPk, "onescol"], writes=["B3"])
                    fcs = False
                S.act(lambda e: e.activation(out=gsb[:, gb, i0:4], in_=banks[3][:, i0:4], func=AF.Exp, scale=-1.0),
                      reads=["B3"], writes=[("gsb", gb)])
            S.act(lambda e: e.activation(out=Ap[:, 0:w], in_=lb[:, 0:w], func=AF.Exp), reads=[lkey], writes=[Apk])
            if diag:
                S.pool(lambda e: e.tensor_tensor(out=Ap[:, 0:128], in0=Ap[:, 0:128], in1=strict[:], op=ALU.mult),
                       reads=[Apk, "strict"], writes=[Apk])

        def stage_c(n):
            h, a, f, i0, w, diag, kl, qr, SP, SPk, Ap, Apk, pbk, pkey, gb = info(n)
            fp = True
            for jj in range(i0, 4):
                blk = (jj - i0) * 128
                S.pe(lambda e, jj=jj, blk=blk, fp=fp: e.matmul(
                    out=pbk[:, jj * 64:(jj + 1) * 64], lhsT=Ap[:, blk:blk + 128], rhs=Vb[:, a, h, :],
                    start=fp, stop=(jj == 3), skip_group_check=True),
                    reads=[Apk, ("Vb", a)], writes=[pkey])
                fp = False

        def stage_d1(n):
            h, a, f, i0, w, diag, kl, qr, SP, SPk, Ap, Apk, pbk, pkey, gb = info(n)
            if a == 0:
                return
            nb = 4 - i0
            accv = accB[:, i0:4, h, :]
            gv = gsb[:, gb, i0:4].unsqueeze(2).broadcast_to([128, nb, 64])
            S.dve(lambda e: e.tensor_tensor(out=accv, in0=accv, in1=gv, op=ALU.mult),
                  reads=[("gsb", gb), ("accB", h)], writes=[("accB", h)])

        def stage_d2(n):
            h, a, f, i0, w, diag, kl, qr, SP, SPk, Ap, Apk, pbk, pkey, gb = info(n)
            nb = 4 - i0
            accv = accB[:, i0:4, h, :]
            pv = pbk[:, i0 * 64:256].rearrange("p (j d) -> p j d", j=nb)
            if a == 0:
                S.dve(lambda e: e.tensor_copy(out=accv, in_=pv), reads=[pkey], writes=[("accB", h)])
            else:
                S.dve(lambda e: e.tensor_tensor(out=accv, in0=accv, in1=pv, op=ALU.add),
                      reads=[pkey, ("accB", h)], writes=[("accB", h)])

        for n in range(-3, NTL + 1):
            if 0 <= n - 1 < NTL:
                stage_d2(n - 1)
            if 0 <= n + 3 < NTL:
                stage_a(n + 3)
            if 0 <= n + 1 < NTL:
                stage_b(n + 1)
            if 0 <= n < NTL:
                stage_d1(n)
                stage_c(n)
            yield

    def gen_attnA(c):
        na = 4 * c + 4
        tiles = [(h, a) for h in range(8) for a in range(na)]
        NTL = len(tiles)

        def info(n):
            h, a = tiles[n]
            f = h // 2
            base = 64 * (h % 2)
            i0 = max(0, a - 4 * c)
            w = 512 - i0 * 128
            sbi = 4 + (n % 2)
            ei = (4, 5, 0)[n % 3]
            E, Ek = p16[ei], ("p16", ei)
            EM, EMk = EMt[n % 2], ("p16", 6 + n % 2)
            return h, a, f, base, i0, w, banks[sbi], "B%d" % sbi, E, Ek, EM, EMk

        def stage_a(n):
            h, a, f, base, i0, w, sbk, sbkey, E, Ek, EM, EMk = info(n)
            S.pe(lambda e: e.matmul(out=sbk[:, 0:w], lhsT=kTa[base:base + 64, f, a * 128:(a + 1) * 128],
                                    rhs=qTa[base:base + 64, f, i0 * 128:512], start=True, stop=True),
                 reads=[("kTa", f, a // 4), ("qTa", f)], writes=[sbkey])
            far0 = None
            nn = 0
            for jj in range(i0, 4):
                d = 4 * c + jj - a
                blk = (jj - i0) * 128
                if d <= 1:
                    tmp, tk = E32[nn % 2], ("p32", 3 + nn % 2)
                    nn += 1
                    S.dve(lambda e, blk=blk, tmp=tmp, d=d: e.scalar_tensor_tensor(
                        out=tmp[:, 0:128], in0=sbk[:, blk:blk + 128], scalar=0.125, in1=btab[:, h, d, :],
                        op0=ALU.mult, op1=ALU.add),
                        reads=[sbkey, ("btab", d)], writes=[tk])
                    S.act(lambda e, blk=blk, tmp=tmp: e.activation(
                        out=E[:, blk:blk + 128], in_=tmp[:, 0:128], func=AF.Exp, bias=b31[:, h:h + 1], scale=1.0),
                        reads=[tk, "b31"], writes=[Ek])
                else:
                    far0 = blk
                    break
            if far0 is not None:
                S.act(lambda e, far0=far0: e.activation(
                    out=E[:, far0:w], in_=sbk[:, far0:w], func=AF.Exp, bias=b31[:, h:h + 1], scale=0.125),
                    reads=[sbkey, "b31"], writes=[Ek])

        def stage_b(n):
            h, a, f, base, i0, w, sbk, sbkey, E, Ek, EM, EMk = info(n)
            S.dve(lambda e: e.tensor_tensor(out=EM[:, 0:w], in0=E[:, 0:w], in1=maskT[:, a, i0 * 128:512], op=ALU.mult),
                  reads=[Ek] + [("maskT", jj) for jj in range(i0, 4)], writes=[EMk])

        def stage_c(n):
            h, a, f, base, i0, w, sbk, sbkey, E, Ek, EM, EMk = info(n)
            obi = 6 + (h % 2)
            ob, obkey = banks[obi], "B%d" % obi
            for jj in range(i0, 4):
                blk = (jj - i0) * 128
                first = (a == 0 and jj == i0)
                last = (a == 4 * c + jj)
                S.pe(lambda e, jj=jj, blk=blk, first=first, last=last: e.matmul(
                    out=ob[:, jj * 65:(jj + 1) * 65], lhsT=EM[:, blk:blk + 128], rhs=Va[:, a, h, :],
                    start=first, stop=last, skip_group_check=True),
                    reads=[EMk, ("Va", a), "Va_ones"], writes=[obkey])
            if a == na - 1:
                rb = h % 2
                S.dve(lambda e: e.reciprocal(
                    out=rden[:, rb, :], in_=ob[:, 0:260].rearrange("p (j d) -> p j d", j=4)[:, :, 64]),
                    reads=[obkey], writes=[("rden", rb)])
                for jj in range(4):
                    S.dve(lambda e, jj=jj: e.scalar_tensor_tensor(
                        out=mixed[:, jj, h * 64:(h + 1) * 64], in0=ob[:, jj * 65:jj * 65 + 64],
                        scalar=rden[:, rb, jj:jj + 1], in1=sg[:, jj, h * 64:(h + 1) * 64],
                        op0=ALU.mult, op1=ALU.mult),
                        reads=[obkey, ("rden", rb), ("sg", jj, 0)], writes=["mixed"])

        for n in range(-3, NTL):
            if 0 <= n + 1 < NTL:
                stage_b(n + 1)
            if 0 <= n + 3 < NTL:
                stage_a(n + 3)
            if 0 <= n < NTL:
                stage_c(n)
            yield

    def emit_finish(c):
        for jj in range(4):
            S.dve(lambda e, jj=jj: e.tensor_tensor(
                out=mixed[:, jj, 512:1024], in0=accB[:, jj, :, :].rearrange("p h d -> p (h d)"),
                in1=sg[:, jj, 512:1024], op=ALU.mult),
                reads=[("accB", h) for h in range(8)] + [("sg", jj, 1)], writes=["mixed"])
        for jj in range(4):
            for c8 in range(8):
                S.pe(lambda e, jj=jj, c8=c8: e.transpose(out=B0bf[:, c8 * 128:(c8 + 1) * 128],
                                                         in_=mixed[:, jj, c8 * 128:(c8 + 1) * 128], identity=ident[:]),
                     reads=["mixed", "ident"], writes=["B0"])
            evac_copy(hTc[:, :, jj * 128:(jj + 1) * 128], B0bf[:, :].rearrange("p (c t) -> p c t", c=8),
                      ["B0"], [("hTc", c8) for c8 in range(8)])
        wo = wo_pref.pop(c)
        for jj in range(4):
            i = 4 * c + jj
            sl_ = i % 2
            S.dma(lambda e, i=i, sl_=sl_: e.dma_start(out=xt[sl_][:], in_=x[i * 128:(i + 1) * 128, :]),
                  writes=[("xt", sl_)], sem="x%d" % sl_)
            for half in range(2):
                sl, sk = wo[half]
                bk, bkey = proj_bank()
                for c8 in range(8):
                    S.pe(lambda e, c8=c8, bk=bk, sl=sl, jj=jj: e.matmul(
                        out=bk[:], lhsT=hTc[:, c8, jj * 128:(jj + 1) * 128], rhs=sl[:, c8, :],
                        start=(c8 == 0), stop=(c8 == 7)),
                        reads=[sk, ("hTc", c8)], writes=[bkey])
                S.dve(lambda e, bk=bk, sl_=sl_, half=half: e.tensor_tensor(
                    out=xt[sl_][:, half * 512:(half + 1) * 512], in0=bk[:], in1=xt[sl_][:, half * 512:(half + 1) * 512],
                    op=ALU.add),
                    reads=[bkey, ("xt", sl_)], writes=[("xt", sl_)])
            o = S.dma(lambda e, i=i, sl_=sl_: e.dma_start(out=out[i * 128:(i + 1) * 128, :], in_=xt[sl_][:]),
                      reads=[("xt", sl_)], writes=[("out", i)], sem="o%d" % sl_)
            out_ops.append(o)

    def run_interleaved(gens, totals):
        done = [0] * len(gens)
        alive = [True] * len(gens)
        while any(alive):
            best = None
            for gi in range(len(gens)):
                if not alive[gi]:
                    continue
                frac = done[gi] / float(totals[gi])
                if best is None or frac < best[0]:
                    best = (frac, gi)
            gi = best[1]
            try:
                wv = next(gens[gi])
                done[gi] += 1.0 if wv is None else wv
            except StopIteration:
                alive[gi] = False

    def count_steps(genf, c):
        return None

    marks = []

    def mark(name):
        marks.append((name, {e: len(S.ops[e]) for e in S.ENGS}))

    for c in range(NCH):
        mark("norm%d" % c)
        emit_norm(c)
        mark("proj1_%d" % c)
        emit_proj1(c)
        mark("X%d" % c)
        na = 4 * c + 4
        n_b = 8 * na + 4
        npair = 1 if c == 0 else 2
        n_ib = npair * (16 * W_IDX + 2 + niter * W_BIS) + sum(((4 * c + j) // 8 + 1) for j in range(4)) + (2 if c == 0 else 0)

        def chain(c=c):
            yield from gen_proj2(c)
            yield from gen_attnB(c)

        run_interleaved([gen_index_bisect(c), chain()], [n_ib, n_b + 32])
        S.alias(["mixed"], [("qTi", f8) for f8 in range(8)])
        mark("A%d" % c)
        wo_pref[c] = [load_slab([(0, 0, 512)], src=w_out), load_slab([(0, 512, 512)], src=w_out)]
        for _ in gen_attnA(c):
            pass
        mark("fin%d" % c)
        emit_finish(c)
    mark("end")
    nc._marks = marks

    S.final_wait_ops = out_ops
    S.run()
    return nc


_CACHE = {}


def _host_inputs(x_b, norm_gain, w_in, q_norm_gain, k_norm_gain, rel_bias, w_out, consts):
    m = dict(consts)
    m["x"] = np.ascontiguousarray(x_b, dtype=np.float32)
    m["w_in"] = np.ascontiguousarray(w_in[0], dtype=np.float32)
    m["w_out"] = np.ascontiguousarray(w_out[0], dtype=np.float32)
    m["gain"] = np.ascontiguousarray(norm_gain[0].reshape(8, 128).T, dtype=np.float32)
    m["gq"] = np.ascontiguousarray(np.tile(q_norm_gain[0], 2).reshape(128, 1), dtype=np.float32)
    m["gk"] = np.ascontiguousarray(np.tile(k_norm_gain[0], 2).reshape(128, 1), dtype=np.float32)
    m["b31"] = np.ascontiguousarray(np.broadcast_to(rel_bias[31][None, :], (128, 8)), dtype=np.float32)
    m["relb"] = np.ascontiguousarray(np.concatenate([rel_bias, np.ones((1, 8), np.float32)], axis=0),
                                     dtype=np.float32)
    return m


def kernel(x, norm_gain, w_in, q_norm_gain, k_norm_gain, rel_bias, w_out):
    x = np.asarray(x)
    B, L, _ = x.shape
    key = ("nc", L)
    consts = _consts()
    nc = build(L=L)
    in_maps = [_host_inputs(x[b], np.asarray(norm_gain), np.asarray(w_in), np.asarray(q_norm_gain),
                            np.asarray(k_norm_gain), np.asarray(rel_bias), np.asarray(w_out), consts)
               for b in range(B)]
    res = run_bass_kernel_spmd(nc, in_maps, core_ids=list(range(B)))
    return np.stack([np.asarray(r["out"]) for r in res.results], axis=0).astype(np.float32)
```

```python
import math
import numpy as np
import ml_dtypes
import concourse.bass as bass
import concourse.mybir as mybir
from concourse.bass_utils import run_bass_kernel_spmd

F32 = mybir.dt.float32
BF16 = mybir.dt.bfloat16
AF = mybir.ActivationFunctionType
ALU = mybir.AluOpType
AX = mybir.AxisListType

D_MODEL = 1024
D_IN = 5200
IDX_SCALE = (16 * 64) ** -0.5
RMS_EPS = 1e-6
NEG_BIG = -30000.0


class _Op:
    __slots__ = ("eng", "fn", "deps", "tok_sem", "tok_val", "needed", "is_dma", "idx")

    def __init__(self, eng, fn, is_dma, idx):
        self.eng = eng
        self.fn = fn
        self.deps = []
        self.tok_sem = None
        self.tok_val = None
        self.needed = False
        self.is_dma = is_dma
        self.idx = idx


class Sched:
    ENGS = ("pe", "act", "dve", "pool", "sp")

    def __init__(self, nc):
        self.nc = nc
        self.ops = {e: [] for e in self.ENGS}
        self.last_w = {}
        self.readers = {}
        self.final_wait_ops = []

    def alias(self, dst_keys, src_keys):
        acc = []
        for k in src_keys:
            lw = self.last_w.get(k)
            if lw is not None:
                acc.append(lw)
            acc.extend(self.readers.get(k, ()))
        for k in dst_keys:
            self.readers.setdefault(k, [])
            self.readers[k] = list(self.readers[k]) + acc

    def _add(self, eng, fn, reads, writes, is_dma=False, dma_sem=None):
        op = _Op(eng, fn, is_dma, len(self.ops[eng]))
        excl = [k for k in reads if isinstance(k, str) and len(k) == 2 and k[0] == "B" and k[1].isdigit()]
        if excl:
            reads = [k for k in reads if k not in excl]
            writes = list(writes) + excl
        cand = []
        for k in reads:
            lw = self.last_w.get(k)
            if lw is not None:
                cand.append((lw, True))
        for k in writes:
            lw = self.last_w.get(k)
            if lw is not None:
                cand.append((lw, False))
            for r in self.readers.get(k, ()):
                cand.append((r, False))
        best = {}
        for d, raw in cand:
            if d is op:
                continue
            if (not d.is_dma) and (not is_dma) and d.eng == eng:
                if (not raw) or eng == "pe":
                    continue
            if d.is_dma:
                key = ("dma", d.tok_sem)
            else:
                key = ("eng", d.eng)
            cur = best.get(key)
            if cur is None or d.idx > cur.idx:
                best[key] = d
        op.deps = list(best.values())
        for k in writes:
            self.last_w[k] = op
            self.readers[k] = []
        for k in reads:
            self.readers.setdefault(k, []).append(op)
        if is_dma:
            op.tok_sem = dma_sem
        self.ops[eng].append(op)
        return op

    def pe(self, fn, reads=(), writes=()):
        return self._add("pe", fn, reads, writes)

    def act(self, fn, reads=(), writes=()):
        return self._add("act", fn, reads, writes)

    def dve(self, fn, reads=(), writes=()):
        return self._add("dve", fn, reads, writes)

    def pool(self, fn, reads=(), writes=()):
        return self._add("pool", fn, reads, writes)

    def dma(self, fn, reads=(), writes=(), sem="d0", queue="sp"):
        return self._add(queue, fn, reads, writes, is_dma=True, dma_sem=sem)

    def finalize(self):
        for e in self.ENGS:
            for op in self.ops[e]:
                for d in op.deps:
                    d.needed = True
        for op in self.final_wait_ops:
            op.needed = True
        cnt = {e: 0 for e in self.ENGS}
        dcnt = {}
        for e in self.ENGS:
            for op in self.ops[e]:
                if op.is_dma:
                    dcnt[op.tok_sem] = dcnt.get(op.tok_sem, 0) + 16
                    op.tok_val = dcnt[op.tok_sem]
                elif op.needed:
                    cnt[e] += 1
                    op.tok_sem = "E_" + e
                    op.tok_val = cnt[e]
        names = set(dcnt.keys()) | {"E_" + e for e in self.ENGS if cnt[e] > 0}
        return sorted(names)

    def emit(self, engine_obj, eng, sems):
        waited = {}
        for op in self.ops[eng]:
            for d in op.deps:
                s, v = d.tok_sem, d.tok_val
                if waited.get(s, 0) >= v:
                    continue
                engine_obj.wait_ge(sems[s], v)
                waited[s] = v
            ins = op.fn(engine_obj)
            if op.is_dma:
                ins.then_inc(sems[op.tok_sem], 16)
            elif op.needed:
                ins.then_inc(sems[op.tok_sem], 1)
        if eng == "sp":
            for op in self.final_wait_ops:
                s, v = op.tok_sem, op.tok_val
                if waited.get(s, 0) >= v:
                    continue
                engine_obj.wait_ge(sems[s], v)
                waited[s] = v

    def run(self):
        nc = self.nc
        names = self.finalize()
        sems = {n: nc.alloc_semaphore("s_" + n) for n in names}
        sch = self
        with nc.Block() as block:
            @block.sync
            def _(e):
                sch.emit(e, "sp", sems)

            @block.scalar
            def _(e):
                sch.emit(e, "act", sems)

            @block.vector
            def _(e):
                sch.emit(e, "dve", sems)

            @block.tensor
            def _(e):
                sch.emit(e, "pe", sems)

            @block.gpsimd
            def _(e):
                sch.emit(e, "pool", sems)


def _t5_bucket_np(d):
    d = np.maximum(d, 0).astype(np.int64)
    d_f = np.maximum(d, 1).astype(np.float32)
    large = 16 + (np.log(d_f / np.float32(16)) / np.float32(math.log(128 / 16))
                  * np.float32(16)).astype(np.int32)
    large = np.minimum(large, 31)
    return np.where(d < 16, d, large)


def _consts():
    bf = ml_dtypes.bfloat16
    c = {}
    c["ident"] = np.eye(128, dtype=np.float32).astype(bf)
    j = np.arange(128)[:, None]
    s = np.arange(128)[None, :]
    c["negtri"] = np.where(j >= s, -1.0, 0.0).astype(np.float32).astype(bf)
    c["strict"] = np.where(j < s, 1.0, 0.0).astype(np.float32).astype(bf)
    c["blockones"] = np.where((j // 64) == (s // 64), 1.0 / 64, 0.0).astype(np.float32).astype(bf)
    c["onescol"] = np.ones((128, 1), np.float32).astype(bf)
    oh = np.zeros((33, 384), np.float32)
    for i in range(384):
        d = i - 128
        if d < 0:
            oh[32, i] = NEG_BIG
        else:
            b = int(_t5_bucket_np(np.array([d]))[0])
            oh[b, i] += 1.0
            oh[31, i] -= 1.0
    c["onehot"] = oh
    return c


def build(L=2048, niter=22, dbg=False):
    NT = L // 128
    NCH = L // 512
    TOPK = min(256, L // 4)
    nc = bass.Bass("TRN2", target_bir_lowering=False, dynamic_dma_scratch_size=8192)

    def din(name, shape, dt=F32):
        return nc.dram_tensor(name, list(shape), dt, kind="ExternalInput").ap()

    x = din("x", [L, D_MODEL])
    w_in = din("w_in", [D_MODEL, D_IN])
    w_out = din("w_out", [D_MODEL, D_MODEL])
    gain_d = din("gain", [128, 8])
    gq_d = din("gq", [128, 1])
    gk_d = din("gk", [128, 1])
    b31_d = din("b31", [128, 8])
    relb_d = din("relb", [33, 8])
    onehot_d = din("onehot", [33, 384])
    ident_d = din("ident", [128, 128], BF16)
    negtri_d = din("negtri", [128, 128], BF16)
    strict_d = din("strict", [128, 128], BF16)
    blockones_d = din("blockones", [128, 128], BF16)
    onescol_d = din("onescol", [128, 1], BF16)
    out = nc.dram_tensor("out", [L, D_MODEL], F32, kind="ExternalOutput").ap()
    scr1_t = nc.dram_tensor("scr1", [8, 384], F32, kind="Internal")
    scr2_t = nc.dram_tensor("scr2", [128, 8 * 384], F32, kind="Internal")

    def sb(name, shape, dt):
        return nc.alloc_sbuf_tensor(name, list(shape), dt)

    gain = sb("gain_s", [128, 8], F32)
    gq = sb("gq_s", [128, 1], F32)
    gk = sb("gk_s", [128, 1], F32)
    b31 = sb("b31_s", [128, 8], F32)
    relb = sb("relb_s", [33, 8], F32)
    onehot = sb("onehot_s", [33, 384], F32)
    ident = sb("ident_s", [128, 128], BF16)
    negtri = sb("negtri_s", [128, 128], BF16)
    strict = sb("strict_s", [128, 128], BF16)
    blockones = sb("blockones_s", [128, 128], BF16)
    onescol = sb("onescol_s", [128, 1], BF16)
    fb = sb("fb_s", [8, 384], F32)
    btab = sb("btab", [128, 8, 2, 128], F32)
    kTa = sb("kTa", [128, 4, L], BF16)
    kTb = sb("kTb", [128, 4, L], BF16)
    kTi = sb("kTi", [128, L], BF16)
    Va = sb("Va", [128, NT, 8, 65], BF16)
    Vb = sb("Vb", [128, NT, 8, 64], BF16)
    hTc = sb("hTc", [128, 8, 512], BF16)
    qTa = sb("qTa", [128, 4, 512], BF16)
    qTb = sb("qTb", [128, 4, 512], BF16)
    qTi_raw = sb("qTi", [128, 4096], BF16)
    qTi = qTi_raw[:].rearrange("p (c t) -> p c t", c=8)
    mixed = qTi_raw[:].rearrange("p (j f) -> p j f", j=4)
    sg = sb("sg", [128, 4, 1024], BF16)
    ws = sb("ws", [128, 4, 16], F32)
    maskT = sb("maskT", [128, NT, 512], BF16)
    slab = [sb("slab0", [128, 8, 512], BF16), sb("slab1", [128, 8, 512], BF16)]
    score = [sb("score0", [128, L], F32), sb("score1", [128, L], F32)]
    accB_t = sb("accB", [128, 2048], F32)
    accB = accB_t[:].rearrange("p (j h d) -> p j h d", j=4, h=8)
    maskrow = [sb("maskrow0", [128, L], BF16), sb("maskrow1", [128, L], BF16)]
    xt = [sb("xt0", [128, 1024], F32), sb("xt1", [128, 1024], F32)]
    xs = [sb("xs0", [128, 1024], BF16), sb("xs1", [128, 1024], BF16)]
    junk = sb("junk", [128, 1024], BF16)
    NP32 = 8
    NP16 = 8
    p32 = [sb("p32_%d" % i, [128, 512], F32) for i in range(NP32)]
    p16 = [sb("p16_%d" % i, [128, 512], BF16) for i in range(NP16)]
    small = sb("small", [128, 64], F32)
    bis = sb("bis", [128, 2, 8 + 32], F32)
    halves = sb("halves", [128, 32], F32)
    rden = sb("rden", [128, 2, 4], F32)
    gsb = sb("gsb", [128, 2, 4], F32)

    banks = [nc.alloc_psum_tensor("B%d" % i, [128, 512], F32) for i in range(8)]
    B0bf = banks[0][:].bitcast(BF16)

    S = Sched(nc)
    cnt32 = [0]
    cnt16 = [0]

    def get32():
        i = 3 + (cnt32[0] % 3)
        cnt32[0] += 1
        return p32[i], ("p32", i)

    def get16():
        i = cnt16[0] % 4
        cnt16[0] += 1
        return p16[i], ("p16", i)

    def cload(dst, src, key):
        S.dma(lambda e: e.dma_start(out=dst[:], in_=src), writes=[key], sem="c_" + key)

    cload(gain, gain_d, "gain")
    cload(gq, gq_d, "gq")
    cload(gk, gk_d, "gk")
    cload(b31, b31_d, "b31")
    cload(relb, relb_d, "relb")
    cload(onehot, onehot_d, "onehot")
    cload(ident, ident_d, "ident")
    cload(negtri, negtri_d, "negtri")
    cload(strict, strict_d, "strict")
    cload(blockones, blockones_d, "blockones")
    cload(onescol, onescol_d, "onescol")
    for k in range(niter + 1):
        S.dve(lambda e, k=k: e.memset(halves[:, k:k + 1], 2.0 ** -(k + 1)), writes=[("halves", k)])
    S.pool(lambda e: e.memset(Va[:, :, :, 64:65], 1.0), writes=["Va_ones"])

    S.pe(lambda e: e.matmul(out=banks[3][0:8, 0:384], lhsT=relb[:, :], rhs=onehot[:, :], start=True, stop=True),
         reads=["relb", "onehot"], writes=["B3"])
    S.act(lambda e: e.activation(out=fb[:], in_=banks[3][0:8, 0:384], func=AF.Copy), reads=["B3"], writes=["fb"])
    S.dma(lambda e: e.dma_start(out=scr1_t.ap(), in_=fb[:]), reads=["fb"], writes=["scr1"], sem="c_scr1")
    scr2_v = scr2_t.ap().rearrange("p (h i) -> p h i", h=8)
    S.dma(lambda e: e.dma_start(out=scr2_v, in_=scr1_t.ap().partition_broadcast(128)),
          reads=["scr1"], writes=["scr2"], sem="c_scr2")
    for r in range(2):
        src = bass.AP(tensor=scr2_t, offset=128 * (r + 1), ap=[[8 * 384 - 1, 128], [384, 8], [1, 128]])
        S.dma(lambda e, r=r, src=src: e.dma_start(out=btab[:, :, r, :], in_=src),
              reads=["scr2"], writes=[("btab", r)], sem="c_btab%d" % r)

    slab_ctr = [0]

    def load_slab(pieces, src=None):
        if src is None:
            src = w_in
        i = slab_ctr[0] % 2
        slab_ctr[0] += 1
        key = ("slab", i)
        for (d0, s0, n) in pieces:
            S.dma(lambda e, i=i, d0=d0, s0=s0, n=n, src=src: e.dma_start(
                out=slab[i][:, :, d0:d0 + n],
                in_=src[:, s0:s0 + n].rearrange("(c p) n -> p c n", p=128)),
                writes=[key], sem="w%d" % i, queue="pool")
        return slab[i], key

    pbank_ctr = [0]

    def proj_bank():
        i = 1 + (pbank_ctr[0] % 2)
        pbank_ctr[0] += 1
        return banks[i], "B%d" % i

    def fm_matmuls(sl, skey, f, ncols_feat=128, col0=None, bank=None):
        bk, bkey = proj_bank() if bank is None else bank
        c0 = f * 128 if col0 is None else col0
        for c8 in range(8):
            S.pe(lambda e, c8=c8, bk=bk, sl=sl, c0=c0: e.matmul(
                out=bk[0:ncols_feat, :], lhsT=sl[:, c8, c0:c0 + ncols_feat], rhs=hTc[:, c8, :],
                start=(c8 == 0), stop=(c8 == 7)),
                reads=[skey, ("hTc", c8)], writes=[bkey])
        return bk, bkey

    def tm_matmuls(sl, skey, j, ncols=512, col0=0, bank=None):
        bk, bkey = proj_bank() if bank is None else bank
        for c8 in range(8):
            S.pe(lambda e, c8=c8, bk=bk, sl=sl: e.matmul(
                out=bk[:, 0:ncols], lhsT=hTc[:, c8, j * 128:(j + 1) * 128], rhs=sl[:, c8, col0:col0 + ncols],
                start=(c8 == 0), stop=(c8 == 7)),
                reads=[skey, ("hTc", c8)], writes=[bkey])
        return bk, bkey

    def fm_g(sl, skey, f, bank):
        bk, bkey = bank
        c0 = f * 128
        for c8 in range(8):
            S.pe(lambda e, c8=c8, bk=bk, sl=sl, c0=c0: e.matmul(
                out=bk[:, :], lhsT=sl[:, c8, c0:c0 + 128], rhs=hTc[:, c8, :],
                start=(c8 == 0), stop=(c8 == 7)),
                reads=[skey, ("hTc", c8)], writes=[bkey])
            if c8 % 2 == 1 and c8 < 7:
                yield 0.25
        return bk, bkey

    def tm_g(sl, skey, j, bank):
        bk, bkey = bank
        for c8 in range(8):
            S.pe(lambda e, c8=c8, bk=bk, sl=sl: e.matmul(
                out=bk[:, 0:512], lhsT=hTc[:, c8, j * 128:(j + 1) * 128], rhs=sl[:, c8, 0:512],
                start=(c8 == 0), stop=(c8 == 7)),
                reads=[skey, ("hTc", c8)], writes=[bkey])
            if c8 % 2 == 1 and c8 < 7:
                yield 0.25
        return bk, bkey

    def qknorm(bk, bkey, gvec, gkey, dst_ap, dst_key):
        sq, sqk = get16()
        S.act(lambda e: e.activation(out=sq[:], in_=bk[:], func=AF.Square), reads=[bkey], writes=[sqk])
        S.pe(lambda e: e.matmul(out=banks[3][:], lhsT=blockones[:], rhs=sq[:], start=True, stop=True),
             reads=[sqk, "blockones"], writes=["B3"])
        lt, ltk = get32()
        S.act(lambda e: e.activation(out=lt[:], in_=banks[3][:], func=AF.Ln, bias=RMS_EPS, scale=1.0),
              reads=["B3"], writes=[ltk])
        S.act(lambda e: e.activation(out=lt[:], in_=lt[:], func=AF.Exp, scale=-0.5), reads=[ltk], writes=[ltk])
        S.dve(lambda e: e.scalar_tensor_tensor(out=dst_ap, in0=bk[:], scalar=gvec[:, 0:1], in1=lt[:],
                                               op0=ALU.mult, op1=ALU.mult),
              reads=[bkey, ltk, gkey], writes=[dst_key])

    evac_ctr = [0]

    def evac_copy(dst_ap, src_ap, reads, writes, scale=None):
        i = evac_ctr[0]
        evac_ctr[0] += 1
        if i % 2 == 0:
            if scale is None:
                S.act(lambda e: e.activation(out=dst_ap, in_=src_ap, func=AF.Copy), reads=reads, writes=writes)
            else:
                S.act(lambda e: e.activation(out=dst_ap, in_=src_ap, func=AF.Copy, scale=scale),
                      reads=reads, writes=writes)
        else:
            if scale is None:
                S.dve(lambda e: e.tensor_copy(out=dst_ap, in_=src_ap), reads=reads, writes=writes)
            else:
                S.dve(lambda e: e.tensor_scalar(out=dst_ap, in0=src_ap, scalar1=scale, scalar2=None, op0=ALU.mult),
                      reads=reads, writes=writes)

    out_ops = []
    wo_pref = {}

    R32 = p32[0:3]
    E32 = p32[3:5]
    SPt = p16[0:2]
    APt = p16[2:4]
    Et = p16[4:6]
    EMt = p16[6:8]

    def emit_norm(c):
        for j in range(4):
            i = 4 * c + j
            sl_ = i % 2
            S.dma(lambda e, i=i, sl_=sl_: e.dma_start(out=xt[sl_][:], in_=x[i * 128:(i + 1) * 128, :]),
                  writes=[("xt", sl_)], sem="x%d" % sl_)
            S.act(lambda e, sl_=sl_: e.activation(out=junk[:], in_=xt[sl_][:], func=AF.Square,
                                                  accum_out=small[:, sl_:sl_ + 1]),
                  reads=[("xt", sl_)], writes=[("ss", sl_), "junk"])
            S.act(lambda e, sl_=sl_: e.activation(out=small[:, 2 + sl_:3 + sl_], in_=small[:, sl_:sl_ + 1],
                                                  func=AF.Ln, scale=1.0 / D_MODEL, bias=RMS_EPS),
                  reads=[("ss", sl_)], writes=[("lnv", sl_)])
            S.act(lambda e, sl_=sl_: e.activation(out=small[:, 4 + sl_:5 + sl_], in_=small[:, 2 + sl_:3 + sl_],
                                                  func=AF.Exp, scale=-0.5),
                  reads=[("lnv", sl_)], writes=[("rstd", sl_)])
            S.dve(lambda e, sl_=sl_: e.tensor_scalar(out=xs[sl_][:], in0=xt[sl_][:], scalar1=small[:, 4 + sl_:5 + sl_],
                                                     scalar2=None, op0=ALU.mult),
                  reads=[("xt", sl_), ("rstd", sl_)], writes=[("xs", sl_)])
            for c8 in range(8):
                S.pe(lambda e, c8=c8, sl_=sl_: e.transpose(out=B0bf[:, c8 * 128:(c8 + 1) * 128],
                                                           in_=xs[sl_][:, c8 * 128:(c8 + 1) * 128], identity=ident[:]),
                     reads=[("xs", sl_), "ident"], writes=["B0"])
            for c8 in range(8):
                dst = hTc[:, c8, j * 128:(j + 1) * 128]
                srcp = B0bf[:, c8 * 128:(c8 + 1) * 128]
                if j % 2 == 0:
                    S.dve(lambda e, dst=dst, srcp=srcp, c8=c8: e.tensor_scalar(
                        out=dst, in0=srcp, scalar1=gain[:, c8:c8 + 1], scalar2=None, op0=ALU.mult),
                        reads=["B0", "gain"], writes=[("hTc", c8)])
                else:
                    S.act(lambda e, dst=dst, srcp=srcp, c8=c8: e.activation(
                        out=dst, in_=srcp, func=AF.Copy, scale=gain[:, c8:c8 + 1]),
                        reads=["B0", "gain"], writes=[("hTc", c8)])

    def emit_proj1(c):
        tok0 = c * 512
        sl, sk = load_slab([(0, 3072, 64), (64, 3072, 64), (128, 3136, 16)])
        bk, bkey = fm_matmuls(sl, sk, 0)
        evac_copy(kTi[:, tok0:tok0 + 512], bk[:], [bkey], [("kTi", c)])
        for j in range(4):
            bk, bkey = tm_matmuls(sl, sk, j, ncols=16, col0=128)
            S.dve(lambda e, bk=bk, j=j: e.tensor_scalar(out=ws[:, j, :], in0=bk[:, 0:16], scalar1=IDX_SCALE,
                                                        scalar2=None, op0=ALU.mult),
                  reads=[bkey], writes=[("ws", j)])
        S.alias([("qTi", f8) for f8 in range(8)], ["mixed"])
        for half in range(2):
            sl, sk = load_slab([(0, 2048 + 512 * half, 512)])
            for f in range(4):
                bk, bkey = fm_matmuls(sl, sk, f)
                evac_copy(qTi[:, 4 * half + f, :], bk[:], [bkey], [("qTi", 4 * half + f)])

    def gen_proj2(c):
        tok0 = c * 512
        pb2 = [0]

        def bank2():
            i = (6, 7, 4, 5)[pb2[0] % 4]
            pb2[0] += 1
            return banks[i], "B%d" % i

        sl, sk = load_slab([(0, 512, 512)])
        for f in range(4):
            bk, bkey = yield from fm_g(sl, sk, f, bank2())
            qknorm(bk, bkey, gk, "gk", kTa[:, f, tok0:tok0 + 512], ("kTa", f, c))
            yield 0.25
        sl, sk = load_slab([(0, 1024, 512)])
        for j in range(4):
            i = 4 * c + j
            bk, bkey = yield from tm_g(sl, sk, j, bank2())
            evac_copy(Va[:, i, :, 0:64], bk[:].rearrange("p (h d) -> p h d", h=8), [bkey, "Va_ones"], [("Va", i)])
            yield 0.25
        sl, sk = load_slab([(0, 3664, 512)])
        for f in range(4):
            bk, bkey = yield from fm_g(sl, sk, f, bank2())
            evac_copy(kTb[:, f, tok0:tok0 + 512], bk[:], [bkey], [("kTb", f, c)], scale=0.125)
            yield 0.25
        sl, sk = load_slab([(0, 4176, 512)])
        for j in range(4):
            i = 4 * c + j
            bk, bkey = yield from tm_g(sl, sk, j, bank2())
            evac_copy(Vb[:, i, :, :], bk[:].rearrange("p (h d) -> p h d", h=8), [bkey], [("Vb", i)])
            yield 0.25
        sl, sk = load_slab([(0, 0, 512)])
        for f in range(4):
            bk, bkey = yield from fm_g(sl, sk, f, bank2())
            qknorm(bk, bkey, gq, "gq", qTa[:, f, :], ("qTa", f))
            yield 0.25
        sl, sk = load_slab([(0, 3152, 512)])
        for f in range(4):
            bk, bkey = yield from fm_g(sl, sk, f, bank2())
            evac_copy(qTb[:, f, :], bk[:], [bkey], [("qTb", f)])
            yield 0.25
        for half, col in ((0, 1536), (1, 4688)):
            sl, sk = load_slab([(0, col, 512)])
            for j in range(4):
                bk, bkey = yield from tm_g(sl, sk, j, bank2())
                S.act(lambda e, bk=bk, j=j, half=half: e.activation(
                    out=sg[:, j, half * 512:(half + 1) * 512], in_=bk[:], func=AF.Silu),
                    reads=[bkey], writes=[("sg", j, half)])
                yield 0.25

    W_IDX = 1.0
    W_BIS = 2.2

    def gen_index_bisect(c):
        dctr = [0]
        rctr = [0]

        def indexer_tiles(js):
          for hh in range(16):
            f8 = hh // 2
            base = 64 * (hh % 2)
            for scn in range(c + 1):
              for j in js:
                sc_t = score[j % 2]
                sckey = ("score", j % 2)
                if True:
                    w = 512 if scn < c else (j + 1) * 128
                    bi = (1, 2, 0)[dctr[0] % 3]
                    dctr[0] += 1
                    bk = banks[bi]
                    bkey = "B%d" % bi
                    S.pe(lambda e, bk=bk, f8=f8, base=base, scn=scn, w=w, j=j: e.matmul(
                        out=bk[:, 0:w], lhsT=qTi[base:base + 64, f8, j * 128:(j + 1) * 128],
                        rhs=kTi[base:base + 64, scn * 512:scn * 512 + w], start=True, stop=True),
                        reads=[("qTi", f8), ("kTi", scn)], writes=[bkey])
                    ri = (0, 1, 2, 6, 7)[rctr[0] % 5]
                    rctr[0] += 1
                    r, rk = p32[ri], ("p32", ri)
                    S.act(lambda e, bk=bk, r=r, w=w: e.activation(out=r[:, 0:w], in_=bk[:, 0:w], func=AF.Relu),
                          reads=[bkey], writes=[rk])
                    dst = sc_t[:, scn * 512:scn * 512 + w]
                    if hh == 0:
                        S.dve(lambda e, dst=dst, r=r, w=w, j=j, hh=hh: e.tensor_scalar(
                            out=dst, in0=r[:, 0:w], scalar1=ws[:, j, hh:hh + 1], scalar2=None, op0=ALU.mult),
                            reads=[rk, ("ws", j)], writes=[sckey])
                    else:
                        S.dve(lambda e, dst=dst, r=r, w=w, j=j, hh=hh: e.scalar_tensor_tensor(
                            out=dst, in0=r[:, 0:w], scalar=ws[:, j, hh:hh + 1], in1=dst,
                            op0=ALU.mult, op1=ALU.add),
                            reads=[rk, ("ws", j), sckey], writes=[sckey])
            yield W_IDX

        def bisect_tiles(js):
            bs = [j % 2 for j in js]
            b0, b1 = min(bs), max(bs) + 1
            bkeys = [("bis", b) for b in bs]
            for j in js:
                i = 4 * c + j
                Si = 128 * (i + 1)
                b = j % 2
                sc_t = score[b]
                sckey = ("score", b)
                bk_ = ("bis", b)
                S.dve(lambda e, sc_t=sc_t, Si=Si, b=b: e.tensor_reduce(out=bis[:, b, 0:1], in_=sc_t[:, 0:Si],
                                                                       axis=AX.X, op=ALU.min),
                      reads=[sckey], writes=[bk_])
                S.pool(lambda e, sc_t=sc_t, Si=Si: e.affine_select(
                    out=sc_t[:, Si - 128:Si], in_=sc_t[:, Si - 128:Si], pattern=[[-1, 128]],
                    compare_op=ALU.is_ge, fill=-3.0e38, base=0, channel_multiplier=1),
                    reads=[sckey, bk_], writes=[sckey])
                S.dve(lambda e, sc_t=sc_t, Si=Si, b=b: e.tensor_reduce(out=bis[:, b, 1:2], in_=sc_t[:, 0:Si],
                                                                       axis=AX.X, op=ALU.max),
                      reads=[sckey, bk_], writes=[bk_])
                yield
            if len(bs) == 2:
                S.dve(lambda e: e.tensor_tensor(out=bis[:, 0, 0:1], in0=bis[:, 0, 0:1], in1=bis[:, 1, 0:1], op=ALU.min),
                      reads=bkeys, writes=[("bis", 0)])
                S.dve(lambda e: e.tensor_tensor(out=bis[:, 0, 1:2], in0=bis[:, 0, 1:2], in1=bis[:, 1, 1:2], op=ALU.max),
                      reads=bkeys, writes=[("bis", 0)])
            S.dve(lambda e: e.tensor_tensor(out=bis[:, b0, 2:3], in0=bis[:, b0, 1:2], in1=bis[:, b0, 0:1],
                                            op=ALU.subtract), reads=bkeys, writes=[("bis", b0)])
            S.dve(lambda e: e.tensor_scalar(out=bis[:, b0, 8:8 + niter + 1], in0=halves[:, 0:niter + 1],
                                            scalar1=bis[:, b0, 2:3], scalar2=None, op0=ALU.mult),
                  reads=bkeys + [("halves", k) for k in range(niter + 1)], writes=[("bis", b0)])
            for b in bs:
                S.dve(lambda e, b=b: e.tensor_tensor(out=bis[:, b, 3:4], in0=bis[:, b0, 0:1], in1=bis[:, b0, 8:9],
                                                     op=ALU.add), reads=bkeys, writes=[("bis", b)])
            yield
            for k in range(niter):
                for j in js:
                    i = 4 * c + j
                    Si = 128 * (i + 1)
                    b = j % 2
                    S.dve(lambda e, sc_t=score[b], Si=Si, b=b, jk=R32[b][:].bitcast(mybir.dt.uint8): e.tensor_scalar(
                        out=jk[:, 0:Si], in0=sc_t[:, 0:Si], scalar1=bis[:, b, 3:4], scalar2=None,
                        op0=ALU.is_ge, op1=ALU.add, accum_out=bis[:, b, 4:5]),
                        reads=[("score", b), ("bis", b)], writes=[("p32", b), ("bis", b)])
                S.dve(lambda e, k=k: e.tensor_scalar(
                    out=bis[:, b0:b1, 5], in0=bis[:, b0:b1, 4], scalar1=float(TOPK) - 0.5,
                    scalar2=bis[:, b0, 8 + k:9 + k], op0=(ALU.is_ge if k < niter - 1 else ALU.is_lt),
                    op1=ALU.mult),
                    reads=bkeys, writes=bkeys)
                if k < niter - 1:
                    S.dve(lambda e, k=k: e.scalar_tensor_tensor(
                        out=bis[:, b0:b1, 3], in0=bis[:, b0:b1, 3], scalar=bis[:, b0, 9 + k:10 + k],
                        in1=bis[:, b0:b1, 5], op0=ALU.subtract, op1=ALU.add),
                        reads=bkeys, writes=bkeys)
                else:
                    S.dve(lambda e: e.tensor_tensor(out=bis[:, b0:b1, 6], in0=bis[:, b0:b1, 3],
                                                    in1=bis[:, b0:b1, 5], op=ALU.subtract),
                          reads=bkeys, writes=bkeys)
                yield W_BIS
            for j in js:
                i = 4 * c + j
                Si = 128 * (i + 1)
                b = j % 2
                sc_t = score[b]
                sckey = ("score", b)
                bk_ = ("bis", b)
                mr = maskrow[b]
                mrk = ("maskrow", b)
                S.dve(lambda e, sc_t=sc_t, Si=Si, b=b, mr=mr: e.tensor_scalar(
                    out=mr[:, 0:Si], in0=sc_t[:, 0:Si], scalar1=bis[:, b, 6:7], scalar2=None, op0=ALU.is_ge),
                    reads=[sckey, bk_], writes=[mrk])
                a0 = 0
                while a0 <= i:
                    n = min(8, i + 1 - a0)
                    for q in range(n):
                        a = a0 + q
                        S.pe(lambda e, mr=mr, a=a, q=q: e.transpose(out=B0bf[:, q * 128:(q + 1) * 128],
                                                                    in_=mr[:, a * 128:(a + 1) * 128], identity=ident[:]),
                             reads=[mrk, "ident"], writes=["B0"])
                    dst = maskT[:, a0:a0 + n, j * 128:(j + 1) * 128]
                    srcp = B0bf[:, 0:n * 128].rearrange("p (n t) -> p n t", n=n)
                    S.act(lambda e, dst=dst, srcp=srcp: e.activation(out=dst, in_=srcp, func=AF.Copy),
                          reads=["B0"], writes=[("maskT", j)])
                    a0 += n
                    yield

        for jp in range(2):
            js = []
            for j in (2 * jp, 2 * jp + 1):
                i = 4 * c + j
                if 128 * (i + 1) <= TOPK:
                    S.pool(lambda e, i=i, j=j: e.memset(maskT[:, 0:i + 1, j * 128:(j + 1) * 128], 1.0),
                           writes=[("maskT", j)])
                    yield
                else:
                    js.append(j)
            if js:
                yield from indexer_tiles(js)
                yield from bisect_tiles(js)

    def gen_attnB(c):
        na = 4 * c + 4
        tiles = [(h, a) for h in range(8) for a in range(na)]
        NTL = len(tiles)
        zb, zkey = banks[4], "B4"
        lb, lkey = banks[5], "B5"
        st = {}

        def info(n):
            h, a = tiles[n]
            f = h // 2
            base = 64 * (h % 2)
            i0 = max(0, a - 4 * c)
            w = 512 - i0 * 128
            diag = a >= 4 * c
            kl = kTb[base:base + 64, f, a * 128:(a + 1) * 128]
            qr = qTb[base:base + 64, f, i0 * 128:512]
            spi = (0, 1, 4)[n % 3]
            SP, SPk = p16[spi], ("p16", spi)
            Ap, Apk = APt[n % 2], ("p16", 2 + n % 2)
            pbi = 6 + (n % 2)
            return h, a, f, i0, w, diag, kl, qr, SP, SPk, Ap, Apk, banks[pbi], "B%d" % pbi, n % 2

        def stage_a(n):
            h, a, f, i0, w, diag, kl, qr, SP, SPk, Ap, Apk, pbk, pkey, gb = info(n)
            S.pe(lambda e: e.matmul(out=zb[:, 0:w], lhsT=kl, rhs=qr, start=True, stop=True),
                 reads=[("kTb", f, a // 4), ("qTb", f)], writes=[zkey])
            e1, e1k = E32[n % 2], ("p32", 3 + n % 2)
            S.act(lambda e: e.activation(out=e1[:, 0:w], in_=zb[:, 0:w], func=AF.Exp), reads=[zkey], writes=[e1k])
            S.act(lambda e: e.activation(out=SP[:, 0:w], in_=e1[:, 0:w], func=AF.Ln, bias=1.0, scale=1.0),
                  reads=[e1k], writes=[SPk])
            if diag:
                S.pool(lambda e: e.tensor_tensor(out=SP[:, 0:128], in0=SP[:, 0:128], in1=strict[:], op=ALU.mult),
                       reads=[SPk, "strict"], writes=[SPk])

        def stage_b(n):
            h, a, f, i0, w, diag, kl, qr, SP, SPk, Ap, Apk, pbk, pkey, gb = info(n)
            S.pe(lambda e: e.matmul(out=lb[:, 0:w], lhsT=kl, rhs=qr, start=True, stop=False),
                 reads=[("kTb", f, a // 4), ("qTb", f)], writes=[lkey])
            S.pe(lambda e: e.matmul(out=lb[:, 0:w], lhsT=negtri[:], rhs=SP[:, 0:w], start=False, stop=True),
                 reads=[SPk, "negtri"], writes=[lkey])
            if a > 0:
                fcs = True
                for jj in range(i0, 4):
                    blk = (jj - i0) * 128
                    S.pe(lambda e, blk=blk, jj=jj, fcs=fcs: e.matmul(
                        out=banks[3][:, jj:jj + 1], lhsT=SP[:, blk:blk + 128], rhs=onescol[:, 0:1],
                        start=fcs, stop=(jj == 3), skip_group_check=True),
                        reads=[SPk, "onescol"], writes=["B3"])
                    fcs = False
                S.act(lambda e: e.activation(out=gsb[:, gb, i0:4], in_=banks[3][:, i0:4], func=AF.Exp, scale=-1.0),
                      reads=["B3"], writes=[("gsb", gb)])
            S.act(lambda e: e.activation(out=Ap[:, 0:w], in_=lb[:, 0:w], func=AF.Exp), reads=[lkey], writes=[Apk])
            if diag:
                S.pool(lambda e: e.tensor_tensor(out=Ap[:, 0:128], in0=Ap[:, 0:128], in1=strict[:], op=ALU.mult),
                       reads=[Apk, "strict"], writes=[Apk])

        def stage_c(n):
            h, a, f, i0, w, diag, kl, qr, SP, SPk, Ap, Apk, pbk, pkey, gb = info(n)
            fp = True
            for jj in range(i0, 4):
                blk = (jj - i0) * 128
                S.pe(lambda e, jj=jj, blk=blk, fp=fp: e.matmul(
                    out=pbk[:, jj * 64:(jj + 1) * 64], lhsT=Ap[:, blk:blk + 128], rhs=Vb[:, a, h, :],
                    start=fp, stop=(jj == 3), skip_group_check=True),
                    reads=[Apk, ("Vb", a)], writes=[pkey])
                fp = False

        def stage_d1(n):
            h, a, f, i0, w, diag, kl, qr, SP, SPk, Ap, Apk, pbk, pkey, gb = info(n)
            if a == 0:
                return
            nb = 4 - i0
            accv = accB[:, i0:4, h, :]
            gv = gsb[:, gb, i0:4].unsqueeze(2).broadcast_to([128, nb, 64])
            S.dve(lambda e: e.tensor_tensor(out=accv, in0=accv, in1=gv, op=ALU.mult),
                  reads=[("gsb", gb), ("accB", h)], writes=[("accB", h)])

        def stage_d2(n):
            h, a, f, i0, w, diag, kl, qr, SP, SPk, Ap, Apk, pbk, pkey, gb = info(n)
            nb = 4 - i0
            accv = accB[:, i0:4, h, :]
            pv = pbk[:, i0 * 64:256].rearrange("p (j d) -> p j d", j=nb)
            if a == 0:
                S.dve(lambda e: e.tensor_copy(out=accv, in_=pv), reads=[pkey], writes=[("accB", h)])
            else:
                S.dve(lambda e: e.tensor_tensor(out=accv, in0=accv, in1=pv, op=ALU.add),
                      reads=[pkey, ("accB", h)], writes=[("accB", h)])

        for n in range(-3, NTL + 1):
            if 0 <= n - 1 < NTL:
                stage_d2(n - 1)
            if 0 <= n + 3 < NTL:
                stage_a(n + 3)
            if 0 <= n + 1 < NTL:
                stage_b(n + 1)
            if 0 <= n < NTL:
                stage_d1(n)
                stage_c(n)
            yield

    def gen_attnA(c):
        na = 4 * c + 4
        tiles = [(h, a) for h in range(8) for a in range(na)]
        NTL = len(tiles)

        def info(n):
            h, a = tiles[n]
            f = h // 2
            base = 64 * (h % 2)
            i0 = max(0, a - 4 * c)
            w = 512 - i0 * 128
            sbi = 4 + (n % 2)
            ei = (4, 5, 0)[n % 3]
            E, Ek = p16[ei], ("p16", ei)
            EM, EMk = EMt[n % 2], ("p16", 6 + n % 2)
            return h, a, f, base, i0, w, banks[sbi], "B%d" % sbi, E, Ek, EM, EMk

        def stage_a(n):
            h, a, f, base, i0, w, sbk, sbkey, E, Ek, EM, EMk = info(n)
            S.pe(lambda e: e.matmul(out=sbk[:, 0:w], lhsT=kTa[base:base + 64, f, a * 128:(a + 1) * 128],
                                    rhs=qTa[base:base + 64, f, i0 * 128:512], start=True, stop=True),
                 reads=[("kTa", f, a // 4), ("qTa", f)], writes=[sbkey])
            far0 = None
            nn = 0
            for jj in range(i0, 4):
                d = 4 * c + jj - a
                blk = (jj - i0) * 128
                if d <= 1:
                    tmp, tk = E32[nn % 2], ("p32", 3 + nn % 2)
                    nn += 1
                    S.dve(lambda e, blk=blk, tmp=tmp, d=d: e.scalar_tensor_tensor(
                        out=tmp[:, 0:128], in0=sbk[:, blk:blk + 128], scalar=0.125, in1=btab[:, h, d, :],
                        op0=ALU.mult, op1=ALU.add),
                        reads=[sbkey, ("btab", d)], writes=[tk])
                    S.act(lambda e, blk=blk, tmp=tmp: e.activation(
                        out=E[:, blk:blk + 128], in_=tmp[:, 0:128], func=AF.Exp, bias=b31[:, h:h + 1], scale=1.0),
                        reads=[tk, "b31"], writes=[Ek])
                else:
                    far0 = blk
                    break
            if far0 is not None:
                S.act(lambda e, far0=far0: e.activation(
                    out=E[:, far0:w], in_=sbk[:, far0:w], func=AF.Exp, bias=b31[:, h:h + 1], scale=0.125),
                    reads=[sbkey, "b31"], writes=[Ek])

        def stage_b(n):
            h, a, f, base, i0, w, sbk, sbkey, E, Ek, EM, EMk = info(n)
            S.dve(lambda e: e.tensor_tensor(out=EM[:, 0:w], in0=E[:, 0:w], in1=maskT[:, a, i0 * 128:512], op=ALU.mult),
                  reads=[Ek] + [("maskT", jj) for jj in range(i0, 4)], writes=[EMk])

        def stage_c(n):
            h, a, f, base, i0, w, sbk, sbkey, E, Ek, EM, EMk = info(n)
            obi = 6 + (h % 2)
            ob, obkey = banks[obi], "B%d" % obi
            for jj in range(i0, 4):
                blk = (jj - i0) * 128
                first = (a == 0 and jj == i0)
                last = (a == 4 * c + jj)
                S.pe(lambda e, jj=jj, blk=blk, first=first, last=last: e.matmul(
                    out=ob[:, jj * 65:(jj + 1) * 65], lhsT=EM[:, blk:blk + 128], rhs=Va[:, a, h, :],
                    start=first, stop=last, skip_group_check=True),
                    reads=[EMk, ("Va", a), "Va_ones"], writes=[obkey])
            if a == na - 1:
                rb = h % 2
                S.dve(lambda e: e.reciprocal(
                    out=rden[:, rb, :], in_=ob[:, 0:260].rearrange("p (j d) -> p j d", j=4)[:, :, 64]),
                    reads=[obkey], writes=[("rden", rb)])
                for jj in range(4):
                    S.dve(lambda e, jj=jj: e.scalar_tensor_tensor(
                        out=mixed[:, jj, h * 64:(h + 1) * 64], in0=ob[:, jj * 65:jj * 65 + 64],
                        scalar=rden[:, rb, jj:jj + 1], in1=sg[:, jj, h * 64:(h + 1) * 64],
                        op0=ALU.mult, op1=ALU.mult),
                        reads=[obkey, ("rden", rb), ("sg", jj, 0)], writes=["mixed"])

        for n in range(-3, NTL):
            if 0 <= n + 1 < NTL:
                stage_b(n + 1)
            if 0 <= n + 3 < NTL:
                stage_a(n + 3)
            if 0 <= n < NTL:
                stage_c(n)
            yield

    def emit_finish(c):
        for jj in range(4):
            S.dve(lambda e, jj=jj: e.tensor_tensor(
                out=mixed[:, jj, 512:1024], in0=accB[:, jj, :, :].rearrange("p h d -> p (h d)"),
                in1=sg[:, jj, 512:1024], op=ALU.mult),
                reads=[("accB", h) for h in range(8)] + [("sg", jj, 1)], writes=["mixed"])
        for jj in range(4):
            for c8 in range(8):
                S.pe(lambda e, jj=jj, c8=c8: e.transpose(out=B0bf[:, c8 * 128:(c8 + 1) * 128],
                                                         in_=mixed[:, jj, c8 * 128:(c8 + 1) * 128], identity=ident[:]),
                     reads=["mixed", "ident"], writes=["B0"])
            evac_copy(hTc[:, :, jj * 128:(jj + 1) * 128], B0bf[:, :].rearrange("p (c t) -> p c t", c=8),
                      ["B0"], [("hTc", c8) for c8 in range(8)])
        wo = wo_pref.pop(c)
        for jj in range(4):
            i = 4 * c + jj
            sl_ = i % 2
            S.dma(lambda e, i=i, sl_=sl_: e.dma_start(out=xt[sl_][:], in_=x[i * 128:(i + 1) * 128, :]),
                  writes=[("xt", sl_)], sem="x%d" % sl_)
            for half in range(2):
                sl, sk = wo[half]
                bk, bkey = proj_bank()
                for c8 in range(8):
                    S.pe(lambda e, c8=c8, bk=bk, sl=sl, jj=jj: e.matmul(
                        out=bk[:], lhsT=hTc[:, c8, jj * 128:(jj + 1) * 128], rhs=sl[:, c8, :],
                        start=(c8 == 0), stop=(c8 == 7)),
                        reads=[sk, ("hTc", c8)], writes=[bkey])
                S.dve(lambda e, bk=bk, sl_=sl_, half=half: e.tensor_tensor(
                    out=xt[sl_][:, half * 512:(half + 1) * 512], in0=bk[:], in1=xt[sl_][:, half * 512:(half + 1) * 512],
                    op=ALU.add),
                    reads=[bkey, ("xt", sl_)], writes=[("xt", sl_)])
            o = S.dma(lambda e, i=i, sl_=sl_: e.dma_start(out=out[i * 128:(i + 1) * 128, :], in_=xt[sl_][:]),
                      reads=[("xt", sl_)], writes=[("out", i)], sem="o%d" % sl_)
            out_ops.append(o)

    def run_interleaved(gens, totals):
        done = [0] * len(gens)
        alive = [True] * len(gens)
        while any(alive):
            best = None
            for gi in range(len(gens)):
                if not alive[gi]:
                    continue
                frac = done[gi] / float(totals[gi])
                if best is None or frac < best[0]:
                    best = (frac, gi)
            gi = best[1]
            try:
                wv = next(gens[gi])
                done[gi] += 1.0 if wv is None else wv
            except StopIteration:
                alive[gi] = False

    def count_steps(genf, c):
        return None

    marks = []

    def mark(name):
        marks.append((name, {e: len(S.ops[e]) for e in S.ENGS}))

    for c in range(NCH):
        mark("norm%d" % c)
        emit_norm(c)
        mark("proj1_%d" % c)
        emit_proj1(c)
        mark("X%d" % c)
        na = 4 * c + 4
        n_b = 8 * na + 4
        npair = 1 if c == 0 else 2
        n_ib = npair * (16 * W_IDX + 2 + niter * W_BIS) + sum(((4 * c + j) // 8 + 1) for j in range(4)) + (2 if c == 0 else 0)

        def chain(c=c):
            yield from gen_proj2(c)
            yield from gen_attnB(c)

        run_interleaved([gen_index_bisect(c), chain()], [n_ib, n_b + 32])
        S.alias(["mixed"], [("qTi", f8) for f8 in range(8)])
        mark("A%d" % c)
        wo_pref[c] = [load_slab([(0, 0, 512)], src=w_out), load_slab([(0, 512, 512)], src=w_out)]
        for _ in gen_attnA(c):
            pass
        mark("fin%d" % c)
        emit_finish(c)
    mark("end")
    nc._marks = marks

    S.final_wait_ops = out_ops
    S.run()
    return nc


_CACHE = {}


def _host_inputs(x_b, norm_gain, w_in, q_norm_gain, k_norm_gain, rel_bias, w_out, consts):
    m = dict(consts)
    m["x"] = np.ascontiguousarray(x_b, dtype=np.float32)
    m["w_in"] = np.ascontiguousarray(w_in[0], dtype=np.float32)
    m["w_out"] = np.ascontiguousarray(w_out[0], dtype=np.float32)
    m["gain"] = np.ascontiguousarray(norm_gain[0].reshape(8, 128).T, dtype=np.float32)
    m["gq"] = np.ascontiguousarray(np.tile(q_norm_gain[0], 2).reshape(128, 1), dtype=np.float32)
    m["gk"] = np.ascontiguousarray(np.tile(k_norm_gain[0], 2).reshape(128, 1), dtype=np.float32)
    m["b31"] = np.ascontiguousarray(np.broadcast_to(rel_bias[31][None, :], (128, 8)), dtype=np.float32)
    m["relb"] = np.ascontiguousarray(np.concatenate([rel_bias, np.ones((1, 8), np.float32)], axis=0),
                                     dtype=np.float32)
    return m


def kernel(x, norm_gain, w_in, q_norm_gain, k_norm_gain, rel_bias, w_out):
    x = np.asarray(x)
    B, L, _ = x.shape
    key = ("nc", L)
    consts = _consts()
    nc = build(L=L)
    in_maps = [_host_inputs(x[b], np.asarray(norm_gain), np.asarray(w_in), np.asarray(q_norm_gain),
                            np.asarray(k_norm_gain), np.asarray(rel_bias), np.asarray(w_out), consts)
               for b in range(B)]
    res = run_bass_kernel_spmd(nc, in_maps, core_ids=list(range(B)))
    return np.stack([np.asarray(r["out"]) for r in res.results], axis=0).astype(np.float32)
```

```python
import math
import numpy as np
import ml_dtypes
import concourse.bass as bass
import concourse.mybir as mybir
from concourse.bass_utils import run_bass_kernel_spmd

F32 = mybir.dt.float32
BF16 = mybir.dt.bfloat16
AF = mybir.ActivationFunctionType
ALU = mybir.AluOpType
AX = mybir.AxisListType

D_MODEL = 1024
D_IN = 5200
IDX_SCALE = (16 * 64) ** -0.5
RMS_EPS = 1e-6
NEG_BIG = -30000.0


class _Op:
    __slots__ = ("eng", "fn", "deps", "tok_sem", "tok_val", "needed", "is_dma", "idx")

    def __init__(self, eng, fn, is_dma, idx):
        self.eng = eng
        self.fn = fn
        self.deps = []
        self.tok_sem = None
        self.tok_val = None
        self.needed = False
        self.is_dma = is_dma
        self.idx = idx


class Sched:
    ENGS = ("pe", "act", "dve", "pool", "sp")

    def __init__(self, nc):
        self.nc = nc
        self.ops = {e: [] for e in self.ENGS}
        self.last_w = {}
        self.readers = {}
        self.final_wait_ops = []

    def alias(self, dst_keys, src_keys):
        acc = []
        for k in src_keys:
            lw = self.last_w.get(k)
            if lw is not None:
                acc.append(lw)
            acc.extend(self.readers.get(k, ()))
        for k in dst_keys:
            self.readers.setdefault(k, [])
            self.readers[k] = list(self.readers[k]) + acc

    def _add(self, eng, fn, reads, writes, is_dma=False, dma_sem=None):
        op = _Op(eng, fn, is_dma, len(self.ops[eng]))
        excl = [k for k in reads if isinstance(k, str) and len(k) == 2 and k[0] == "B" and k[1].isdigit()]
        if excl:
            reads = [k for k in reads if k not in excl]
            writes = list(writes) + excl
        cand = []
        for k in reads:
            lw = self.last_w.get(k)
            if lw is not None:
                cand.append((lw, True))
        for k in writes:
            lw = self.last_w.get(k)
            if lw is not None:
                cand.append((lw, False))
            for r in self.readers.get(k, ()):
                cand.append((r, False))
        best = {}
        for d, raw in cand:
            if d is op:
                continue
            if (not d.is_dma) and (not is_dma) and d.eng == eng:
                if (not raw) or eng == "pe":
                    continue
            if d.is_dma:
                key = ("dma", d.tok_sem)
            else:
                key = ("eng", d.eng)
            cur = best.get(key)
            if cur is None or d.idx > cur.idx:
                best[key] = d
        op.deps = list(best.values())
        for k in writes:
            self.last_w[k] = op
            self.readers[k] = []
        for k in reads:
            self.readers.setdefault(k, []).append(op)
        if is_dma:
            op.tok_sem = dma_sem
        self.ops[eng].append(op)
        return op

    def pe(self, fn, reads=(), writes=()):
        return self._add("pe", fn, reads, writes)

    def act(self, fn, reads=(), writes=()):
        return self._add("act", fn, reads, writes)

    def dve(self, fn, reads=(), writes=()):
        return self._add("dve", fn, reads, writes)

    def pool(self, fn, reads=(), writes=()):
        return self._add("pool", fn, reads, writes)

    def dma(self, fn, reads=(), writes=(), sem="d0", queue="sp"):
        return self._add(queue, fn, reads, writes, is_dma=True, dma_sem=sem)

    def finalize(self):
        for e in self.ENGS:
            for op in self.ops[e]:
                for d in op.deps:
                    d.needed = True
        for op in self.final_wait_ops:
            op.needed = True
        cnt = {e: 0 for e in self.ENGS}
        dcnt = {}
        for e in self.ENGS:
            for op in self.ops[e]:
                if op.is_dma:
                    dcnt[op.tok_sem] = dcnt.get(op.tok_sem, 0) + 16
                    op.tok_val = dcnt[op.tok_sem]
                elif op.needed:
                    cnt[e] += 1
                    op.tok_sem = "E_" + e
                    op.tok_val = cnt[e]
        names = set(dcnt.keys()) | {"E_" + e for e in self.ENGS if cnt[e] > 0}
        return sorted(names)

    def emit(self, engine_obj, eng, sems):
        waited = {}
        for op in self.ops[eng]:
            for d in op.deps:
                s, v = d.tok_sem, d.tok_val
                if waited.get(s, 0) >= v:
                    continue
                engine_obj.wait_ge(sems[s], v)
                waited[s] = v
            ins = op.fn(engine_obj)
            if op.is_dma:
                ins.then_inc(sems[op.tok_sem], 16)
            elif op.needed:
                ins.then_inc(sems[op.tok_sem], 1)
        if eng == "sp":
            for op in self.final_wait_ops:
                s, v = op.tok_sem, op.tok_val
                if waited.get(s, 0) >= v:
                    continue
                engine_obj.wait_ge(sems[s], v)
                waited[s] = v

    def run(self):
        nc = self.nc
        names = self.finalize()
        sems = {n: nc.alloc_semaphore("s_" + n) for n in names}
        sch = self
        with nc.Block() as block:
            @block.sync
            def _(e):
                sch.emit(e, "sp", sems)

            @block.scalar
            def _(e):
                sch.emit(e, "act", sems)

            @block.vector
            def _(e):
                sch.emit(e, "dve", sems)

            @block.tensor
            def _(e):
                sch.emit(e, "pe", sems)

            @block.gpsimd
            def _(e):
                sch.emit(e, "pool", sems)


def _t5_bucket_np(d):
    d = np.maximum(d, 0).astype(np.int64)
    d_f = np.maximum(d, 1).astype(np.float32)
    large = 16 + (np.log(d_f / np.float32(16)) / np.float32(math.log(128 / 16))
                  * np.float32(16)).astype(np.int32)
    large = np.minimum(large, 31)
    return np.where(d < 16, d, large)


def _consts():
    bf = ml_dtypes.bfloat16
    c = {}
    c["ident"] = np.eye(128, dtype=np.float32).astype(bf)
    j = np.arange(128)[:, None]
    s = np.arange(128)[None, :]
    c["negtri"] = np.where(j >= s, -1.0, 0.0).astype(np.float32).astype(bf)
    c["strict"] = np.where(j < s, 1.0, 0.0).astype(np.float32).astype(bf)
    c["blockones"] = np.where((j // 64) == (s // 64), 1.0 / 64, 0.0).astype(np.float32).astype(bf)
    c["onescol"] = np.ones((128, 1), np.float32).astype(bf)
    oh = np.zeros((33, 384), np.float32)
    for i in range(384):
        d = i - 128
        if d < 0:
            oh[32, i] = NEG_BIG
        else:
            b = int(_t5_bucket_np(np.array([d]))[0])
            oh[b, i] += 1.0
            oh[31, i] -= 1.0
    c["onehot"] = oh
    return c


def build(L=2048, niter=22, dbg=False):
    NT = L // 128
    NCH = L // 512
    TOPK = min(256, L // 4)
    nc = bass.Bass("TRN2", target_bir_lowering=False, dynamic_dma_scratch_size=8192)

    def din(name, shape, dt=F32):
        return nc.dram_tensor(name, list(shape), dt, kind="ExternalInput").ap()

    x = din("x", [L, D_MODEL])
    w_in = din("w_in", [D_MODEL, D_IN])
    w_out = din("w_out", [D_MODEL, D_MODEL])
    gain_d = din("gain", [128, 8])
    gq_d = din("gq", [128, 1])
    gk_d = din("gk", [128, 1])
    b31_d = din("b31", [128, 8])
    relb_d = din("relb", [33, 8])
    onehot_d = din("onehot", [33, 384])
    ident_d = din("ident", [128, 128], BF16)
    negtri_d = din("negtri", [128, 128], BF16)
    strict_d = din("strict", [128, 128], BF16)
    blockones_d = din("blockones", [128, 128], BF16)
    onescol_d = din("onescol", [128, 1], BF16)
    out = nc.dram_tensor("out", [L, D_MODEL], F32, kind="ExternalOutput").ap()
    scr1_t = nc.dram_tensor("scr1", [8, 384], F32, kind="Internal")
    scr2_t = nc.dram_tensor("scr2", [128, 8 * 384], F32, kind="Internal")

    def sb(name, shape, dt):
        return nc.alloc_sbuf_tensor(name, list(shape), dt)

    gain = sb("gain_s", [128, 8], F32)
    gq = sb("gq_s", [128, 1], F32)
    gk = sb("gk_s", [128, 1], F32)
    b31 = sb("b31_s", [128, 8], F32)
    relb = sb("relb_s", [33, 8], F32)
    onehot = sb("onehot_s", [33, 384], F32)
    ident = sb("ident_s", [128, 128], BF16)
    negtri = sb("negtri_s", [128, 128], BF16)
    strict = sb("strict_s", [128, 128], BF16)
    blockones = sb("blockones_s", [128, 128], BF16)
    onescol = sb("onescol_s", [128, 1], BF16)
    fb = sb("fb_s", [8, 384], F32)
    btab = sb("btab", [128, 8, 2, 128], F32)
    kTa = sb("kTa", [128, 4, L], BF16)
    kTb = sb("kTb", [128, 4, L], BF16)
    kTi = sb("kTi", [128, L], BF16)
    Va = sb("Va", [128, NT, 8, 65], BF16)
    Vb = sb("Vb", [128, NT, 8, 64], BF16)
    hTc = sb("hTc", [128, 8, 512], BF16)
    qTa = sb("qTa", [128, 4, 512], BF16)
    qTb = sb("qTb", [128, 4, 512], BF16)
    qTi_raw = sb("qTi", [128, 4096], BF16)
    qTi = qTi_raw[:].rearrange("p (c t) -> p c t", c=8)
    mixed = qTi_raw[:].rearrange("p (j f) -> p j f", j=4)
    sg = sb("sg", [128, 4, 1024], BF16)
    ws = sb("ws", [128, 4, 16], F32)
    maskT = sb("maskT", [128, NT, 512], BF16)
    slab = [sb("slab0", [128, 8, 512], BF16), sb("slab1", [128, 8, 512], BF16)]
    score = [sb("score0", [128, L], F32), sb("score1", [128, L], F32)]
    accB_t = sb("accB", [128, 2048], F32)
    accB = accB_t[:].rearrange("p (j h d) -> p j h d", j=4, h=8)
    maskrow = [sb("maskrow0", [128, L], BF16), sb("maskrow1", [128, L], BF16)]
    xt = [sb("xt0", [128, 1024], F32), sb("xt1", [128, 1024], F32)]
    xs = [sb("xs0", [128, 1024], BF16), sb("xs1", [128, 1024], BF16)]
    junk = sb("junk", [128, 1024], BF16)
    NP32 = 8
    NP16 = 8
    p32 = [sb("p32_%d" % i, [128, 512], F32) for i in range(NP32)]
    p16 = [sb("p16_%d" % i, [128, 512], BF16) for i in range(NP16)]
    small = sb("small", [128, 64], F32)
    bis = sb("bis", [128, 2, 8 + 32], F32)
    halves = sb("halves", [128, 32], F32)
    rden = sb("rden", [128, 2, 4], F32)
    gsb = sb("gsb", [128, 2, 4], F32)

    banks = [nc.alloc_psum_tensor("B%d" % i, [128, 512], F32) for i in range(8)]
    B0bf = banks[0][:].bitcast(BF16)

    S = Sched(nc)
    cnt32 = [0]
    cnt16 = [0]

    def get32():
        i = 3 + (cnt32[0] % 3)
        cnt32[0] += 1
        return p32[i], ("p32", i)

    def get16():
        i = cnt16[0] % 4
        cnt16[0] += 1
        return p16[i], ("p16", i)

    def cload(dst, src, key):
        S.dma(lambda e: e.dma_start(out=dst[:], in_=src), writes=[key], sem="c_" + key)

    cload(gain, gain_d, "gain")
    cload(gq, gq_d, "gq")
    cload(gk, gk_d, "gk")
    cload(b31, b31_d, "b31")
    cload(relb, relb_d, "relb")
    cload(onehot, onehot_d, "onehot")
    cload(ident, ident_d, "ident")
    cload(negtri, negtri_d, "negtri")
    cload(strict, strict_d, "strict")
    cload(blockones, blockones_d, "blockones")
    cload(onescol, onescol_d, "onescol")
    for k in range(niter + 1):
        S.dve(lambda e, k=k: e.memset(halves[:, k:k + 1], 2.0 ** -(k + 1)), writes=[("halves", k)])
    S.pool(lambda e: e.memset(Va[:, :, :, 64:65], 1.0), writes=["Va_ones"])

    S.pe(lambda e: e.matmul(out=banks[3][0:8, 0:384], lhsT=relb[:, :], rhs=onehot[:, :], start=True, stop=True),
         reads=["relb", "onehot"], writes=["B3"])
    S.act(lambda e: e.activation(out=fb[:], in_=banks[3][0:8, 0:384], func=AF.Copy), reads=["B3"], writes=["fb"])
    S.dma(lambda e: e.dma_start(out=scr1_t.ap(), in_=fb[:]), reads=["fb"], writes=["scr1"], sem="c_scr1")
    scr2_v = scr2_t.ap().rearrange("p (h i) -> p h i", h=8)
    S.dma(lambda e: e.dma_start(out=scr2_v, in_=scr1_t.ap().partition_broadcast(128)),
          reads=["scr1"], writes=["scr2"], sem="c_scr2")
    for r in range(2):
        src = bass.AP(tensor=scr2_t, offset=128 * (r + 1), ap=[[8 * 384 - 1, 128], [384, 8], [1, 128]])
        S.dma(lambda e, r=r, src=src: e.dma_start(out=btab[:, :, r, :], in_=src),
              reads=["scr2"], writes=[("btab", r)], sem="c_btab%d" % r)

    slab_ctr = [0]

    def load_slab(pieces, src=None):
        if src is None:
            src = w_in
        i = slab_ctr[0] % 2
        slab_ctr[0] += 1
        key = ("slab", i)
        for (d0, s0, n) in pieces:
            S.dma(lambda e, i=i, d0=d0, s0=s0, n=n, src=src: e.dma_start(
                out=slab[i][:, :, d0:d0 + n],
                in_=src[:, s0:s0 + n].rearrange("(c p) n -> p c n", p=128)),
                writes=[key], sem="w%d" % i, queue="pool")
        return slab[i], key

    pbank_ctr = [0]

    def proj_bank():
        i = (1, 2, 0)[pbank_ctr[0] % 3]
        pbank_ctr[0] += 1
        return banks[i], "B%d" % i

    def fm_matmuls(sl, skey, f, ncols_feat=128, col0=None, bank=None):
        bk, bkey = proj_bank() if bank is None else bank
        c0 = f * 128 if col0 is None else col0
        for c8 in range(8):
            S.pe(lambda e, c8=c8, bk=bk, sl=sl, c0=c0: e.matmul(
                out=bk[0:ncols_feat, :], lhsT=sl[:, c8, c0:c0 + ncols_feat], rhs=hTc[:, c8, :],
                start=(c8 == 0), stop=(c8 == 7)),
                reads=[skey, ("hTc", c8)], writes=[bkey])
        return bk, bkey

    def tm_matmuls(sl, skey, j, ncols=512, col0=0, bank=None):
        bk, bkey = proj_bank() if bank is None else bank
        for c8 in range(8):
            S.pe(lambda e, c8=c8, bk=bk, sl=sl: e.matmul(
                out=bk[:, 0:ncols], lhsT=hTc[:, c8, j * 128:(j + 1) * 128], rhs=sl[:, c8, col0:col0 + ncols],
                start=(c8 == 0), stop=(c8 == 7)),
                reads=[skey, ("hTc", c8)], writes=[bkey])
        return bk, bkey

    def fm_g(sl, skey, f, bank):
        bk, bkey = bank
        c0 = f * 128
        for c8 in range(8):
            S.pe(lambda e, c8=c8, bk=bk, sl=sl, c0=c0: e.matmul(
                out=bk[:, :], lhsT=sl[:, c8, c0:c0 + 128], rhs=hTc[:, c8, :],
                start=(c8 == 0), stop=(c8 == 7)),
                reads=[skey, ("hTc", c8)], writes=[bkey])
            if c8 % 2 == 1 and c8 < 7:
                yield 0.25
        return bk, bkey

    def tm_g(sl, skey, j, bank):
        bk, bkey = bank
        for c8 in range(8):
            S.pe(lambda e, c8=c8, bk=bk, sl=sl: e.matmul(
                out=bk[:, 0:512], lhsT=hTc[:, c8, j * 128:(j + 1) * 128], rhs=sl[:, c8, 0:512],
                start=(c8 == 0), stop=(c8 == 7)),
                reads=[skey, ("hTc", c8)], writes=[bkey])
            if c8 % 2 == 1 and c8 < 7:
                yield 0.25
        return bk, bkey

    def qknorm(bk, bkey, gvec, gkey, dst_ap, dst_key):
        sq, sqk = get16()
        S.act(lambda e: e.activation(out=sq[:], in_=bk[:], func=AF.Square), reads=[bkey], writes=[sqk])
        S.pe(lambda e: e.matmul(out=banks[3][:], lhsT=blockones[:], rhs=sq[:], start=True, stop=True),
             reads=[sqk, "blockones"], writes=["B3"])
        lt, ltk = get32()
        S.act(lambda e: e.activation(out=lt[:], in_=banks[3][:], func=AF.Ln, bias=RMS_EPS, scale=1.0),
              reads=["B3"], writes=[ltk])
        S.act(lambda e: e.activation(out=lt[:], in_=lt[:], func=AF.Exp, scale=-0.5), reads=[ltk], writes=[ltk])
        S.dve(lambda e: e.scalar_tensor_tensor(out=dst_ap, in0=bk[:], scalar=gvec[:, 0:1], in1=lt[:],
                                               op0=ALU.mult, op1=ALU.mult),
              reads=[bkey, ltk, gkey], writes=[dst_key])

    evac_ctr = [0]

    def evac_copy(dst_ap, src_ap, reads, writes, scale=None):
        i = evac_ctr[0]
        evac_ctr[0] += 1
        if i % 2 == 0:
            if scale is None:
                S.act(lambda e: e.activation(out=dst_ap, in_=src_ap, func=AF.Copy), reads=reads, writes=writes)
            else:
                S.act(lambda e: e.activation(out=dst_ap, in_=src_ap, func=AF.Copy, scale=scale),
                      reads=reads, writes=writes)
        else:
            if scale is None:
                S.dve(lambda e: e.tensor_copy(out=dst_ap, in_=src_ap), reads=reads, writes=writes)
            else:
                S.dve(lambda e: e.tensor_scalar(out=dst_ap, in0=src_ap, scalar1=scale, scalar2=None, op0=ALU.mult),
                      reads=reads, writes=writes)

    out_ops = []
    wo_pref = {}

    R32 = p32[0:3]
    E32 = p32[3:5]
    SPt = p16[0:2]
    APt = p16[2:4]
    Et = p16[4:6]
    EMt = p16[6:8]

    def emit_norm(c):
        for j in range(4):
            i = 4 * c + j
            sl_ = i % 2
            S.dma(lambda e, i=i, sl_=sl_: e.dma_start(out=xt[sl_][:], in_=x[i * 128:(i + 1) * 128, :]),
                  writes=[("xt", sl_)], sem="x%d" % sl_)
            S.act(lambda e, sl_=sl_: e.activation(out=junk[:], in_=xt[sl_][:], func=AF.Square,
                                                  accum_out=small[:, sl_:sl_ + 1]),
                  reads=[("xt", sl_)], writes=[("ss", sl_), "junk"])
            S.act(lambda e, sl_=sl_: e.activation(out=small[:, 2 + sl_:3 + sl_], in_=small[:, sl_:sl_ + 1],
                                                  func=AF.Ln, scale=1.0 / D_MODEL, bias=RMS_EPS),
                  reads=[("ss", sl_)], writes=[("lnv", sl_)])
            S.act(lambda e, sl_=sl_: e.activation(out=small[:, 4 + sl_:5 + sl_], in_=small[:, 2 + sl_:3 + sl_],
                                                  func=AF.Exp, scale=-0.5),
                  reads=[("lnv", sl_)], writes=[("rstd", sl_)])
            S.dve(lambda e, sl_=sl_: e.tensor_scalar(out=xs[sl_][:], in0=xt[sl_][:], scalar1=small[:, 4 + sl_:5 + sl_],
                                                     scalar2=None, op0=ALU.mult),
                  reads=[("xt", sl_), ("rstd", sl_)], writes=[("xs", sl_)])
            for c8 in range(8):
                S.pe(lambda e, c8=c8, sl_=sl_: e.transpose(out=B0bf[:, c8 * 128:(c8 + 1) * 128],
                                                           in_=xs[sl_][:, c8 * 128:(c8 + 1) * 128], identity=ident[:]),
                     reads=[("xs", sl_), "ident"], writes=["B0"])
            for c8 in range(8):
                dst = hTc[:, c8, j * 128:(j + 1) * 128]
                srcp = B0bf[:, c8 * 128:(c8 + 1) * 128]
                if j % 2 == 0:
                    S.dve(lambda e, dst=dst, srcp=srcp, c8=c8: e.tensor_scalar(
                        out=dst, in0=srcp, scalar1=gain[:, c8:c8 + 1], scalar2=None, op0=ALU.mult),
                        reads=["B0", "gain"], writes=[("hTc", c8)])
                else:
                    S.act(lambda e, dst=dst, srcp=srcp, c8=c8: e.activation(
                        out=dst, in_=srcp, func=AF.Copy, scale=gain[:, c8:c8 + 1]),
                        reads=["B0", "gain"], writes=[("hTc", c8)])

    def emit_proj1(c):
        tok0 = c * 512
        sl, sk = load_slab([(0, 3072, 64), (64, 3072, 64), (128, 3136, 16)])
        bk, bkey = fm_matmuls(sl, sk, 0)
        evac_copy(kTi[:, tok0:tok0 + 512], bk[:], [bkey], [("kTi", c)])
        for j in range(4):
            bk, bkey = tm_matmuls(sl, sk, j, ncols=16, col0=128)
            S.dve(lambda e, bk=bk, j=j: e.tensor_scalar(out=ws[:, j, :], in0=bk[:, 0:16], scalar1=IDX_SCALE,
                                                        scalar2=None, op0=ALU.mult),
                  reads=[bkey], writes=[("ws", j)])
        S.alias([("qTi", f8) for f8 in range(8)], ["mixed"])
        for half in range(2):
            sl, sk = load_slab([(0, 2048 + 512 * half, 512)])
            for f in range(4):
                bk, bkey = fm_matmuls(sl, sk, f)
                evac_copy(qTi[:, 4 * half + f, :], bk[:], [bkey], [("qTi", 4 * half + f)])

    def gen_proj2(c):
        tok0 = c * 512
        pb2 = [0]

        def bank2():
            i = (6, 7, 4, 5)[pb2[0] % 4]
            pb2[0] += 1
            return banks[i], "B%d" % i

        sl, sk = load_slab([(0, 512, 512)])
        for f in range(4):
            bk, bkey = yield from fm_g(sl, sk, f, bank2())
            qknorm(bk, bkey, gk, "gk", kTa[:, f, tok0:tok0 + 512], ("kTa", f, c))
            yield 0.25
        sl, sk = load_slab([(0, 1024, 512)])
        for j in range(4):
            i = 4 * c + j
            bk, bkey = yield from tm_g(sl, sk, j, bank2())
            evac_copy(Va[:, i, :, 0:64], bk[:].rearrange("p (h d) -> p h d", h=8), [bkey, "Va_ones"], [("Va", i)])
            yield 0.25
        sl, sk = load_slab([(0, 3664, 512)])
        for f in range(4):
            bk, bkey = yield from fm_g(sl, sk, f, bank2())
            evac_copy(kTb[:, f, tok0:tok0 + 512], bk[:], [bkey], [("kTb", f, c)], scale=0.125)
            yield 0.25
        sl, sk = load_slab([(0, 4176, 512)])
        for j in range(4):
            i = 4 * c + j
            bk, bkey = yield from tm_g(sl, sk, j, bank2())
            evac_copy(Vb[:, i, :, :], bk[:].rearrange("p (h d) -> p h d", h=8), [bkey], [("Vb", i)])
            yield 0.25
        sl, sk = load_slab([(0, 0, 512)])
        for f in range(4):
            bk, bkey = yield from fm_g(sl, sk, f, bank2())
            qknorm(bk, bkey, gq, "gq", qTa[:, f, :], ("qTa", f))
            yield 0.25
        sl, sk = load_slab([(0, 3152, 512)])
        for f in range(4):
            bk, bkey = yield from fm_g(sl, sk, f, bank2())
            evac_copy(qTb[:, f, :], bk[:], [bkey], [("qTb", f)])
            yield 0.25
        for half, col in ((0, 1536), (1, 4688)):
            sl, sk = load_slab([(0, col, 512)])
            for j in range(4):
                bk, bkey = yield from tm_g(sl, sk, j, bank2())
                S.act(lambda e, bk=bk, j=j, half=half: e.activation(
                    out=sg[:, j, half * 512:(half + 1) * 512], in_=bk[:], func=AF.Silu),
                    reads=[bkey], writes=[("sg", j, half)])
                yield 0.25

    W_IDX = 1.0
    W_BIS = 2.2

    def gen_index_bisect(c):
        dctr = [0]
        rctr = [0]

        def indexer_tiles(js):
          for hh in range(16):
            f8 = hh // 2
            base = 64 * (hh % 2)
            for scn in range(c + 1):
              for j in js:
                sc_t = score[j % 2]
                sckey = ("score", j % 2)
                if True:
                    w = 512 if scn < c else (j + 1) * 128
                    bi = (1, 2, 0)[dctr[0] % 3]
                    dctr[0] += 1
                    bk = banks[bi]
                    bkey = "B%d" % bi
                    S.pe(lambda e, bk=bk, f8=f8, base=base, scn=scn, w=w, j=j: e.matmul(
                        out=bk[:, 0:w], lhsT=qTi[base:base + 64, f8, j * 128:(j + 1) * 128],
                        rhs=kTi[base:base + 64, scn * 512:scn * 512 + w], start=True, stop=True),
                        reads=[("qTi", f8), ("kTi", scn)], writes=[bkey])
                    ri = (0, 1, 2, 6, 7)[rctr[0] % 5]
                    rctr[0] += 1
                    r, rk = p32[ri], ("p32", ri)
                    S.act(lambda e, bk=bk, r=r, w=w: e.activation(out=r[:, 0:w], in_=bk[:, 0:w], func=AF.Relu),
                          reads=[bkey], writes=[rk])
                    dst = sc_t[:, scn * 512:scn * 512 + w]
                    if hh == 0:
                        S.dve(lambda e, dst=dst, r=r, w=w, j=j, hh=hh: e.tensor_scalar(
                            out=dst, in0=r[:, 0:w], scalar1=ws[:, j, hh:hh + 1], scalar2=None, op0=ALU.mult),
                            reads=[rk, ("ws", j)], writes=[sckey])
                    else:
                        S.dve(lambda e, dst=dst, r=r, w=w, j=j, hh=hh: e.scalar_tensor_tensor(
                            out=dst, in0=r[:, 0:w], scalar=ws[:, j, hh:hh + 1], in1=dst,
                            op0=ALU.mult, op1=ALU.add),
                            reads=[rk, ("ws", j), sckey], writes=[sckey])
            yield W_IDX

        def bisect_tiles(js):
            bs = [j % 2 for j in js]
            b0, b1 = min(bs), max(bs) + 1
            bkeys = [("bis", b) for b in bs]
            for j in js:
                i = 4 * c + j
                Si = 128 * (i + 1)
                b = j % 2
                sc_t = score[b]
                sckey = ("score", b)
                bk_ = ("bis", b)
                S.dve(lambda e, sc_t=sc_t, Si=Si, b=b: e.tensor_reduce(out=bis[:, b, 0:1], in_=sc_t[:, 0:Si],
                                                                       axis=AX.X, op=ALU.min),
                      reads=[sckey], writes=[bk_])
                S.pool(lambda e, sc_t=sc_t, Si=Si: e.affine_select(
                    out=sc_t[:, Si - 128:Si], in_=sc_t[:, Si - 128:Si], pattern=[[-1, 128]],
                    compare_op=ALU.is_ge, fill=-3.0e38, base=0, channel_multiplier=1),
                    reads=[sckey, bk_], writes=[sckey])
                S.dve(lambda e, sc_t=sc_t, Si=Si, b=b: e.tensor_reduce(out=bis[:, b, 1:2], in_=sc_t[:, 0:Si],
                                                                       axis=AX.X, op=ALU.max),
                      reads=[sckey, bk_], writes=[bk_])
                yield
            if len(bs) == 2:
                S.dve(lambda e: e.tensor_tensor(out=bis[:, 0, 0:1], in0=bis[:, 0, 0:1], in1=bis[:, 1, 0:1], op=ALU.min),
                      reads=bkeys, writes=[("bis", 0)])
                S.dve(lambda e: e.tensor_tensor(out=bis[:, 0, 1:2], in0=bis[:, 0, 1:2], in1=bis[:, 1, 1:2], op=ALU.max),
                      reads=bkeys, writes=[("bis", 0)])
            S.dve(lambda e: e.tensor_tensor(out=bis[:, b0, 2:3], in0=bis[:, b0, 1:2], in1=bis[:, b0, 0:1],
                                            op=ALU.subtract), reads=bkeys, writes=[("bis", b0)])
            S.dve(lambda e: e.tensor_scalar(out=bis[:, b0, 8:8 + niter + 1], in0=halves[:, 0:niter + 1],
                                            scalar1=bis[:, b0, 2:3], scalar2=None, op0=ALU.mult),
                  reads=bkeys + [("halves", k) for k in range(niter + 1)], writes=[("bis", b0)])
            for b in bs:
                S.dve(lambda e, b=b: e.tensor_tensor(out=bis[:, b, 3:4], in0=bis[:, b0, 0:1], in1=bis[:, b0, 8:9],
                                                     op=ALU.add), reads=bkeys, writes=[("bis", b)])
            yield
            for k in range(niter):
                for j in js:
                    i = 4 * c + j
                    Si = 128 * (i + 1)
                    b = j % 2
                    S.dve(lambda e, sc_t=score[b], Si=Si, b=b, jk=R32[b][:].bitcast(mybir.dt.uint8): e.tensor_scalar(
                        out=jk[:, 0:Si], in0=sc_t[:, 0:Si], scalar1=bis[:, b, 3:4], scalar2=None,
                        op0=ALU.is_ge, op1=ALU.add, accum_out=bis[:, b, 4:5]),
                        reads=[("score", b), ("bis", b)], writes=[("p32", b), ("bis", b)])
                S.dve(lambda e, k=k: e.tensor_scalar(
                    out=bis[:, b0:b1, 5], in0=bis[:, b0:b1, 4], scalar1=float(TOPK) - 0.5,
                    scalar2=bis[:, b0, 8 + k:9 + k], op0=(ALU.is_ge if k < niter - 1 else ALU.is_lt),
                    op1=ALU.mult),
                    reads=bkeys, writes=bkeys)
                if k < niter - 1:
                    S.dve(lambda e, k=k: e.scalar_tensor_tensor(
                        out=bis[:, b0:b1, 3], in0=bis[:, b0:b1, 3], scalar=bis[:, b0, 9 + k:10 + k],
                        in1=bis[:, b0:b1, 5], op0=ALU.subtract, op1=ALU.add),
                        reads=bkeys, writes=bkeys)
                else:
                    S.dve(lambda e: e.tensor_tensor(out=bis[:, b0:b1, 6], in0=bis[:, b0:b1, 3],
                                                    in1=bis[:, b0:b1, 5], op=ALU.subtract),
                          reads=bkeys, writes=bkeys)
                yield W_BIS
            for j in js:
                i = 4 * c + j
                Si = 128 * (i + 1)
                b = j % 2
                sc_t = score[b]
                sckey = ("score", b)
                bk_ = ("bis", b)
                mr = maskrow[b]
                mrk = ("maskrow", b)
                S.dve(lambda e, sc_t=sc_t, Si=Si, b=b, mr=mr: e.tensor_scalar(
                    out=mr[:, 0:Si], in0=sc_t[:, 0:Si], scalar1=bis[:, b, 6:7], scalar2=None, op0=ALU.is_ge),
                    reads=[sckey, bk_], writes=[mrk])
                a0 = 0
                while a0 <= i:
                    n = min(8, i + 1 - a0)
                    for q in range(n):
                        a = a0 + q
                        S.pe(lambda e, mr=mr, a=a, q=q: e.transpose(out=B0bf[:, q * 128:(q + 1) * 128],
                                                                    in_=mr[:, a * 128:(a + 1) * 128], identity=ident[:]),
                             reads=[mrk, "ident"], writes=["B0"])
                    dst = maskT[:, a0:a0 + n, j * 128:(j + 1) * 128]
                    srcp = B0bf[:, 0:n * 128].rearrange("p (n t) -> p n t", n=n)
                    S.act(lambda e, dst=dst, srcp=srcp: e.activation(out=dst, in_=srcp, func=AF.Copy),
                          reads=["B0"], writes=[("maskT", j)])
                    a0 += n
                    yield

        for jp in range(2):
            js = []
            for j in (2 * jp, 2 * jp + 1):
                i = 4 * c + j
                if 128 * (i + 1) <= TOPK:
                    S.pool(lambda e, i=i, j=j: e.memset(maskT[:, 0:i + 1, j * 128:(j + 1) * 128], 1.0),
                           writes=[("maskT", j)])
                    yield
                else:
                    js.append(j)
            if js:
                yield from indexer_tiles(js)
                yield from bisect_tiles(js)

    def gen_attnB(c):
        na = 4 * c + 4
        tiles = [(h, a) for h in range(8) for a in range(na)]
        NTL = len(tiles)
        zb, zkey = banks[4], "B4"
        lb, lkey = banks[5], "B5"
        st = {}

        def info(n):
            h, a = tiles[n]
            f = h // 2
            base = 64 * (h % 2)
            i0 = max(0, a - 4 * c)
            w = 512 - i0 * 128
            diag = a >= 4 * c
            kl = kTb[base:base + 64, f, a * 128:(a + 1) * 128]
            qr = qTb[base:base + 64, f, i0 * 128:512]
            spi = (0, 1, 4)[n % 3]
            SP, SPk = p16[spi], ("p16", spi)
            Ap, Apk = APt[n % 2], ("p16", 2 + n % 2)
            pbi = 6 + (n % 2)
            return h, a, f, i0, w, diag, kl, qr, SP, SPk, Ap, Apk, banks[pbi], "B%d" % pbi, n % 2

        def stage_a(n):
            h, a, f, i0, w, diag, kl, qr, SP, SPk, Ap, Apk, pbk, pkey, gb = info(n)
            S.pe(lambda e: e.matmul(out=zb[:, 0:w], lhsT=kl, rhs=qr, start=True, stop=True),
                 reads=[("kTb", f, a // 4), ("qTb", f)], writes=[zkey])
            e1, e1k = E32[n % 2], ("p32", 3 + n % 2)
            S.act(lambda e: e.activation(out=e1[:, 0:w], in_=zb[:, 0:w], func=AF.Exp), reads=[zkey], writes=[e1k])
            S.act(lambda e: e.activation(out=SP[:, 0:w], in_=e1[:, 0:w], func=AF.Ln, bias=1.0, scale=1.0),
                  reads=[e1k], writes=[SPk])
            if diag:
                S.pool(lambda e: e.tensor_tensor(out=SP[:, 0:128], in0=SP[:, 0:128], in1=strict[:], op=ALU.mult),
                       reads=[SPk, "strict"], writes=[SPk])

        def stage_b(n):
            h, a, f, i0, w, diag, kl, qr, SP, SPk, Ap, Apk, pbk, pkey, gb = info(n)
            S.pe(lambda e: e.matmul(out=lb[:, 0:w], lhsT=kl, rhs=qr, start=True, stop=False),
                 reads=[("kTb", f, a // 4), ("qTb", f)], writes=[lkey])
            S.pe(lambda e: e.matmul(out=lb[:, 0:w], lhsT=negtri[:], rhs=SP[:, 0:w], start=False, stop=True),
                 reads=[SPk, "negtri"], writes=[lkey])
            if a > 0:
                fcs = True
                for jj in range(i0, 4):
                    blk = (jj - i0) * 128
                    S.pe(lambda e, blk=blk, jj=jj, fcs=fcs: e.matmul(
                        out=banks[3][:, jj:jj + 1], lhsT=SP[:, blk:blk + 128], rhs=onescol[:, 0:1],
                        start=fcs, stop=(jj == 3), skip_group_check=True),
                        reads=[SPk, "onescol"], writes=["B3"])
                    fcs = False
                S.act(lambda e: e.activation(out=gsb[:, gb, i0:4], in_=banks[3][:, i0:4], func=AF.Exp, scale=-1.0),
                      reads=["B3"], writes=[("gsb", gb)])
            S.act(lambda e: e.activation(out=Ap[:, 0:w], in_=lb[:, 0:w], func=AF.Exp), reads=[lkey], writes=[Apk])
            if diag:
                S.pool(lambda e: e.tensor_tensor(out=Ap[:, 0:128], in0=Ap[:, 0:128], in1=strict[:], op=ALU.mult),
                       reads=[Apk, "strict"], writes=[Apk])

        def stage_c(n):
            h, a, f, i0, w, diag, kl, qr, SP, SPk, Ap, Apk, pbk, pkey, gb = info(n)
            fp = True
            for jj in range(i0, 4):
                blk = (jj - i0) * 128
                S.pe(lambda e, jj=jj, blk=blk, fp=fp: e.matmul(
                    out=pbk[:, jj * 64:(jj + 1) * 64], lhsT=Ap[:, blk:blk + 128], rhs=Vb[:, a, h, :],
                    start=fp, stop=(jj == 3), skip_group_check=True),
                    reads=[Apk, ("Vb", a)], writes=[pkey])
                fp = False

        def stage_d1(n):
            h, a, f, i0, w, diag, kl, qr, SP, SPk, Ap, Apk, pbk, pkey, gb = info(n)
            if a == 0:
                return
            nb = 4 - i0
            accv = accB[:, i0:4, h, :]
            gv = gsb[:, gb, i0:4].unsqueeze(2).broadcast_to([128, nb, 64])
            S.dve(lambda e: e.tensor_tensor(out=accv, in0=accv, in1=gv, op=ALU.mult),
                  reads=[("gsb", gb), ("accB", h)], writes=[("accB", h)])

        def stage_d2(n):
            h, a, f, i0, w, diag, kl, qr, SP, SPk, Ap, Apk, pbk, pkey, gb = info(n)
            nb = 4 - i0
            accv = accB[:, i0:4, h, :]
            pv = pbk[:, i0 * 64:256].rearrange("p (j d) -> p j d", j=nb)
            if a == 0:
                S.dve(lambda e: e.tensor_copy(out=accv, in_=pv), reads=[pkey], writes=[("accB", h)])
            else:
                S.dve(lambda e: e.tensor_tensor(out=accv, in0=accv, in1=pv, op=ALU.add),
                      reads=[pkey, ("accB", h)], writes=[("accB", h)])

        for n in range(-3, NTL + 1):
            if 0 <= n - 1 < NTL:
                stage_d2(n - 1)
            if 0 <= n + 3 < NTL:
                stage_a(n + 3)
            if 0 <= n + 1 < NTL:
                stage_b(n + 1)
            if 0 <= n < NTL:
                stage_d1(n)
                stage_c(n)
            yield

    def gen_attnA(c):
        na = 4 * c + 4
        tiles = [(h, a) for h in range(8) for a in range(na)]
        NTL = len(tiles)

        def info(n):
            h, a = tiles[n]
            f = h // 2
            base = 64 * (h % 2)
            i0 = max(0, a - 4 * c)
            w = 512 - i0 * 128
            sbi = 4 + (n % 2)
            ei = (4, 5, 0)[n % 3]
            E, Ek = p16[ei], ("p16", ei)
            EM, EMk = EMt[n % 2], ("p16", 6 + n % 2)
            return h, a, f, base, i0, w, banks[sbi], "B%d" % sbi, E, Ek, EM, EMk

        def stage_a(n):
            h, a, f, base, i0, w, sbk, sbkey, E, Ek, EM, EMk = info(n)
            S.pe(lambda e: e.matmul(out=sbk[:, 0:w], lhsT=kTa[base:base + 64, f, a * 128:(a + 1) * 128],
                                    rhs=qTa[base:base + 64, f, i0 * 128:512], start=True, stop=True),
                 reads=[("kTa", f, a // 4), ("qTa", f)], writes=[sbkey])
            far0 = None
            nn = 0
            for jj in range(i0, 4):
                d = 4 * c + jj - a
                blk = (jj - i0) * 128
                if d <= 1:
                    tmp, tk = E32[nn % 2], ("p32", 3 + nn % 2)
                    nn += 1
                    S.dve(lambda e, blk=blk, tmp=tmp, d=d: e.scalar_tensor_tensor(
                        out=tmp[:, 0:128], in0=sbk[:, blk:blk + 128], scalar=0.125, in1=btab[:, h, d, :],
                        op0=ALU.mult, op1=ALU.add),
                        reads=[sbkey, ("btab", d)], writes=[tk])
                    S.act(lambda e, blk=blk, tmp=tmp: e.activation(
                        out=E[:, blk:blk + 128], in_=tmp[:, 0:128], func=AF.Exp, bias=b31[:, h:h + 1], scale=1.0),
                        reads=[tk, "b31"], writes=[Ek])
                else:
                    far0 = blk
                    break
            if far0 is not None:
                S.act(lambda e, far0=far0: e.activation(
                    out=E[:, far0:w], in_=sbk[:, far0:w], func=AF.Exp, bias=b31[:, h:h + 1], scale=0.125),
                    reads=[sbkey, "b31"], writes=[Ek])

        def stage_b(n):
            h, a, f, base, i0, w, sbk, sbkey, E, Ek, EM, EMk = info(n)
            S.dve(lambda e: e.tensor_tensor(out=EM[:, 0:w], in0=E[:, 0:w], in1=maskT[:, a, i0 * 128:512], op=ALU.mult),
                  reads=[Ek] + [("maskT", jj) for jj in range(i0, 4)], writes=[EMk])

        def stage_c(n):
            h, a, f, base, i0, w, sbk, sbkey, E, Ek, EM, EMk = info(n)
            obi = 6 + (h % 2)
            ob, obkey = banks[obi], "B%d" % obi
            for jj in range(i0, 4):
                blk = (jj - i0) * 128
                first = (a == 0 and jj == i0)
                last = (a == 4 * c + jj)
                S.pe(lambda e, jj=jj, blk=blk, first=first, last=last: e.matmul(
                    out=ob[:, jj * 65:(jj + 1) * 65], lhsT=EM[:, blk:blk + 128], rhs=Va[:, a, h, :],
                    start=first, stop=last, skip_group_check=True),
                    reads=[EMk, ("Va", a), "Va_ones"], writes=[obkey])
            if a == na - 1:
                rb = h % 2
                S.dve(lambda e: e.reciprocal(
                    out=rden[:, rb, :], in_=ob[:, 0:260].rearrange("p (j d) -> p j d", j=4)[:, :, 64]),
                    reads=[obkey], writes=[("rden", rb)])
                for jj in range(4):
                    S.dve(lambda e, jj=jj: e.scalar_tensor_tensor(
                        out=mixed[:, jj, h * 64:(h + 1) * 64], in0=ob[:, jj * 65:jj * 65 + 64],
                        scalar=rden[:, rb, jj:jj + 1], in1=sg[:, jj, h * 64:(h + 1) * 64],
                        op0=ALU.mult, op1=ALU.mult),
                        reads=[obkey, ("rden", rb), ("sg", jj, 0)], writes=["mixed"])

        for n in range(-3, NTL):
            if 0 <= n + 1 < NTL:
                stage_b(n + 1)
            if 0 <= n + 3 < NTL:
                stage_a(n + 3)
            if 0 <= n < NTL:
                stage_c(n)
            yield

    def emit_finish(c):
        for jj in range(4):
            S.dve(lambda e, jj=jj: e.tensor_tensor(
                out=mixed[:, jj, 512:1024], in0=accB[:, jj, :, :].rearrange("p h d -> p (h d)"),
                in1=sg[:, jj, 512:1024], op=ALU.mult),
                reads=[("accB", h) for h in range(8)] + [("sg", jj, 1)], writes=["mixed"])
        for jj in range(4):
            for c8 in range(8):
                S.pe(lambda e, jj=jj, c8=c8: e.transpose(out=B0bf[:, c8 * 128:(c8 + 1) * 128],
                                                         in_=mixed[:, jj, c8 * 128:(c8 + 1) * 128], identity=ident[:]),
                     reads=["mixed", "ident"], writes=["B0"])
            evac_copy(hTc[:, :, jj * 128:(jj + 1) * 128], B0bf[:, :].rearrange("p (c t) -> p c t", c=8),
                      ["B0"], [("hTc", c8) for c8 in range(8)])
        wo = wo_pref.pop(c)
        for jj in range(4):
            i = 4 * c + jj
            sl_ = i % 2
            S.dma(lambda e, i=i, sl_=sl_: e.dma_start(out=xt[sl_][:], in_=x[i * 128:(i + 1) * 128, :]),
                  writes=[("xt", sl_)], sem="x%d" % sl_)
            for half in range(2):
                sl, sk = wo[half]
                bk, bkey = proj_bank()
                for c8 in range(8):
                    S.pe(lambda e, c8=c8, bk=bk, sl=sl, jj=jj: e.matmul(
                        out=bk[:], lhsT=hTc[:, c8, jj * 128:(jj + 1) * 128], rhs=sl[:, c8, :],
                        start=(c8 == 0), stop=(c8 == 7)),
                        reads=[sk, ("hTc", c8)], writes=[bkey])
                S.dve(lambda e, bk=bk, sl_=sl_, half=half: e.tensor_tensor(
                    out=xt[sl_][:, half * 512:(half + 1) * 512], in0=bk[:], in1=xt[sl_][:, half * 512:(half + 1) * 512],
                    op=ALU.add),
                    reads=[bkey, ("xt", sl_)], writes=[("xt", sl_)])
            o = S.dma(lambda e, i=i, sl_=sl_: e.dma_start(out=out[i * 128:(i + 1) * 128, :], in_=xt[sl_][:]),
                      reads=[("xt", sl_)], writes=[("out", i)], sem="o%d" % sl_)
            out_ops.append(o)

    def run_interleaved(gens, totals):
        done = [0] * len(gens)
        alive = [True] * len(gens)
        while any(alive):
            best = None
            for gi in range(len(gens)):
                if not alive[gi]:
                    continue
                frac = done[gi] / float(totals[gi])
                if best is None or frac < best[0]:
                    best = (frac, gi)
            gi = best[1]
            try:
                wv = next(gens[gi])
                done[gi] += 1.0 if wv is None else wv
            except StopIteration:
                alive[gi] = False

    def count_steps(genf, c):
        return None

    marks = []

    def mark(name):
        marks.append((name, {e: len(S.ops[e]) for e in S.ENGS}))

    for c in range(NCH):
        mark("norm%d" % c)
        emit_norm(c)
        mark("proj1_%d" % c)
        emit_proj1(c)
        mark("X%d" % c)
        na = 4 * c + 4
        n_b = 8 * na + 4
        npair = 1 if c == 0 else 2
        n_ib = npair * (16 * W_IDX + 2 + niter * W_BIS) + sum(((4 * c + j) // 8 + 1) for j in range(4)) + (2 if c == 0 else 0)

        def chain(c=c):
            yield from gen_proj2(c)
            yield from gen_attnB(c)

        run_interleaved([gen_index_bisect(c), chain()], [n_ib, n_b + 32])
        S.alias(["mixed"], [("qTi", f8) for f8 in range(8)])
        mark("A%d" % c)
        wo_pref[c] = [load_slab([(0, 0, 512)], src=w_out), load_slab([(0, 512, 512)], src=w_out)]
        for _ in gen_attnA(c):
            pass
        mark("fin%d" % c)
        emit_finish(c)
    mark("end")
    nc._marks = marks

    S.final_wait_ops = out_ops
    S.run()
    return nc


_CACHE = {}


def _host_inputs(x_b, norm_gain, w_in, q_norm_gain, k_norm_gain, rel_bias, w_out, consts):
    m = dict(consts)
    m["x"] = np.ascontiguousarray(x_b, dtype=np.float32)
    m["w_in"] = np.ascontiguousarray(w_in[0], dtype=np.float32)
    m["w_out"] = np.ascontiguousarray(w_out[0], dtype=np.float32)
    m["gain"] = np.ascontiguousarray(norm_gain[0].reshape(8, 128).T, dtype=np.float32)
    m["gq"] = np.ascontiguousarray(np.tile(q_norm_gain[0], 2).reshape(128, 1), dtype=np.float32)
    m["gk"] = np.ascontiguousarray(np.tile(k_norm_gain[0], 2).reshape(128, 1), dtype=np.float32)
    m["b31"] = np.ascontiguousarray(np.broadcast_to(rel_bias[31][None, :], (128, 8)), dtype=np.float32)
    m["relb"] = np.ascontiguousarray(np.concatenate([rel_bias, np.ones((1, 8), np.float32)], axis=0),
                                     dtype=np.float32)
    return m


def kernel(x, norm_gain, w_in, q_norm_gain, k_norm_gain, rel_bias, w_out):
    x = np.asarray(x)
    B, L, _ = x.shape
    key = ("nc", L)
    consts = _consts()
    nc = build(L=L)
    in_maps = [_host_inputs(x[b], np.asarray(norm_gain), np.asarray(w_in), np.asarray(q_norm_gain),
                            np.asarray(k_norm_gain), np.asarray(rel_bias), np.asarray(w_out), consts)
               for b in range(B)]
    res = run_bass_kernel_spmd(nc, in_maps, core_ids=list(range(B)))
    return np.stack([np.asarray(r["out"]) for r in res.results], axis=0).astype(np.float32)
```
